# Optimizing a Trainium2 kernel written in Bass

```python
import math
import jax
import jax.numpy as jnp
from jax import lax
import numpy as np

D_MODEL = 2048
BATCH = 4
SEQ = 4096
DEPTH = 2

GRID_W = 64
CTX_LEN = 256
ROPE_BASE = 10000.0
NORM_EPS = 1e-6

N_DIFF_HEADS = 8
DIFF_DH = 64
DIFF_WIDTH = N_DIFF_HEADS * 2 * DIFF_DH
Q_BLOCK = 128

SSD_INNER = 1024
SSD_HEAD_DIM = 64
SSD_HEADS = SSD_INNER // SSD_HEAD_DIM
SSD_GROUPS = 4
SSD_STATE = 128
SSD_CONV_W = 5
SSD_CHUNK = 128
SSD_XBC = SSD_INNER + 2 * SSD_GROUPS * SSD_STATE

WIN_HEADS = 16
WIN_KV_HEADS = 4
WIN_DH = 64
WINDOW = 128
WIN_BLOCK = 128
WIN_WIDTH = WIN_HEADS * WIN_DH
WIN_KV_WIDTH = WIN_KV_HEADS * WIN_DH

D_FF = 5632
FFN_CONV_W = 3

IN_SIZES = (DIFF_WIDTH, DIFF_WIDTH, DIFF_WIDTH,
            SSD_INNER, SSD_XBC, 2 * SSD_HEADS,
            WIN_WIDTH, WIN_KV_WIDTH, WIN_KV_WIDTH,
            3 * D_MODEL)
IN_WIDTH = sum(IN_SIZES)

kernel_name = 'hybrid_diff_ssd_window_convffn_trunk'


def split_cols(u, sizes):
    out, start = [], 0
    for s in sizes:
        out.append(u[..., start:start + s])
        start += s
    return out


def rmsnorm(x, w):
    x32 = x.astype(jnp.float32)
    y = x32 * lax.rsqrt(jnp.mean(x32 * x32, axis=-1, keepdims=True) + NORM_EPS)
    return (y * w.astype(jnp.float32)).astype(x.dtype)


def axial_rope(n, head_dim):
    rows = n // GRID_W
    row = jnp.repeat(jnp.arange(rows, dtype=jnp.float32), GRID_W)
    col = jnp.tile(jnp.arange(GRID_W, dtype=jnp.float32), rows)
    axis_dim = head_dim // 2
    inv = ROPE_BASE ** (-jnp.arange(0, axis_dim, 2, dtype=jnp.float32) / axis_dim)
    ang = jnp.concatenate([row[:, None] * inv, col[:, None] * inv], axis=-1)
    return jnp.cos(ang), jnp.sin(ang)


def apply_rope(x, cos, sin):
    shape = (cos.shape[0],) + (1,) * (x.ndim - 3) + (cos.shape[1],)
    cos, sin = cos.reshape(shape), sin.reshape(shape)
    x1, x2 = jnp.split(x.astype(jnp.float32), 2, axis=-1)
    return jnp.concatenate([x1 * cos - x2 * sin, x2 * cos + x1 * sin], axis=-1).astype(x.dtype)


def dwconv_centred(u, w, b):
    k = w.shape[0]
    p = k // 2
    length = u.shape[1]
    up = jnp.pad(u, ((0, 0), (p, p), (0, 0)))
    out = b
    for j in range(k):
        out = out + up[:, j:j + length] * w[j]
    return out


def _diff_attend(q, k, v, lam):
    s = jnp.einsum('bqhjd,bkhjd->bhjqk', q, k, preferred_element_type=jnp.float32) * (DIFF_DH ** -0.5)
    p = jax.nn.softmax(s, axis=-1)
    pd = p[:, :, 0] - lam * p[:, :, 1]
    return jnp.einsum('bhqk,bkhe->bqhe', pd.astype(v.dtype), v)


def diff_attention(q, k, v, qc, kc, vc, lam_p, norm_w, lam_init, cos, sin, need_ctx):
    b, n, _ = q.shape
    m = qc.shape[1]
    H, d = N_DIFF_HEADS, DIFF_DH
    q = apply_rope(q.reshape(b, n, H, 2, d), cos, sin)
    k = apply_rope(k.reshape(b, n, H, 2, d), cos, sin)
    v = v.reshape(b, n, H, 2 * d)
    qc = qc.reshape(b, m, H, 2, d)
    kc = kc.reshape(b, m, H, 2, d)
    vc = vc.reshape(b, m, H, 2 * d)
    lp = lam_p.astype(jnp.float32)
    lam = jnp.exp(jnp.sum(lp[0] * lp[1])) - jnp.exp(jnp.sum(lp[2] * lp[3])) + lam_init
    k_all = jnp.concatenate([k, kc], axis=1)
    v_all = jnp.concatenate([v, vc], axis=1)
    nb = n // Q_BLOCK
    q_blocks = jnp.moveaxis(q.reshape(b, nb, Q_BLOCK, H, 2, d), 1, 0)
    o = lax.map(lambda qb: _diff_attend(qb, k_all, v_all, lam), q_blocks)
    o = jnp.moveaxis(o, 0, 1).reshape(b, n, H, 2 * d)

    def finish(o_, length):
        return (rmsnorm(o_, norm_w) * (1.0 - lam_init)).reshape(b, length, H * 2 * d)

    y = finish(o, n)
    yc = finish(_diff_attend(qc, kc, vc, lam), m) if need_ctx else None
    return y, yc


def segsum(a):
    t = a.shape[-1]
    cs = jnp.cumsum(a, axis=-1)
    diff = cs[..., :, None] - cs[..., None, :]
    return jnp.where(jnp.tril(jnp.ones((t, t), dtype=bool)), diff, -jnp.inf)


def ssd_scan(x, dt, a, bm, cm, h0):
    b, l, h, p = x.shape
    g, n = bm.shape[2], bm.shape[3]
    r = h // g
    nc = l // SSD_CHUNK
    xs = (x.astype(jnp.float32) * dt[..., None]).reshape(b, nc, SSD_CHUNK, g, r, p)
    ad = jnp.moveaxis((dt * a).reshape(b, nc, SSD_CHUNK, g, r), (1, 2), (3, 4))
    bc = bm.astype(jnp.float32).reshape(b, nc, SSD_CHUNK, g, n)
    cc = cm.astype(jnp.float32).reshape(b, nc, SSD_CHUNK, g, n)
    a_cum = jnp.cumsum(ad, axis=-1)
    decay_in = jnp.exp(segsum(ad))
    cb = jnp.einsum('bclgn,bcsgn->bgcls', cc, bc)
    y_diag = jnp.einsum('bgcls,bgrcls,bcsgrp->bclgrp', cb, decay_in, xs)
    decay_states = jnp.exp(a_cum[..., -1:] - a_cum)
    states = jnp.einsum('bclgn,bgrcl,bclgrp->bcgrpn', bc, decay_states, xs)
    states = jnp.concatenate([h0.astype(jnp.float32).reshape(b, 1, g, r, p, n), states], axis=1)
    decay_chunk = jnp.exp(segsum(jnp.pad(a_cum[..., -1], ((0, 0), (0, 0), (0, 0), (1, 0)))))
    states = jnp.einsum('bgrzc,bcgrpn->bzgrpn', decay_chunk, states)
    y_off = jnp.einsum('bclgn,bcgrpn,bgrcl->bclgrp', cc, states[:, :-1], jnp.exp(a_cum))
    y = (y_diag + y_off).reshape(b, l, h, p)
    return y, states[:, -1].reshape(b, h, p, n)


def ssd_mixer(z, xbc, dtr, zc, xbcc, dtrc, conv_w, conv_b, a_log, dt_bias, d_skip, norm_w, need_ctx):
    def prep(xbc_, dtr_):
        u = jax.nn.silu(dwconv_centred(xbc_, conv_w, conv_b))
        bsz, length = u.shape[:2]
        xs, bm, cm = split_cols(u, (SSD_INNER, SSD_GROUPS * SSD_STATE, SSD_GROUPS * SSD_STATE))
        xs = xs.reshape(bsz, length, SSD_HEADS, SSD_HEAD_DIM)
        bm = bm.reshape(bsz, length, SSD_GROUPS, SSD_STATE)
        cm = cm.reshape(bsz, length, SSD_GROUPS, SSD_STATE)
        dt = jax.nn.softplus(dtr_.astype(jnp.float32).reshape(bsz, length, 2, SSD_HEADS)
                             + dt_bias.astype(jnp.float32))
        return xs, bm, cm, dt

    a = -jnp.exp(a_log.astype(jnp.float32))
    xs, bm, cm, dt = prep(xbc, dtr)
    xs_c, bm_c, cm_c, dt_c = prep(xbcc, dtrc)
    b = xs.shape[0]
    h0 = jnp.zeros((b, SSD_HEADS, SSD_HEAD_DIM, SSD_STATE), jnp.float32)
    y_lat, y_ctx = 0.0, 0.0
    for direction in range(2):
        orient = (lambda t: t) if direction == 0 else (lambda t: jnp.flip(t, axis=1))
        yc_d, state = ssd_scan(orient(xs_c), orient(dt_c[:, :, direction]), a[direction],
                               orient(bm_c), orient(cm_c), h0)
        y_d, _ = ssd_scan(orient(xs), orient(dt[:, :, direction]), a[direction], orient(bm), orient(cm), state)
        y_lat = y_lat + orient(y_d)
        y_ctx = y_ctx + orient(yc_d)

    def finish(y, xs_, z_):
        y = y + d_skip.astype(jnp.float32)[:, None] * xs_
        y = y.reshape(z_.shape).astype(z_.dtype)
        return rmsnorm(y * jax.nn.silu(z_), norm_w)

    out = finish(y_lat, xs, z)
    out_c = finish(y_ctx, xs_c, zc) if need_ctx else None
    return out, out_c


def window_attention(q, k, v, qc, kc, vc, sink, cos, sin, need_ctx):
    b, n, _ = q.shape
    m = qc.shape[1]
    KV, G, dh = WIN_KV_HEADS, WIN_HEADS // WIN_KV_HEADS, WIN_DH
    scale = dh ** -0.5
    q = apply_rope(q.reshape(b, n, KV, G, dh), cos, sin)
    k = apply_rope(k.reshape(b, n, KV, dh), cos, sin)
    v = v.reshape(b, n, KV, dh)
    qc = qc.reshape(b, m, KV, G, dh)
    kc = kc.reshape(b, m, KV, dh)
    vc = vc.reshape(b, m, KV, dh)
    sink = sink.astype(jnp.float32).reshape(KV, G)
    nb = n // WIN_BLOCK
    qb = q.reshape(b, nb, WIN_BLOCK, KV, G, dh)

    def band(t):
        tp = jnp.pad(t, ((0, 0), (WIN_BLOCK, WIN_BLOCK), (0, 0), (0, 0))).reshape(b, nb + 2, WIN_BLOCK, KV, dh)
        return jnp.concatenate([tp[:, :-2], tp[:, 1:-1], tp[:, 2:]], axis=2)

    kb, vb = band(k), band(v)
    rel = jnp.arange(3 * WIN_BLOCK)[None, :] - WIN_BLOCK - jnp.arange(WIN_BLOCK)[:, None]
    kpos = (jnp.arange(nb)[:, None] - 1) * WIN_BLOCK + jnp.arange(3 * WIN_BLOCK)[None, :]
    mask = (jnp.abs(rel) <= WINDOW)[None] & ((kpos >= 0) & (kpos < n))[:, None, :]
    s_win = jnp.einsum('bnqkgd,bnmkd->bnkgqm', qb, kb, preferred_element_type=jnp.float32) * scale
    s_win = jnp.where(mask[None, :, None, None], s_win, -jnp.inf)
    s_ctx = jnp.einsum('bnqkgd,bmkd->bnkgqm', qb, kc, preferred_element_type=jnp.float32) * scale
    s_sink = jnp.broadcast_to(sink[None, None, :, :, None, None], s_win.shape[:-1] + (1,))
    p = jax.nn.softmax(jnp.concatenate([s_win, s_ctx, s_sink], axis=-1), axis=-1)
    pw = p[..., :3 * WIN_BLOCK].astype(v.dtype)
    pc = p[..., 3 * WIN_BLOCK:3 * WIN_BLOCK + m].astype(v.dtype)
    o = jnp.einsum('bnkgqm,bnmkd->bnqkgd', pw, vb) + jnp.einsum('bnkgqm,bmkd->bnqkgd', pc, vc)
    y = o.reshape(b, n, WIN_WIDTH)
    yc = None
    if need_ctx:
        s = jnp.einsum('bqkgd,bmkd->bkgqm', qc, kc, preferred_element_type=jnp.float32) * scale
        s = jnp.concatenate([s, jnp.broadcast_to(sink[None, :, :, None, None], s.shape[:-1] + (1,))], axis=-1)
        pcc = jax.nn.softmax(s, axis=-1)[..., :m].astype(vc.dtype)
        yc = jnp.einsum('bkgqm,bmkd->bqkgd', pcc, vc).reshape(b, m, WIN_WIDTH)
    return y, yc


def token_mixer(h, hc, w_in, diff_lambda, diff_norm, lam_init, ssd_conv_w, ssd_conv_b, ssd_a_log, ssd_dt_bias,
                ssd_d, ssd_norm, win_sink, w_br_diff, w_br_ssd, w_br_win, w_out, rope_d, rope_w, need_ctx):
    dq, dk, dv, sz, sxbc, sdt, wq, wk, wv, gl = split_cols(h @ w_in, IN_SIZES)
    dqc, dkc, dvc, szc, sxbcc, sdtc, wqc, wkc, wvc, glc = split_cols(hc @ w_in, IN_SIZES)
    yd, ydc = diff_attention(dq, dk, dv, dqc, dkc, dvc, diff_lambda, diff_norm, lam_init, *rope_d, need_ctx)
    ys, ysc = ssd_mixer(sz, sxbc, sdt, szc, sxbcc, sdtc, ssd_conv_w, ssd_conv_b, ssd_a_log, ssd_dt_bias,
                        ssd_d, ssd_norm, need_ctx)
    yw, ywc = window_attention(wq, wk, wv, wqc, wkc, wvc, win_sink, *rope_w, need_ctx)

    def merge(gates, a_, s_, w_):
        gd, gs, gw = jnp.split(jax.nn.sigmoid(gates), 3, axis=-1)
        return (gd * (a_ @ w_br_diff) + gs * (s_ @ w_br_ssd) + gw * (w_ @ w_br_win)) @ w_out

    y = merge(gl, yd, ys, yw)
    yc = merge(glc, ydc, ysc, ywc) if need_ctx else None
    return y, yc


def conv_ffn(h, w_up, conv_w, conv_b, w_down):
    u = dwconv_centred(h @ w_up, conv_w, conv_b)
    a, g = jnp.split(u, 2, axis=-1)
    return (jax.nn.silu(a) * g) @ w_down


def setup_inputs(seed: int = 0) -> dict:
    key = jax.random.key(seed)
    ks = jax.random.split(key, 26)
    f32 = jnp.float32
    L, D = DEPTH, D_MODEL

    def normal(k, shape, scale):
        return jax.random.normal(k, shape, f32) * scale

    def gain(k, shape, scale=0.02):
        return 1.0 + scale * jax.random.normal(k, shape, f32)

    a_log = jnp.log(jax.random.uniform(ks[13], (L, 2, SSD_HEADS), f32, 1.0, 16.0))
    dt0 = jnp.exp(jax.random.uniform(ks[14], (L, 2, SSD_HEADS), f32, math.log(1e-3), math.log(1e-1)))
    dt_bias = dt0 + jnp.log(-jnp.expm1(-dt0))
    return {
        'x': normal(ks[0], (BATCH, SEQ, D), 1.0),
        'c': normal(ks[1], (BATCH, D), 1.0),
        'ctx': normal(ks[2], (BATCH, CTX_LEN, D), 1.0),
        'c_ctx': normal(ks[3], (D,), 1.0),
        'w_ada': normal(ks[4], (L, D, 6 * D), D ** -0.5),
        'b_ada': normal(ks[5], (L, 6 * D), 0.02),
        'norm_g': gain(ks[6], (L, 4, D)),
        'w_in': normal(ks[7], (L, D, IN_WIDTH), D ** -0.5),
        'diff_lambda': normal(ks[8], (L, 4, DIFF_DH), 0.1),
        'diff_norm': gain(ks[9], (L, 2 * DIFF_DH)),
        'ssd_conv_w': normal(ks[10], (L, SSD_CONV_W, SSD_XBC), SSD_CONV_W ** -0.5),
        'ssd_conv_b': normal(ks[11], (L, SSD_XBC), 0.02),
        'ssd_a_log': a_log,
        'ssd_dt_bias': dt_bias,
        'ssd_d': gain(ks[12], (L, SSD_HEADS), 0.1),
        'ssd_norm': gain(ks[15], (L, SSD_INNER)),
        'win_sink': normal(ks[16], (L, WIN_HEADS), 0.5),
        'w_br_diff': normal(ks[17], (L, DIFF_WIDTH, D), DIFF_WIDTH ** -0.5),
        'w_br_ssd': normal(ks[18], (L, SSD_INNER, D), SSD_INNER ** -0.5),
        'w_br_win': normal(ks[19], (L, WIN_WIDTH, D), WIN_WIDTH ** -0.5),
        'w_out': normal(ks[20], (L, D, D), D ** -0.5),
        'ffn_w_up': normal(ks[21], (L, D, 2 * D_FF), D ** -0.5),
        'ffn_conv_w': normal(ks[22], (L, FFN_CONV_W, 2 * D_FF), FFN_CONV_W ** -0.5),
        'ffn_conv_b': normal(ks[23], (L, 2 * D_FF), 0.02),
        'ffn_w_down': normal(ks[24], (L, D_FF, D), D_FF ** -0.5),
    }


def reference(x, c, ctx, c_ctx, w_ada, b_ada, norm_g, w_in, diff_lambda, diff_norm, ssd_conv_w, ssd_conv_b,
              ssd_a_log, ssd_dt_bias, ssd_d, ssd_norm, win_sink, w_br_diff, w_br_ssd, w_br_win, w_out,
              ffn_w_up, ffn_conv_w, ffn_conv_b, ffn_w_down):
    n = x.shape[1]
    rope_d = axial_rope(n, DIFF_DH)
    rope_w = axial_rope(n, WIN_DH)
    cond = jax.nn.silu(c)
    cond_ctx = jax.nn.silu(c_ctx)
    xc = ctx
    for l in range(DEPTH):
        need_ctx = l < DEPTH - 1
        lam_init = 0.8 - 0.6 * math.exp(-0.3 * l)
        mod = (cond @ w_ada[l] + b_ada[l])[:, None, :]
        mod_c = cond_ctx @ w_ada[l] + b_ada[l]
        sh1, sc1, g1, sh2, sc2, g2 = jnp.split(mod, 6, axis=-1)
        sh1c, sc1c, g1c, sh2c, sc2c, g2c = jnp.split(mod_c, 6, axis=-1)
        h = rmsnorm(x, norm_g[l, 0]) * (1.0 + sc1) + sh1
        hc = rmsnorm(xc, norm_g[l, 0]) * (1.0 + sc1c) + sh1c
        y, yc = token_mixer(h, hc, w_in[l], diff_lambda[l], diff_norm[l], lam_init, ssd_conv_w[l], ssd_conv_b[l],
                            ssd_a_log[l], ssd_dt_bias[l], ssd_d[l], ssd_norm[l], win_sink[l],
                            w_br_diff[l], w_br_ssd[l], w_br_win[l], w_out[l], rope_d, rope_w, need_ctx)
        x = x + g1 * rmsnorm(y, norm_g[l, 1])
        h = rmsnorm(x, norm_g[l, 2]) * (1.0 + sc2) + sh2
        x = x + g2 * rmsnorm(conv_ffn(h, ffn_w_up[l], ffn_conv_w[l], ffn_conv_b[l], ffn_w_down[l]), norm_g[l, 3])
        if need_ctx:
            xc = xc + g1c * rmsnorm(yc, norm_g[l, 1])
            hc = rmsnorm(xc, norm_g[l, 2]) * (1.0 + sc2c) + sh2c
            xc = xc + g2c * rmsnorm(conv_ffn(hc, ffn_w_up[l], ffn_conv_w[l], ffn_conv_b[l], ffn_w_down[l]),
                                    norm_g[l, 3])
    return x
```

```python
import math
import numpy as np
import ml_dtypes
import concourse.bass as bass
import concourse.mybir as mybir
from concourse.bass_utils import run_bass_kernel_spmd

F32 = mybir.dt.float32
BF16 = mybir.dt.bfloat16
AF = mybir.ActivationFunctionType
ALU = mybir.AluOpType
AX = mybir.AxisListType

D = 2048
DEPTH = 2
CTX = 256
GRID_W = 64
EPS = 1e-6
IN_SIZES = (1024, 1024, 1024, 1024, 2048, 32, 1024, 256, 256, 6144)
IN_OFF = [0]
for _s in IN_SIZES:
    IN_OFF.append(IN_OFF[-1] + _s)
IN_WIDTH = IN_OFF[-1]
D_FF = 5632
NDMA_SEM = 12


class _Op:
    __slots__ = ("eng", "fn", "deps", "sig", "sigval", "dma", "dsem", "dval", "dprev")


class Sched:
    ENGS = ("pe", "act", "dve", "pool", "sp")

    def __init__(self, nc):
        self.nc = nc
        self.ops = []
        self.last_w = {}
        self.readers = {}
        self.last_on = {e: None for e in self.ENGS}
        self.open_dma = {}
        self.dcnt = {e: 0 for e in self.ENGS}

    def op(self, eng, fn, reads=(), writes=(), dma=False):
        o = _Op()
        o.eng, o.fn, o.dma = eng, fn, dma
        o.sig = False
        psr = [r for r in reads if isinstance(r, str) and r.startswith("ps") and r[2:].isdigit()]
        if psr:
            reads = [r for r in reads if r not in psr]
            writes = list(writes) + psr
        deps = set()
        for r in reads:
            w = self.last_w.get(r)
            if w is not None:
                deps.add(w)
        for w_ in writes:
            w = self.last_w.get(w_)
            if w is not None:
                deps.add(w)
            for rd in self.readers.get(w_, ()):
                deps.add(rd)
        for r in reads:
            self.readers.setdefault(r, []).append(o)
        for w_ in writes:
            self.last_w[w_] = o
            self.readers[w_] = []
        deps.discard(o)
        o.deps = deps
        self.ops.append(o)
        if not dma:
            self.last_on[eng] = o
        if dma:
            i = self.dcnt[eng]
            self.dcnt[eng] += 1
            o.dsem = (eng, i % NDMA_SEM)
            o.dval = 16 * (i // NDMA_SEM + 1)
            o.dprev = 16 * (i // NDMA_SEM)
            self.open_dma[o.dsem] = o
        return o

    def barrier(self):
        pend = [o for o in self.last_on.values() if o is not None and not o.dma] + list(self.open_dma.values())
        for e in self.ENGS:
            o = _Op()
            o.eng, o.fn, o.dma, o.sig = e, None, False, False
            o.deps = set(pend)
            self.ops.append(o)
        self.open_dma = {}
        self.last_w = {}
        self.readers = {}

    def emit(self):
        nc = self.nc
        for o in self.ops:
            for d in o.deps:
                if not d.dma:
                    d.sig = True
        cnt = {e: 0 for e in self.ENGS}
        dcnt = self.dcnt
        for o in self.ops:
            if o.dma:
                pass
            elif o.sig:
                cnt[o.eng] += 1
                o.sigval = cnt[o.eng]
        per = {e: [o for o in self.ops if o.eng == e] for e in self.ENGS}
        import contextlib
        with contextlib.ExitStack() as st:
            esem = {e: st.enter_context(nc.semaphore("s_" + e)) for e in self.ENGS}
            dsem = {}
            for e in ("sp", "act", "pool"):
                if dcnt[e]:
                    for i in range(NDMA_SEM):
                        dsem[(e, i)] = st.enter_context(nc.semaphore("d_%s%d" % (e, i)))
            block = st.enter_context(nc.Block())

            def run(ename, eng):
                waited = {}

                def wait(sem_key, sem, val):
                    if waited.get(sem_key, 0) < val:
                        eng.wait_ge(sem, val)
                        waited[sem_key] = val

                for o in per[ename]:
                    for d in o.deps:
                        if d.dma:
                            wait(d.dsem, dsem[d.dsem], d.dval)
                        else:
                            if d.eng == ename and ename == "pe":
                                continue
                            wait(d.eng, esem[d.eng], d.sigval)
                    if o.fn is None:
                        continue
                    if o.dma and o.dprev:
                        wait(o.dsem, dsem[o.dsem], o.dprev)
                    inst = o.fn(eng)
                    if o.dma:
                        inst.then_inc(dsem[o.dsem], 16)
                    elif o.sig:
                        inst.then_inc(esem[ename], 1)

            @block.tensor
            def _(e):
                run("pe", e)

            @block.scalar
            def _(e):
                run("act", e)

            @block.vector
            def _(e):
                run("dve", e)

            @block.gpsimd
            def _(e):
                run("pool", e)

            @block.sync
            def _(e):
                run("sp", e)


ENG_OF = {"act": "act", "dve": "dve", "pool": "pool"}


def MM(S, items, r, w):
    items = list(items)

    def fn(e):
        last = None
        for (out, lhsT, rhs, st, sp) in items:
            last = e.matmul(out, lhsT=lhsT, rhs=rhs, start=st, stop=sp, skip_group_check=True)
        return last
    return S.op("pe", fn, r, w)


def TR(S, items, ident, r, w):
    items = list(items)

    def fn(e):
        last = None
        for (out, in_) in items:
            last = e.transpose(out, in_, ident)
        return last
    return S.op("pe", fn, r, w)


def ACT(S, out, in_, func, r, w, scale=None, bias=None, accum=None):
    kw = {}
    if scale is not None:
        kw["scale"] = scale
    if bias is not None:
        kw["bias"] = bias
    if accum is not None:
        kw["accum_out"] = accum
    return S.op("act", lambda e: e.activation(out=out, in_=in_, func=func, **kw), r, w)


def TS(S, eng, out, in0, s1, s2, op0, op1, r, w):
    if op1 is None:
        return S.op(eng, lambda e: e.tensor_scalar(out, in0, s1, None, op0), r, w)
    return S.op(eng, lambda e: e.tensor_scalar(out, in0, s1, s2, op0, op1), r, w)


def TT(S, eng, out, in0, in1, op, r, w):
    return S.op(eng, lambda e: e.tensor_tensor(out, in0, in1, op), r, w)


def STT(S, out, in0, scalar, in1, op0, op1, r, w):
    return S.op("dve", lambda e: e.scalar_tensor_tensor(out, in0, scalar, in1, op0, op1), r, w)


def CP(S, eng, out, in_, r, w):
    if eng == "act":
        return S.op("act", lambda e: e.copy(out, in_), r, w)
    return S.op(eng, lambda e: e.tensor_copy(out, in_), r, w)


def MSET(S, eng, ap, val, r, w):
    return S.op(eng, lambda e: e.memset(ap, val), r, w)


def RECIP(S, out, in_, r, w):
    return S.op("dve", lambda e: e.reciprocal(out, in_), r, w)


def DMA(S, q, out, in_, r, w):
    return S.op(q, lambda e: e.dma_start(out=out, in_=in_), r, w, dma=True)


class Arena:
    def __init__(self, ap, nwords):
        self.ap = ap
        self.n = nwords
        self.top = 0
        self.floor = 0
        self.uid = 0

    def alloc(self, shape, dtype, name=None):
        nelem = 1
        for s in shape:
            nelem *= s
        nbytes = nelem * (2 if dtype == BF16 else 4)
        nw = (nbytes + 31) // 32 * 8
        assert self.top + nw <= self.n, "arena overflow %s %d+%d>%d" % (name, self.top, nw, self.n)
        a = self.ap[:, self.top:self.top + nw]
        self.top += nw
        if dtype == BF16:
            a = a.bitcast(BF16)
        a = a[:, 0:nelem]
        if len(shape) == 2:
            a = a.rearrange("p (a b) -> p a b", a=shape[0])
        elif len(shape) == 3:
            a = a.rearrange("p (a b c) -> p a b c", a=shape[0], b=shape[1])
        self.uid += 1
        return a

    def keep(self):
        self.floor = self.top

    def reset(self):
        self.top = self.floor


def host_consts(SEQ):
    T = SEQ + CTX
    bf = ml_dtypes.bfloat16
    c = {}
    c["ident_bf"] = np.eye(128, dtype=np.float32).astype(bf)
    c["ident_f"] = np.eye(128, dtype=np.float32)
    pm = np.zeros((128, 128), np.float32)
    for base in (0, 64):
        for i in range(32):
            pm[base + i + 32, base + i] = -1.0
            pm[base + i, base + i + 32] = 1.0
    c["perm_bf"] = pm.astype(bf)
    s_ = np.arange(128)[:, None]
    l_ = np.arange(128)[None, :]
    c["tri_f"] = (s_ <= l_).astype(np.float32)
    c["tri_b"] = (s_ >= l_).astype(np.float32)
    c["ones_f"] = np.ones((128, 128), np.float32)
    c["neg_f"] = np.where(l_ >= s_, 0.0, -30000.0).astype(np.float32)
    c["neg_b"] = np.where(l_ <= s_, 0.0, -30000.0).astype(np.float32)
    c["wm_prev"] = (s_ >= l_).astype(np.float32).astype(bf)
    c["wm_next"] = (s_ <= l_).astype(np.float32).astype(bf)
    t = np.arange(SEQ)
    row = (t // GRID_W).astype(np.float32)
    col = (t % GRID_W).astype(np.float32)
    inv = (10000.0 ** (-np.arange(0, 32, 2, dtype=np.float32) / 32.0)).astype(np.float32)
    ang = np.concatenate([row[:, None] * inv, col[:, None] * inv], axis=-1).astype(np.float32)
    cosT = np.ones((128, T), np.float32)
    sinT = np.zeros((128, T), np.float32)
    for p in range(128):
        cosT[p, :SEQ] = np.cos(ang[:, p % 32])
        sinT[p, :SEQ] = np.sin(ang[:, p % 32])
    c["cosT"] = cosT
    c["sinT"] = sinT
    return c


CONST_SPECS = [("ident_bf", BF16, None), ("ident_f", F32, None), ("perm_bf", BF16, None),
               ("tri_f", F32, None), ("tri_b", F32, None), ("ones_f", F32, None),
               ("neg_f", F32, None), ("neg_b", F32, None), ("wm_prev", BF16, None), ("wm_next", BF16, None)]

W_SPECS = [("w_ada", [D, 6 * D]), ("b_ada", [1, 6 * D]), ("norm_g", [4, D]), ("w_in", [D, IN_WIDTH]),
           ("diff_lambda", [1, 256]), ("diff_norm", [1, 128]), ("ssd_conv_w", [5, 2048]),
           ("ssd_conv_b", [1, 2048]), ("ssd_a_log", [1, 32]), ("ssd_dt_bias", [1, 32]),
           ("ssd_d", [1, 16]), ("ssd_norm", [1, 1024]), ("win_sink", [1, 16]),
           ("w_br_diff", [1024, D]), ("w_br_ssd", [1024, D]), ("w_br_win", [1024, D]),
           ("w_out", [D, D]), ("ffn_w_up", [D, 2 * D_FF]), ("ffn_conv_w", [3, 2 * D_FF]),
           ("ffn_conv_b", [1, 2 * D_FF]), ("ffn_w_down", [D_FF, D])]


class Builder:
    def __init__(self, SEQ, debug=False, nlayers=DEPTH, stop_after=None):
        self.SEQ = SEQ
        self.T = SEQ + CTX
        self.NL = SEQ // 128
        self.NT = self.T // 128
        self.debug = debug
        self.nlayers = nlayers
        self.stop_after = stop_after
        nc = bass.Bass("TRN2", target_bir_lowering=False)
        self.nc = nc
        self.S = Sched(nc)
        T = self.T
        dt = nc.dram_tensor
        self.x_in = dt("x", [SEQ, D], F32, kind="ExternalInput").ap()
        self.ctx_in = dt("ctx", [CTX, D], F32, kind="ExternalInput").ap()
        self.cvec = dt("cvec", [2, D], F32, kind="ExternalInput").ap()
        wspec = dict(W_SPECS)

        class _Lazy(dict):
            def __missing__(d_, name):
                d_[name] = dt(name, [DEPTH] + wspec[name], F32, kind="ExternalInput").ap()
                return d_[name]
        self.W = _Lazy()
        self.C = {}
        for name, dty, _ in CONST_SPECS:
            self.C[name] = dt(name, [128, 128], dty, kind="ExternalInput").ap()
        self.C["cosT"] = dt("cosT", [128, T], F32, kind="ExternalInput").ap()
        self.C["sinT"] = dt("sinT", [128, T], F32, kind="ExternalInput").ap()
        self.out = dt("out", [SEQ, D], F32, kind="ExternalOutput").ap()
        sk = "ExternalOutput" if debug else "Internal"
        self.scr = {}
        for name, shp, dty in [
            ("X", [T, D], F32), ("X2", [T, D], F32), ("MOD", [2, 6 * D], F32),
            ("QDT", [8, 128, T], BF16), ("KDT", [8, 128, T], BF16), ("VD", [T, 1024], BF16),
            ("ZS", [T, 1024], BF16), ("XBCT", [16, 128, T], BF16), ("DT", [T, 32], F32),
            ("WQT", [16, 64, T], BF16), ("WKT", [4, 64, T], BF16), ("WV", [T, 256], BF16),
            ("GT", [48, 128, T], BF16),
            ("UT", [16, 128, T], BF16), ("XSB", [T, 1536], BF16), ("HB", [self.NT, 128, 1024], BF16),
            ("YDT", [8, 128, T], BF16), ("YST", [8, 128, T], BF16), ("YWT", [8, 128, T], BF16),
        ]:
            self.scr[name] = dt("s_" + name, shp, dty, kind=sk).ap()

    def build(self):
        nc, S = self.nc, self.S
        import contextlib
        with contextlib.ExitStack() as st:
            NW = 50000
            arena = st.enter_context(nc.sbuf_tensor("arena", [128, NW], F32))
            self.A = Arena(arena[:, :], NW)
            self.ps = [st.enter_context(nc.psum_tensor("psb%d" % i, [128, 512], F32)) for i in range(8)]
            self.psn = ["ps%d" % i for i in range(8)]
            self.setup_consts()
            for l in range(self.nlayers):
                self.layer(l)
                if self.stop_after is not None and l == self.stop_after[0]:
                    break
            self.finish()
            S.barrier()
            S.emit()
        return nc

    def newphase(self):
        self.S.barrier()
        self.A.reset()

    def setup_consts(self):
        S, A = self.S, self.A
        self.K = {}
        for name, dty, _ in CONST_SPECS:
            t = A.alloc([128], dty)
            DMA(S, "sp", t, self.C[name][:, :], [], ["c_" + name])
            self.K[name] = t
        self.modT = [A.alloc([96], F32), A.alloc([96], F32)]
        self.ngT = A.alloc([64], F32)
        self.colA1 = [A.alloc([16], F32) for _ in range(2)]
        self.colA2 = [A.alloc([16], F32) for _ in range(2)]
        A.keep()
        X = self.scr["X"]
        DMA(S, "sp", X[0:self.SEQ, :], self.x_in[:, :], [], [])
        DMA(S, "sp", X[self.SEQ:self.T, :], self.ctx_in[:, :], [], [])
        S.barrier()

    def finish(self):
        S = self.S
        S.barrier()
        DMA(S, "sp", self.out[:, :], self.scr["X"][0:self.SEQ, :], [], [])

    def layer(self, l):
        self.need_ctx = l < DEPTH - 1
        self.lam_init = 0.8 - 0.6 * math.exp(-0.3 * l)
        sa = self.stop_after[1] if (self.stop_after is not None and self.stop_after[0] == l) else None
        if sa == "setup":
            return
        self.phase_mod(l)
        if sa == "mod":
            return
        self.phase_inproj(l)
        if sa == "inproj":
            return
        self.phase_diff(l)
        if sa == "diff":
            return
        self.phase_win(l)
        if sa == "win":
            return
        self.phase_ssd(l)
        if sa == "ssd":
            return
        self.phase_merge(l)
        if sa == "merge":
            return
        self.phase_ffn(l)

    def phase_mod(self, l):
        S, A, ps, psn, K = self.S, self.A, self.ps, self.psn, self.K
        self.newphase()
        W = self.W
        identf = K["ident_f"]
        cc = A.alloc([D], F32)
        DMA(S, "sp", cc[0:2, :], self.cvec[:, :], [], ["cc"])
        ACT(S, cc[0:2, :], cc[0:2, :], AF.Silu, ["cc"], ["cc"])
        TR(S, [(ps[0][:, 2 * k:2 * k + 2], cc[0:2, k * 128:(k + 1) * 128]) for k in range(16)],
           identf[0:2, 0:2], ["cc", "c_ident_f"], [psn[0]])
        import os
        stage = int(os.environ.get("MODSTAGE", "99"))
        if stage < 1:
            return
        condT = A.alloc([32], F32)
        CP(S, "dve", condT, ps[0][:, 0:32], [psn[0]], ["condT"])
        if stage < 2:
            return
        modrow = A.alloc([6 * D], F32)
        bb = A.alloc([6 * D], F32)
        DMA(S, "sp", bb[0:2, :], W["b_ada"][l][0:1, :].broadcast_to([2, 6 * D]), [], ["bb"])
        wA = [A.alloc([16, 512], F32) for _ in range(2)]
        nchunk = 6 * D // 512

        def load(j):
            DMA(S, "sp", wA[j % 2], W["w_ada"][l][:, j * 512:(j + 1) * 512].rearrange("(k p) c -> p k c", p=128),
                [], ["wA%d" % (j % 2)])
        load(0)
        for j in range(nchunk):
            if j + 1 < nchunk:
                load(j + 1)
            b = 1 + j % 2
            MM(S, [(ps[b][0:2, :], condT[:, 2 * k:2 * k + 2], wA[j % 2][:, k, :], k == 0, k == 15) for k in range(16)],
               ["condT", "wA%d" % (j % 2)], [psn[b]])
            TT(S, "dve", modrow[0:2, j * 512:(j + 1) * 512], ps[b][0:2, :], bb[0:2, j * 512:(j + 1) * 512], ALU.add,
               [psn[b], "bb"], ["modrow"])
        DMA(S, "sp", self.scr["MOD"][:, :], modrow[0:2, :], ["modrow"], [])
        if stage < 3:
            return
        self.newphase()
        MOD = self.scr["MOD"]
        for r in range(2):
            m96 = A.alloc([128], F32)
            DMA(S, "sp", m96[0:96, :], MOD[r, :].rearrange("(c p) -> c p", p=128), [], ["m96_%d" % r])
            TR(S, [(ps[r][:, 0:96], m96[0:96, :])], identf[0:96, 0:96], ["m96_%d" % r, "c_ident_f"], [psn[r]])
            CP(S, "dve", self.modT[r], ps[r][:, 0:96], [psn[r]], ["modT%d" % r])
        n64 = A.alloc([128], F32)
        DMA(S, "sp", n64[0:64, :], W["norm_g"][l].rearrange("i (k p) -> (i k) p", p=128), [], ["n64"])
        TR(S, [(ps[2][:, 0:64], n64[0:64, :])], identf[0:64, 0:64], ["n64", "c_ident_f"], [psn[2]])
        CP(S, "dve", self.ngT, ps[2][:, 0:64], [psn[2]], ["ngT"])
        for r in range(2):
            STT(S, self.colA1[r], self.modT[r][:, 16:32], 1.0, self.ngT[:, 0:16], ALU.add, ALU.mult,
                ["modT%d" % r, "ngT"], ["colA1_%d" % r])
            STT(S, self.colA2[r], self.modT[r][:, 64:80], 1.0, self.ngT[:, 32:48], ALU.add, ALU.mult,
                ["modT%d" % r, "ngT"], ["colA2_%d" % r])
        self.colB1 = [self.modT[r][:, 0:16] for r in range(2)]
        self.colB2 = [self.modT[r][:, 48:64] for r in range(2)]

    def norm_tiles(self, jobs, colA, colB, hT, banks):
        S, A, ps, psn, K = self.S, self.A, self.ps, self.psn, self.K
        X = self.scr["X"]
        xt = [A.alloc([D], F32) for _ in range(2)]
        xn = [A.alloc([D], BF16) for _ in range(2)]
        junk = A.alloc([D], BF16)
        ss = [A.alloc([1], F32) for _ in range(2)]
        rs = [A.alloc([1], F32) for _ in range(2)]
        names = []
        for i, (g, lo, n, dst) in enumerate(jobs):
            s = i % 2
            r = 0 if g < self.NL else 1
            DMA(S, "sp", xt[s], X[g * 128:(g + 1) * 128, :], [], ["xt%d" % s])
            ACT(S, junk, xt[s], AF.Square, ["xt%d" % s], ["junk", "ss%d" % s], accum=ss[s])
            ACT(S, rs[s], ss[s], AF.Sqrt, ["ss%d" % s], ["rs%d" % s], scale=1.0 / D, bias=EPS)
            RECIP(S, rs[s], rs[s], ["rs%d" % s], ["rs%d" % s])
            TS(S, "dve", xn[s], xt[s], rs[s], None, ALU.mult, None, ["xt%d" % s, "rs%d" % s], ["xn%d" % s])
            bp = banks[i % len(banks)]
            for half in range(2):
                b = bp[half]
                pb = ps[b][:].bitcast(BF16)
                TR(S, [(pb[:, kk * 128:(kk + 1) * 128], xn[s][:, (half * 8 + kk) * 128:(half * 8 + kk + 1) * 128])
                       for kk in range(8)], K["ident_bf"], ["xn%d" % s, "c_ident_bf"], [psn[b]])
            nm = "hT_%d" % i
            for half in range(2):
                b = bp[half]
                pb = ps[b][:].bitcast(BF16)
                for kk in range(8):
                    k = half * 8 + kk
                    src = pb[:, kk * 128 + lo:kk * 128 + lo + n]
                    if half == 0:
                        ACT(S, hT[:, k, dst:dst + n], src, AF.Identity, [psn[b]], [nm + "_%d" % k],
                            scale=colA[r][:, k:k + 1], bias=colB[r][:, k:k + 1])
                    else:
                        TS(S, "dve", hT[:, k, dst:dst + n], src, colA[r][:, k:k + 1], colB[r][:, k:k + 1],
                           ALU.mult, ALU.add, [psn[b]], [nm + "_%d" % k])
            names.append([nm + "_%d" % k for k in range(16)])
        return names

    def phase_inproj(self, l):
        S, A, ps, psn, K = self.S, self.A, self.ps, self.psn, self.K
        W = self.W
        scr = self.scr
        NT = self.NT
        nblk = (NT + 16) // 17
        per = (NT + nblk - 1) // nblk
        blocks = [list(range(b * per, min(NT, (b + 1) * per))) for b in range(nblk)]
        for tiles in blocks:
            self.newphase()
            ntb = len(tiles)
            ntok = ntb * 128
            tok0 = tiles[0] * 128
            hT = A.alloc([16, ntok], BF16)
            hnames = self.norm_tiles([(g, 0, 128, i * 128) for i, g in enumerate(tiles)],
                                     self.colA1, self.colB1, hT, [(0, 1), (2, 3), (4, 5), (6, 7)])
            self.newphase_keep(hT)
            cs = A.alloc([ntok], F32)
            sn = A.alloc([ntok], F32)
            DMA(S, "sp", cs, self.C["cosT"][:, tok0:tok0 + ntok], [], ["cs"])
            DMA(S, "sp", sn, self.C["sinT"][:, tok0:tok0 + ntok], [], ["sn"])
            dtb = A.alloc([32], F32)
            DMA(S, "sp", dtb, W["ssd_dt_bias"][l][0:1, :].broadcast_to([128, 32]), [], ["dtb"])
            wt = [A.alloc([16, 512], BF16) for _ in range(2)]
            qsb = [A.alloc([512], BF16) for _ in range(2)]
            t1 = [A.alloc([512], F32) for _ in range(2)]
            t2 = [A.alloc([512], F32) for _ in range(2)]
            obf = [A.alloc([ntok], BF16) for _ in range(2)]
            obt = [A.alloc([512], BF16) for _ in range(3)]
            dts = [A.alloc([32], F32) for _ in range(2)]
            cblocks = []
            import os
            gsel = [int(v) for v in os.environ.get("INPROJ_GROUPS", "0,1,2,3,4,5,6,7,8,9").split(",")]
            for gi in gsel:
                for c0 in range(0, IN_SIZES[gi], 512):
                    cblocks.append((gi, c0, min(512, IN_SIZES[gi] - c0)))
            chunks = [(c, min(512, ntok - c)) for c in range(0, ntok, 512)]
            st = {"bank": 0, "fm": 0, "tm": 0, "rp": 0}

            def loadw(i):
                gi, c0, nco = cblocks[i]
                a = IN_OFF[gi] + c0
                DMA(S, "pool", wt[i % 2][:, :, 0:nco],
                    W["w_in"][l][:, a:a + nco].rearrange("(k p) c -> p k c", p=128), [], ["wt%d" % (i % 2)])

            def hres(c, n):
                out = []
                for ti in range(c // 128, (c + n) // 128):
                    out += hnames[ti]
                return out

            loadw(0)
            for i, (gi, c0, nco) in enumerate(cblocks):
                if i + 1 < len(cblocks):
                    loadw(i + 1)
                ws = i % 2
                wn = "wt%d" % ws
                fm = gi in (0, 1, 4, 6, 7, 9)
                if fm:
                    for m in range(nco // 128):
                        fs = st["fm"] % 2
                        st["fm"] += 1
                        fb = (c0 + m * 128) // 128
                        chunk_names = []
                        for ci, (c, n) in enumerate(chunks):
                            b = st["bank"] % 4
                            st["bank"] += 1
                            MM(S, [(ps[b][:, 0:n], wt[ws][:, k, m * 128:(m + 1) * 128], hT[:, k, c:c + n], k == 0, k == 15)
                                   for k in range(16)], [wn], [psn[b]])
                            on = "obf%d_%d" % (fs, ci)
                            chunk_names.append(on)
                            if gi in (0, 1, 6, 7):
                                rs_ = st["rp"] % 2
                                st["rp"] += 1
                                pb = 4 + rs_
                                CP(S, "act", qsb[rs_][:, 0:n], ps[b][:, 0:n], [psn[b]], ["qsb%d" % rs_])
                                MM(S, [(ps[pb][:, 0:n], K["perm_bf"], qsb[rs_][:, 0:n], True, True)],
                                   ["qsb%d" % rs_], [psn[pb]])
                                TT(S, "dve", t1[rs_][:, 0:n], ps[b][:, 0:n], cs[:, c:c + n], ALU.mult,
                                   [psn[b], "cs"], ["t1_%d" % rs_])
                                TT(S, "dve", t2[rs_][:, 0:n], ps[pb][:, 0:n], sn[:, c:c + n], ALU.mult,
                                   [psn[pb], "sn"], ["t2_%d" % rs_])
                                TT(S, "pool", obf[fs][:, c:c + n], t1[rs_][:, 0:n], t2[rs_][:, 0:n], ALU.add,
                                   ["t1_%d" % rs_, "t2_%d" % rs_], [on])
                            elif gi == 4:
                                CP(S, "act", obf[fs][:, c:c + n], ps[b][:, 0:n], [psn[b]], [on])
                            else:
                                ACT(S, obf[fs][:, c:c + n], ps[b][:, 0:n], AF.Sigmoid, [psn[b]], [on])
                        if gi == 0:
                            dst = scr["QDT"][fb][:, tok0:tok0 + ntok]
                        elif gi == 1:
                            dst = scr["KDT"][fb][:, tok0:tok0 + ntok]
                        elif gi == 4:
                            dst = scr["XBCT"][fb][:, tok0:tok0 + ntok]
                        elif gi == 6:
                            dst = scr["WQT"].rearrange("(a two) d t -> a (two d) t", two=2)[fb][:, tok0:tok0 + ntok]
                        elif gi == 7:
                            dst = scr["WKT"].rearrange("(a two) d t -> a (two d) t", two=2)[fb][:, tok0:tok0 + ntok]
                        else:
                            dst = scr["GT"][fb][:, tok0:tok0 + ntok]
                        DMA(S, "sp", dst, obf[fs], chunk_names, [])
                else:
                    for ti, g in enumerate(tiles):
                        b = st["bank"] % 4
                        st["bank"] += 1
                        MM(S, [(ps[b][:, 0:nco], hT[:, k, ti * 128:(ti + 1) * 128], wt[ws][:, k, 0:nco], k == 0, k == 15)
                               for k in range(16)], [wn], [psn[b]])
                        rows = slice(g * 128, (g + 1) * 128)
                        if gi == 5:
                            d_ = st["tm"] % 2
                            st["tm"] += 1
                            TT(S, "dve", dts[d_], ps[b][:, 0:32], dtb, ALU.add, [psn[b], "dtb"], ["dts%d" % d_])
                            ACT(S, dts[d_], dts[d_], AF.Exp, ["dts%d" % d_], ["dts%d" % d_])
                            ACT(S, dts[d_], dts[d_], AF.Ln, ["dts%d" % d_], ["dts%d" % d_], bias=1.0)
                            DMA(S, "sp", scr["DT"][rows, :], dts[d_], ["dts%d" % d_], [])
                            continue
                        o_ = st["tm"] % 3
                        st["tm"] += 1
                        on = "obt%d" % o_
                        if gi == 3:
                            ACT(S, obt[o_][:, 0:nco], ps[b][:, 0:nco], AF.Silu, [psn[b]], [on])
                            dst = scr["ZS"][rows, c0:c0 + nco]
                        elif gi == 2:
                            CP(S, "act", obt[o_][:, 0:nco], ps[b][:, 0:nco], [psn[b]], [on])
                            dst = scr["VD"][rows, c0:c0 + nco]
                        else:
                            CP(S, "act", obt[o_][:, 0:nco], ps[b][:, 0:nco], [psn[b]], [on])
                            dst = scr["WV"][rows, c0:c0 + nco]
                        DMA(S, "sp", dst, obt[o_][:, 0:nco], [on], [])

    def newphase_keep(self, *_):
        self.S.barrier()

    def phase_diff(self, l):
        S, A, ps, psn, K = self.S, self.A, self.ps, self.psn, self.K
        W, scr = self.W, self.scr
        T, NT, NL, SEQ = self.T, self.NT, self.NL, self.SEQ
        self.newphase()
        dl = A.alloc([256], F32)
        DMA(S, "sp", dl, W["diff_lambda"][l][0:1, :].broadcast_to([128, 256]), [], ["dl"])
        jk = A.alloc([128], F32)
        s12 = A.alloc([2], F32)
        TT(S, "dve", jk[:, 0:64], dl[:, 0:64], dl[:, 64:128], ALU.mult, ["dl"], ["jk"])
        ACT(S, jk[:, 0:64], jk[:, 0:64], AF.Identity, ["jk"], ["jk", "s12a"], accum=s12[:, 0:1])
        TT(S, "dve", jk[:, 64:128], dl[:, 128:192], dl[:, 192:256], ALU.mult, ["dl"], ["jk2"])
        ACT(S, jk[:, 64:128], jk[:, 64:128], AF.Identity, ["jk2"], ["jk2", "s12b"], accum=s12[:, 1:2])
        ACT(S, s12, s12, AF.Exp, ["s12a", "s12b"], ["s12"])
        nlam = A.alloc([1], F32)
        TT(S, "dve", nlam, s12[:, 0:1], s12[:, 1:2], ALU.subtract, ["s12"], ["nlam"])
        TS(S, "dve", nlam, nlam, float(self.lam_init), -1.0, ALU.add, ALU.mult, ["nlam"], ["nlam"])
        dn = A.alloc([1], F32)
        DMA(S, "sp", dn, W["diff_norm"][l].rearrange("o p -> p o"), [], ["dn"])
        TS(S, "dve", dn, dn, float(1.0 - self.lam_init), None, ALU.mult, None, ["dn"], ["dn"])
        KT = [A.alloc([T], BF16) for _ in range(2)]
        V = [A.alloc([NT, 129], BF16) for _ in range(2)]
        for s in range(2):
            MSET(S, "pool", V[s][:, :, 128:129], 1.0, [], ["V%d" % s])
        QT = [A.alloc([512], BF16) for _ in range(2)]
        PT = [[A.alloc([512], BF16) for _ in range(2)] for _ in range(2)]
        rr = A.alloc([2], F32)
        o = A.alloc([128], F32)
        sq = A.alloc([128], BF16)
        ssq = A.alloc([1], F32)
        on = A.alloc([512], BF16)
        ob = [A.alloc([512], BF16) for _ in range(2)]

        def acc(j, qi):
            a = j * 4 + qi
            return ps[4 + a // 3][:, (a % 3) * 129:(a % 3) * 129 + 129], psn[4 + a // 3]

        qc = 0
        for h in range(8):
            hs = h % 2
            DMA(S, "sp", KT[hs], scr["KDT"][h][:, :], [], ["KT%d" % hs])
            DMA(S, "sp", V[hs][:, :, 0:128], scr["VD"][:, h * 128:(h + 1) * 128].rearrange("(t p) c -> p t c", p=128),
                [], ["V%d" % hs])
            qchunks = [(q0, min(512, SEQ - q0), list(range(NT))) for q0 in range(0, SEQ, 512)]
            if self.need_ctx:
                qchunks.append((SEQ, CTX, [NL, NL + 1]))
            for (q0, nq, kts) in qchunks:
                qs = qc % 2
                qc += 1
                nqt = nq // 128
                DMA(S, "sp", QT[qs][:, 0:nq], scr["QDT"][h][:, q0:q0 + nq], [], ["QT%d" % qs])
                for b in (4, 5, 6):
                    MSET(S, "dve", ps[b][:, :], 0.0, [], [psn[b]])
                for ki, kt in enumerate(kts):
                    sb = ki % 2
                    for j in range(2):
                        b = sb * 2 + j
                        MM(S, [(ps[b][:, 0:nq], KT[hs][j * 64:(j + 1) * 64, kt * 128:(kt + 1) * 128],
                                QT[qs][j * 64:(j + 1) * 64, 0:nq], True, True)],
                           ["KT%d" % hs, "QT%d" % qs], [psn[b]])
                        ACT(S, PT[sb][j][:, 0:nq], ps[b][:, 0:nq], AF.Exp, [psn[b]], ["PT%d%d" % (sb, j)], scale=0.125)
                    for j in range(2):
                        items = []
                        wb = set()
                        for qi in range(nqt):
                            ap_, bn = acc(j, qi)
                            wb.add(bn)
                            items.append((ap_, PT[sb][j][:, qi * 128:(qi + 1) * 128], V[hs][:, kt, :], False, False))
                        MM(S, items, ["PT%d%d" % (sb, j), "V%d" % hs], sorted(wb))
                for qi in range(nqt):
                    a0, b0 = acc(0, qi)
                    a1, b1 = acc(1, qi)
                    RECIP(S, rr[:, 0:1], a0[:, 128:129], [b0], ["rr0"])
                    RECIP(S, rr[:, 1:2], a1[:, 128:129], [b1], ["rr1"])
                    TT(S, "dve", rr[:, 1:2], rr[:, 1:2], nlam, ALU.mult, ["rr1", "nlam"], ["rr1"])
                    TS(S, "dve", o, a0[:, 0:128], rr[:, 0:1], None, ALU.mult, None, [b0, "rr0"], ["o"])
                    STT(S, o, a1[:, 0:128], rr[:, 1:2], o, ALU.mult, ALU.add, [b1, "rr1", "o"], ["o"])
                    ACT(S, sq, o, AF.Square, ["o"], ["sq", "ssq"], accum=ssq)
                    ACT(S, ssq, ssq, AF.Sqrt, ["ssq"], ["ssq"], scale=1.0 / 128, bias=EPS)
                    RECIP(S, ssq, ssq, ["ssq"], ["ssq"])
                    TS(S, "dve", on[:, qi * 128:(qi + 1) * 128], o, ssq, None, ALU.mult, None, ["o", "ssq"], ["on%d" % qi])
                pb = ps[7][:].bitcast(BF16)
                TR(S, [(pb[:, qi * 128:(qi + 1) * 128], on[:, qi * 128:(qi + 1) * 128]) for qi in range(nqt)],
                   K["ident_bf"], ["on%d" % qi for qi in range(nqt)], [psn[7]])
                ACT(S, ob[qs][:, 0:nq], pb[:, 0:nq], AF.Identity, [psn[7], "dn"], ["ob%d" % qs], scale=dn[:, 0:1])
                DMA(S, "sp", scr["YDT"][h][:, q0:q0 + nq], ob[qs][:, 0:nq], ["ob%d" % qs], [])

    def phase_win(self, l):
        S, A, ps, psn, K = self.S, self.A, self.ps, self.psn, self.K
        W, scr = self.W, self.scr
        T, NT, NL, SEQ = self.T, self.NT, self.NL, self.SEQ
        self.newphase()
        KT = A.alloc([4, T], BF16)
        DMA(S, "sp", KT[0:64], scr["WKT"].rearrange("k d t -> d k t"), [], ["KT"])
        V = A.alloc([NT, 4, 65], BF16)
        MSET(S, "pool", V[:, :, :, 64:65], 1.0, [], ["V"])
        for kvh in range(4):
            DMA(S, "sp", V[:, :, kvh, 0:64], scr["WV"][:, kvh * 64:(kvh + 1) * 64].rearrange("(t p) d -> p t d", p=128),
                [], ["V"])
        es = A.alloc([16], F32)
        DMA(S, "sp", es, W["win_sink"][l][0:1, :].broadcast_to([128, 16]), [], ["es"])
        ACT(S, es, es, AF.Exp, ["es"], ["es"])
        QTb = [A.alloc([16, 128], BF16) for _ in range(2)]
        PTw = [A.alloc([512], BF16) for _ in range(3)]
        tmp = [A.alloc([512], BF16) for _ in range(2)]
        dd = A.alloc([4], F32)
        yw = [A.alloc([1024], BF16) for _ in range(2)]
        ob = [A.alloc([8, 128], BF16) for _ in range(2)]
        qbs = list(range(NL)) + ([NL, NL + 1] if self.need_ctx else [])
        cnt = {"s": 0, "p": 0, "t": 0}
        for qi_, qb in enumerate(qbs):
            s = qi_ % 2
            if qb < NL:
                keys = ([(qb - 1, "prev")] if qb > 0 else []) + [(qb, "c")] + \
                       ([(qb + 1, "next")] if qb < NL - 1 else []) + [(NL, "c"), (NL + 1, "c")]
            else:
                keys = [(NL, "c"), (NL + 1, "c")]
            DMA(S, "sp", QTb[s][0:64], scr["WQT"][:, :, qb * 128:(qb + 1) * 128].rearrange("h d t -> d h t"),
                [], ["QTb%d" % s])
            for kvh in range(4):
                ab = 4 + kvh % 2
                MSET(S, "dve", ps[ab][:, 0:260], 0.0, [], [psn[ab]])
                for (kt, kind) in keys:
                    sb = cnt["s"] % 4
                    cnt["s"] += 1
                    p_ = cnt["p"] % 3
                    cnt["p"] += 1
                    MM(S, [(ps[sb][:, :], KT[0:64, kvh, kt * 128:(kt + 1) * 128],
                            QTb[s][0:64, kvh * 4:(kvh + 1) * 4, :], True, True)], ["KT", "QTb%d" % s], [psn[sb]])
                    if kind == "c":
                        ACT(S, PTw[p_], ps[sb][:, :], AF.Exp, [psn[sb]], ["PTw%d" % p_], scale=0.125)
                    else:
                        t_ = cnt["t"] % 2
                        cnt["t"] += 1
                        ACT(S, tmp[t_], ps[sb][:, :], AF.Exp, [psn[sb]], ["tmp%d" % t_], scale=0.125)
                        mk = K["wm_prev"] if kind == "prev" else K["wm_next"]
                        TT(S, "pool", PTw[p_].rearrange("p (g q) -> p g q", g=4),
                           tmp[t_].rearrange("p (g q) -> p g q", g=4),
                           mk.unsqueeze(1).broadcast_to([128, 4, 128]), ALU.mult, ["tmp%d" % t_], ["PTw%d" % p_])
                    MM(S, [(ps[ab][:, g * 65:(g + 1) * 65], PTw[p_][:, g * 128:(g + 1) * 128], V[:, kt, kvh, :], False, False)
                           for g in range(4)], ["PTw%d" % p_, "V"], [psn[ab]])
                av = ps[ab][:, 0:260].rearrange("p (g e) -> p g e", g=4)
                TT(S, "dve", dd, av[:, :, 64], es[:, kvh * 4:(kvh + 1) * 4], ALU.add, [psn[ab], "es"], ["dd"])
                RECIP(S, dd, dd, ["dd"], ["dd"])
                TT(S, "dve", yw[s][:, kvh * 256:(kvh + 1) * 256].rearrange("p (g e) -> p g e", g=4), av[:, :, 0:64],
                   dd.unsqueeze(2).broadcast_to([128, 4, 64]), ALU.mult, [psn[ab], "dd"], ["yw%d_%d" % (s, kvh)])
            pb = ps[7][:].bitcast(BF16)
            TR(S, [(pb[:, k * 128:(k + 1) * 128], yw[s][:, k * 128:(k + 1) * 128]) for k in range(8)], K["ident_bf"],
               ["yw%d_%d" % (s, kvh) for kvh in range(4)], [psn[7]])
            CP(S, "act", ob[s].rearrange("p k t -> p (k t)"), pb[:, 0:1024], [psn[7]], ["ob%d" % s])
            DMA(S, "sp", scr["YWT"][:, :, qb * 128:(qb + 1) * 128].rearrange("k p t -> p k t"), ob[s], ["ob%d" % s], [])

    def phase_ssd(self, l):
        S, A, ps, psn, K = self.S, self.A, self.ps, self.psn, self.K
        W, scr = self.W, self.scr
        T, NT, NL, SEQ = self.T, self.NT, self.NL, self.SEQ
        identf = K["ident_f"]
        self.newphase()
        cw6 = A.alloc([2048], F32)
        DMA(S, "sp", cw6[0:5, :], W["ssd_conv_w"][l][:, :], [], ["cw6"])
        DMA(S, "sp", cw6[5:6, :], W["ssd_conv_b"][l][:, :], [], ["cw6"])
        TR(S, [(ps[0][:, cb * 6:(cb + 1) * 6], cw6[0:6, cb * 128:(cb + 1) * 128]) for cb in range(16)],
           identf[0:6, 0:6], ["cw6"], [psn[0]])
        cw = A.alloc([16, 6], F32)
        CP(S, "dve", cw.rearrange("p a b -> p (a b)"), ps[0][:, 0:96], [psn[0]], ["cw"])
        xin = [A.alloc([T], BF16) for _ in range(2)]
        acc = [A.alloc([T], F32) for _ in range(2)]
        u = [A.alloc([T], BF16) for _ in range(2)]
        stg = [A.alloc([8, 128], BF16) for _ in range(2)]
        tcnt = 0
        for cb in range(16):
            s = cb % 2
            DMA(S, "sp", xin[s], scr["XBCT"][cb][:, :], [], ["xin%d" % s])
            ACT(S, acc[s], xin[s], AF.Identity, ["xin%d" % s, "cw"], ["acc%d" % s], scale=cw[:, cb, 2:3], bias=cw[:, cb, 5:6])
            for j in (0, 1, 3, 4):
                d = j - 2
                for (a, b) in ((0, SEQ), (SEQ, T)):
                    lo, hi = (a - d, b) if d < 0 else (a, b - d)
                    STT(S, acc[s][:, lo:hi], xin[s][:, lo + d:hi + d], cw[:, cb, j:j + 1], acc[s][:, lo:hi],
                        ALU.mult, ALU.add, ["xin%d" % s, "cw", "acc%d" % s], ["acc%d" % s])
            ACT(S, u[s], acc[s], AF.Silu, ["acc%d" % s], ["u%d" % s])
            if cb >= 8:
                DMA(S, "sp", scr["UT"][cb][:, :], u[s], ["u%d" % s], [])
            if cb < 12:
                for t0 in range(0, NT, 8):
                    nt_ = min(8, NT - t0)
                    q = tcnt % 2
                    tcnt += 1
                    pb = ps[1 + q][:].bitcast(BF16)
                    TR(S, [(pb[:, i * 128:(i + 1) * 128], u[s][:, (t0 + i) * 128:(t0 + i + 1) * 128]) for i in range(nt_)],
                       K["ident_bf"], ["u%d" % s], [psn[1 + q]])
                    CP(S, "dve" if q else "act", stg[q].rearrange("p a b -> p (a b)")[:, 0:nt_ * 128], pb[:, 0:nt_ * 128],
                       [psn[1 + q]], ["stg%d" % q])
                    DMA(S, "sp", scr["XSB"][t0 * 128:(t0 + nt_) * 128, cb * 128:(cb + 1) * 128].rearrange("(t p) c -> p t c", p=128),
                        stg[q][:, 0:nt_, :], ["stg%d" % q], [])
        self.newphase()
        abc = A.alloc([32], F32)
        DMA(S, "sp", abc, W["ssd_a_log"][l][0:1, :].broadcast_to([128, 32]), [], ["abc"])
        ACT(S, abc, abc, AF.Exp, ["abc"], ["abc"])
        TS(S, "dve", abc, abc, -1.0, None, ALU.mult, None, ["abc"], ["abc"])
        dsk = A.alloc([16], F32)
        DMA(S, "sp", dsk, W["ssd_d"][l][0:1, :].broadcast_to([128, 16]), [], ["dsk"])
        sn8 = A.alloc([128], F32)
        DMA(S, "sp", sn8[0:8, :], W["ssd_norm"][l].rearrange("o (k p) -> (o k) p", p=128), [], ["sn8"])
        TR(S, [(ps[0][:, 0:8], sn8[0:8, :])], identf[0:8, 0:8], ["sn8"], [psn[0]])
        snc = A.alloc([8], F32)
        CP(S, "dve", snc, ps[0][:, 0:8], [psn[0]], ["snc"])
        xsb = [A.alloc([1536], BF16) for _ in range(2)]
        dtt = [A.alloc([32], F32) for _ in range(2)]
        ad = [A.alloc([32], F32) for _ in range(2)]
        cs = [A.alloc([64], F32) for _ in range(2)]
        dec = [A.alloc([96], F32) for _ in range(2)]
        ncum = [A.alloc([32], F32) for _ in range(2)]
        xdt = [A.alloc([2, 1024], BF16) for _ in range(2)]
        xdtd = [A.alloc([2, 1024], BF16) for _ in range(2)]
        adB = [A.alloc([32, 128], F32) for _ in range(2)]

        def v16(ap):
            return ap.rearrange("p (h e) -> p h e", e=64)

        def bc(ap, n):
            return ap.unsqueeze(2).broadcast_to([128, n, 64])

        def prep(c, s, dirs, full):
            n_ = "_%d" % s
            DMA(S, "sp", xsb[s], scr["XSB"][c * 128:(c + 1) * 128, :], [], ["xsb" + n_])
            DMA(S, "sp", dtt[s], scr["DT"][c * 128:(c + 1) * 128, :], [], ["dtt" + n_])
            TT(S, "dve", ad[s], dtt[s], abc, ALU.mult, ["dtt" + n_, "abc"], ["ad" + n_])
            MM(S, [(ps[0][:, 0:16], K["tri_f"], ad[s][:, 0:16], True, True),
                   (ps[0][:, 16:32], K["tri_b"], ad[s][:, 16:32], True, True),
                   (ps[0][:, 32:64], K["ones_f"], ad[s][:, 0:32], True, True)], ["ad" + n_], [psn[0]])
            CP(S, "dve", cs[s], ps[0][:, 0:64], [psn[0]], ["cs" + n_])
            TT(S, "dve", dec[s][:, 32:64], cs[s][:, 32:64], cs[s][:, 0:32], ALU.subtract, ["cs" + n_], ["decs" + n_])
            ACT(S, dec[s][:, 32:64], dec[s][:, 32:64], AF.Exp, ["decs" + n_], ["decs" + n_])
            ACT(S, dec[s][:, 0:32], cs[s][:, 0:32], AF.Exp, ["cs" + n_], ["deco" + n_])
            ACT(S, dec[s][:, 64:96], cs[s][:, 32:64], AF.Exp, ["cs" + n_], ["dect" + n_])
            if full:
                TS(S, "dve", ncum[s], cs[s][:, 0:32], -1.0, None, ALU.mult, None, ["cs" + n_], ["ncum" + n_])
                CP(S, "pool", adB[s], ad[s].unsqueeze(2).broadcast_to([128, 32, 128]), ["ad" + n_], ["adB" + n_])
            for d in dirs:
                e1 = "pool" if d == 0 else "dve"
                TT(S, e1, v16(xdt[s][:, d, :]), v16(xsb[s][:, 0:1024]), bc(dtt[s][:, d * 16:(d + 1) * 16], 16), ALU.mult,
                   ["xsb" + n_, "dtt" + n_], ["xdt%d" % d + n_])
                TT(S, e1, v16(xdtd[s][:, d, :]), v16(xdt[s][:, d, :]), bc(dec[s][:, 32 + d * 16:32 + (d + 1) * 16], 16), ALU.mult,
                   ["xdt%d" % d + n_, "decs" + n_], ["xdtd%d" % d + n_])

        def state_update(H, s, d, hname):
            n_ = "_%d" % s
            for g in range(4):
                b = 2 + g % 2
                MM(S, [(ps[b][:, 0:256], xsb[s][:, 1024 + g * 128:1024 + (g + 1) * 128], xdtd[s][:, d, g * 256:(g + 1) * 256], True, True)],
                   ["xsb" + n_, "xdtd%d" % d + n_], [psn[b]])
                hv = v16(H[:, g * 256:(g + 1) * 256])
                TT(S, "dve", hv, hv, bc(dec[s][:, 64 + d * 16 + g * 4:64 + d * 16 + (g + 1) * 4], 4), ALU.mult,
                   [hname, "dect" + n_], [hname])
                TT(S, "dve", H[:, g * 256:(g + 1) * 256], H[:, g * 256:(g + 1) * 256], ps[b][:, 0:256], ALU.add,
                   [hname, psn[b]], [hname])

        Hb = A.alloc([1024], F32)
        Hbb = [A.alloc([1024], BF16) for _ in range(2)]
        MSET(S, "dve", Hb, 0.0, [], ["Hb"])
        order_b = [NL + 1, NL] + list(range(NL - 1, -1, -1))
        for i, c in enumerate(order_b):
            s = i % 2
            prep(c, s, [1], False)
            CP(S, "act", Hbb[s], Hb, ["Hb"], ["Hbb%d" % s])
            DMA(S, "sp", scr["HB"][c], Hbb[s], ["Hbb%d" % s], [])
            state_update(Hb, s, 1, "Hb")
        self.newphase_keep()
        Hf = A.alloc([1024], F32)
        Hfb = A.alloc([1024], BF16)
        MSET(S, "dve", Hf, 0.0, [], ["Hf"])
        bct = [A.alloc([8, 128], BF16) for _ in range(2)]
        hb = [A.alloc([1024], BF16) for _ in range(2)]
        zs = [A.alloc([1024], BF16) for _ in range(2)]
        gt = [A.alloc([128], BF16) for _ in range(2)]
        lt = [A.alloc([128], BF16) for _ in range(2)]
        mt = [A.alloc([128], BF16) for _ in range(2)]
        yo = [A.alloc([1024], F32) for _ in range(2)]
        xd = A.alloc([1024], F32)
        yt = A.alloc([1024], F32)
        junk = A.alloc([1024], BF16)
        ssq = A.alloc([1], F32)
        yn = A.alloc([1024], BF16)
        ob = [A.alloc([8, 128], BF16) for _ in range(2)]
        order_f = [NL, NL + 1] + list(range(NL))
        kk = 0
        for i, c in enumerate(order_f):
            s = i % 2
            n_ = "_%d" % s
            want_out = (c < NL) or self.need_ctx
            prep(c, s, [0, 1] if want_out else [0], want_out)
            if want_out:
                DMA(S, "sp", bct[s], scr["UT"][8:16, :, c * 128:(c + 1) * 128].rearrange("k p t -> p k t"), [], ["bct" + n_])
                DMA(S, "sp", hb[s], scr["HB"][c], [], ["hb" + n_])
                DMA(S, "sp", zs[s], scr["ZS"][c * 128:(c + 1) * 128, :], [], ["zs" + n_])
                CP(S, "act", Hfb, Hf, ["Hf"], ["Hfb"])
                MSET(S, "dve", ps[4][:, :], 0.0, [], [psn[4]])
                MSET(S, "dve", ps[5][:, :], 0.0, [], [psn[5]])
                for g in range(4):
                    MM(S, [(ps[1][:, 0:128], bct[s][:, g, :], bct[s][:, 4 + g, :], True, True)], ["bct" + n_], [psn[1]])
                    CP(S, "act", gt[g % 2], ps[1][:, 0:128], [psn[1]], ["gt%d" % (g % 2)])
                    for d in range(2):
                        for hh in range(4):
                            h = g * 4 + hh
                            c_ = d * 16 + h
                            q = kk % 2
                            kk += 1
                            b = 2 + q
                            MM(S, [(ps[b][:, 0:128], adB[s][:, c_, :], K["tri_f"] if d == 0 else K["tri_b"], True, False),
                                   (ps[b][:, 0:128], identf, K["neg_f"] if d == 0 else K["neg_b"], False, True)],
                               ["adB" + n_], [psn[b]])
                            ACT(S, lt[q], ps[b][:, 0:128], AF.Exp, [psn[b], "ncum" + n_], ["lt%d" % q], bias=ncum[s][:, c_:c_ + 1])
                            TT(S, "pool" if q else "dve", mt[q], lt[q], gt[g % 2], ALU.mult, ["lt%d" % q, "gt%d" % (g % 2)], ["mt%d" % q])
                            yb = 4 + h // 8
                            MM(S, [(ps[yb][:, (h % 8) * 64:(h % 8 + 1) * 64], mt[q], xdt[s][:, d, h * 64:(h + 1) * 64], False, False)],
                               ["mt%d" % q, "xdt%d" % d + n_], [psn[yb]])
                for d in range(2):
                    hsrc = Hfb if d == 0 else hb[s]
                    hn = "Hfb" if d == 0 else "hb" + n_
                    for g in range(4):
                        MM(S, [(ps[6 + g // 2][:, (g % 2) * 256:(g % 2 + 1) * 256], bct[s][:, 4 + g, :], hsrc[:, g * 256:(g + 1) * 256], True, True)],
                           ["bct" + n_, hn], [psn[6 + g // 2]])
                    for half in range(2):
                        TT(S, "dve", yo[d][:, half * 512:(half + 1) * 512].rearrange("p (h e) -> p h e", e=64),
                           ps[6 + half][:, :].rearrange("p (h e) -> p h e", e=64),
                           bc(dec[s][:, d * 16 + half * 8:d * 16 + half * 8 + 8], 8), ALU.mult,
                           [psn[6 + half], "deco" + n_], ["yo%d_%d" % (d, half)])
                TT(S, "pool", yo[0], yo[0], yo[1], ALU.add, ["yo0_0", "yo0_1", "yo1_0", "yo1_1"], ["yo0_0", "yo0_1"])
                TT(S, "pool", v16(xd), v16(xsb[s][:, 0:1024]), bc(dsk, 16), ALU.mult, ["xsb" + n_, "dsk"], ["xd"])
                TT(S, "pool", yo[0], yo[0], xd, ALU.add, ["yo0_0", "yo0_1", "xd"], ["yo0_0", "yo0_1"])
                for half in range(2):
                    TT(S, "dve", yt[:, half * 512:(half + 1) * 512], ps[4 + half][:, :], yo[0][:, half * 512:(half + 1) * 512], ALU.add,
                       [psn[4 + half], "yo0_0", "yo0_1"], ["yt%d" % half])
                TT(S, "dve", yt, yt, zs[s], ALU.mult, ["yt0", "yt1", "zs" + n_], ["yt0", "yt1"])
                ACT(S, junk, yt, AF.Square, ["yt0", "yt1"], ["junk", "ssq"], accum=ssq)
                ACT(S, ssq, ssq, AF.Sqrt, ["ssq"], ["ssq"], scale=1.0 / 1024, bias=EPS)
                RECIP(S, ssq, ssq, ["ssq"], ["ssq"])
                TS(S, "dve", yn, yt, ssq, None, ALU.mult, None, ["yt0", "yt1", "ssq"], ["yn"])
                pb = ps[1][:].bitcast(BF16)
                TR(S, [(pb[:, k * 128:(k + 1) * 128], yn[:, k * 128:(k + 1) * 128]) for k in range(8)], K["ident_bf"], ["yn"], [psn[1]])
                for k in range(8):
                    ACT(S, ob[s][:, k, :], pb[:, k * 128:(k + 1) * 128], AF.Identity, [psn[1], "snc"], ["ob" + n_], scale=snc[:, k:k + 1])
                DMA(S, "sp", scr["YST"][:, :, c * 128:(c + 1) * 128].rearrange("k p t -> p k t"), ob[s], ["ob" + n_], [])
            state_update(Hf, s, 0, "Hf")

    def load_rowG(self, l, which, ngi):
        S, A = self.S, self.A
        rowG = [A.alloc([D], F32) for _ in range(2)]
        ngb = A.alloc([D], F32)
        DMA(S, "sp", ngb, self.W["norm_g"][l][ngi:ngi + 1, :].broadcast_to([128, D]), [], ["ngb"])
        for r in range(2):
            DMA(S, "sp", rowG[r], self.scr["MOD"][r:r + 1, which * D:(which + 1) * D].broadcast_to([128, D]), [], ["rowG%d" % r])
            TT(S, "dve", rowG[r], rowG[r], ngb, ALU.mult, ["rowG%d" % r, "ngb"], ["rowG%d" % r])
        return rowG

    def resid_epilogue(self, y, yname, g, rowG, bufs, i, dst="X"):
        S = self.S
        xt, junk, ss = bufs
        s = i % 2
        r = 0 if g < self.NL else 1
        X = self.scr["X"]
        rows = slice(g * 128, (g + 1) * 128)
        DMA(S, "sp", xt[s], X[rows, :], [], ["ext%d" % s])
        ACT(S, junk, y, AF.Square, [yname], ["ejunk", "ess%d" % s], accum=ss[s])
        ACT(S, ss[s], ss[s], AF.Sqrt, ["ess%d" % s], ["ess%d" % s], scale=1.0 / D, bias=EPS)
        RECIP(S, ss[s], ss[s], ["ess%d" % s], ["ess%d" % s])
        STT(S, y, y, ss[s], rowG[r], ALU.mult, ALU.mult, [yname, "ess%d" % s, "rowG%d" % r], [yname])
        TT(S, "pool", xt[s], xt[s], y, ALU.add, ["ext%d" % s, yname], ["ext%d" % s])
        DMA(S, "sp", self.scr[dst][rows, :], xt[s], ["ext%d" % s], [])

    def phase_merge(self, l):
        S, A, ps, psn, K = self.S, self.A, self.ps, self.psn, self.K
        W, scr = self.W, self.scr
        T, NT, NL, SEQ = self.T, self.NT, self.NL, self.SEQ
        self.newphase()
        rowG = self.load_rowG(l, 2, 1)
        blocks = [list(range(t, min(NL, t + 2))) for t in range(0, NL, 2)]
        if self.need_ctx:
            blocks.append([NL, NL + 1])
        yin = [[A.alloc([8, 256], BF16) for _ in range(3)] for _ in range(2)]
        wb = [[A.alloc([8, 512], BF16) for _ in range(3)] for _ in range(2)]
        gg = [[A.alloc([4, 256], BF16) for _ in range(3)] for _ in range(2)]
        mT = A.alloc([16, 256], BF16)
        tt_ = [A.alloc([256], F32) for _ in range(3)]
        wo = [A.alloc([16, 512], BF16) for _ in range(2)]
        ysb = A.alloc([2, D], F32)
        xt = [A.alloc([D], F32) for _ in range(2)]
        junk = A.alloc([D], BF16)
        ss = [A.alloc([1], F32) for _ in range(2)]
        names = ["YDT", "YST", "YWT"]
        wn = ["w_br_diff", "w_br_ssd", "w_br_win"]
        wc = 0
        oc = 0
        ec = 0
        for bi, tiles in enumerate(blocks):
            bs = bi % 2
            tok0 = tiles[0] * 128
            ntok = len(tiles) * 128
            for br in range(3):
                DMA(S, "sp", yin[bs][br][:, :, 0:ntok], scr[names[br]][:, :, tok0:tok0 + ntok].rearrange("k p t -> p k t"),
                    [], ["yin%d%d" % (bs, br)])
            for f4 in range(4):
                ws = wc % 2
                wc += 1
                for br in range(3):
                    DMA(S, "pool", wb[ws][br], W[wn[br]][l][:, f4 * 512:(f4 + 1) * 512].rearrange("(k p) c -> p k c", p=128),
                        [], ["wb%d%d" % (ws, br)])
                    DMA(S, "sp", gg[ws][br][:, :, 0:ntok],
                        scr["GT"][br * 16 + f4 * 4:br * 16 + f4 * 4 + 4, :, tok0:tok0 + ntok].rearrange("k p t -> p k t"),
                        [], ["gg%d%d" % (ws, br)])
                for m in range(4):
                    f = f4 * 4 + m
                    for br in range(3):
                        MM(S, [(ps[br][:, 0:ntok], wb[ws][br][:, k, m * 128:(m + 1) * 128], yin[bs][br][:, k, 0:ntok], k == 0, k == 7)
                               for k in range(8)], ["wb%d%d" % (ws, br), "yin%d%d" % (bs, br)], [psn[br]])
                        TT(S, "dve", tt_[br][:, 0:ntok], ps[br][:, 0:ntok], gg[ws][br][:, m, 0:ntok], ALU.mult,
                           [psn[br], "gg%d%d" % (ws, br)], ["tt%d" % br])
                    TT(S, "pool", tt_[0][:, 0:ntok], tt_[0][:, 0:ntok], tt_[1][:, 0:ntok], ALU.add, ["tt0", "tt1"], ["tt0"])
                    TT(S, "pool", mT[:, f, 0:ntok], tt_[0][:, 0:ntok], tt_[2][:, 0:ntok], ALU.add, ["tt0", "tt2"], ["mT%d" % f])
            for nch in range(4):
                os_ = oc % 2
                oc += 1
                DMA(S, "pool", wo[os_], W["w_out"][l][:, nch * 512:(nch + 1) * 512].rearrange("(k p) c -> p k c", p=128),
                    [], ["wo%d" % os_])
                for ti in range(len(tiles)):
                    b = 3 + (ti + nch) % 4
                    MM(S, [(ps[b][:, :], mT[:, k, ti * 128:(ti + 1) * 128], wo[os_][:, k, :], k == 0, k == 15) for k in range(16)],
                       ["wo%d" % os_] + ["mT%d" % f for f in range(16)], [psn[b]])
                    CP(S, "act", ysb[:, ti, nch * 512:(nch + 1) * 512], ps[b][:, :], [psn[b]], ["ysbm%d" % ti])
            for ti, g in enumerate(tiles):
                yname = "ysbm%d" % ti
                self.resid_epilogue(ysb[:, ti, :], yname, g, rowG, (xt, junk, ss), ec)
                ec += 1

    def phase_ffn(self, l):
        S, A, ps, psn, K = self.S, self.A, self.ps, self.psn, self.K
        W, scr = self.W, self.scr
        T, NT, NL, SEQ = self.T, self.NT, self.NL, self.SEQ
        identf = K["ident_f"]
        self.newphase()
        floor0 = A.floor
        rowG = self.load_rowG(l, 5, 3)
        fcT = A.alloc([88, 4], F32)
        fc4 = A.alloc([2816], F32)
        for pc in range(4):
            DMA(S, "sp", fc4[0:3, :], W["ffn_conv_w"][l][:, pc * 2816:(pc + 1) * 2816], [], ["fc4"])
            DMA(S, "sp", fc4[3:4, :], W["ffn_conv_b"][l][:, pc * 2816:(pc + 1) * 2816], [], ["fc4"])
            TR(S, [(ps[0][:, j * 4:(j + 1) * 4], fc4[0:4, j * 128:(j + 1) * 128]) for j in range(22)], identf[0:4, 0:4],
               ["fc4"], [psn[0]])
            CP(S, "dve", fcT[:, pc * 22:(pc + 1) * 22, :].rearrange("p a b -> p (a b)"), ps[0][:, 0:88], [psn[0]], ["fcT"])
        S.barrier()
        A.keep()
        NB = 512
        blocks = [(t0, min(NB, SEQ - t0), t0 > 0, t0 + NB < SEQ) for t0 in range(0, SEQ, NB)]
        if self.need_ctx:
            blocks.append((SEQ, CTX, False, False))
        for (t0, nb, lv, rv) in blocks:
            self.newphase()
            g0 = t0 // 128
            ntl = nb // 128
            actT = A.alloc([44, nb], BF16)
            base = A.top
            h2T = A.alloc([16, nb + 256], BF16)
            jobs = [(g0 + i, 0, 128, 128 + i * 128) for i in range(ntl)]
            if lv:
                jobs.append((g0 - 1, 0, 128, 0))
            if rv:
                jobs.append((g0 + ntl, 0, 128, 128 + nb))
            top1 = A.top
            self.norm_tiles(jobs, self.colA2, self.colB2, h2T, [(0, 1), (2, 3), (4, 5), (6, 7)])
            S.barrier()
            A.top = top1
            wa = [A.alloc([16, 256], BF16) for _ in range(2)]
            wg = [A.alloc([16, 256], BF16) for _ in range(2)]
            ta = [A.alloc([256], F32) for _ in range(2)]
            tg = [A.alloc([256], F32) for _ in range(2)]
            chunks = [(c, min(256, nb - c)) for c in range(0, nb, 256)]
            cc_ = 0

            def loadup(i):
                DMA(S, "pool", wa[i % 2], W["ffn_w_up"][l][:, i * 256:(i + 1) * 256].rearrange("(k p) c -> p k c", p=128),
                    [], ["wa%d" % (i % 2)])
                DMA(S, "pool", wg[i % 2], W["ffn_w_up"][l][:, D_FF + i * 256:D_FF + (i + 1) * 256].rearrange("(k p) c -> p k c", p=128),
                    [], ["wg%d" % (i % 2)])
            loadup(0)
            for fb2 in range(22):
                if fb2 + 1 < 22:
                    loadup(fb2 + 1)
                ws = fb2 % 2
                for m in range(2):
                    fb = fb2 * 2 + m
                    for (c, n) in chunks:
                        q = cc_ % 2
                        cc_ += 1
                        first, last = (c == 0), (c + n == nb)
                        lo = 1 if (first and not lv) else 0
                        hi = n - 1 if (last and not rv) else n
                        for (wt_, wname, bnk, tx, txn, fidx) in ((wa[ws], "wa%d" % ws, q * 2, ta[q], "ta%d" % q, fb),
                                                               (wg[ws], "wg%d" % ws, q * 2 + 1, tg[q], "tg%d" % q, 44 + fb)):
                            MM(S, [(ps[bnk][:, 0:n + 2], wt_[:, k, m * 128:(m + 1) * 128], h2T[:, k, 127 + c:127 + c + n + 2], k == 0, k == 15)
                                   for k in range(16)], [wname], [psn[bnk]])
                            ACT(S, tx[:, 0:n], ps[bnk][:, 1:n + 1], AF.Identity, [psn[bnk]], [txn],
                                scale=fcT[:, fidx, 1:2], bias=fcT[:, fidx, 3:4])
                            STT(S, tx[:, lo:n], ps[bnk][:, lo:n], fcT[:, fidx, 0:1], tx[:, lo:n], ALU.mult, ALU.add,
                                [psn[bnk], txn], [txn])
                            STT(S, tx[:, 0:hi], ps[bnk][:, 2:2 + hi], fcT[:, fidx, 2:3], tx[:, 0:hi], ALU.mult, ALU.add,
                                [psn[bnk], txn], [txn])
                        ACT(S, ta[q][:, 0:n], ta[q][:, 0:n], AF.Silu, ["ta%d" % q], ["ta%d" % q])
                        TT(S, "pool", actT[:, fb, c:c + n], ta[q][:, 0:n], tg[q][:, 0:n], ALU.mult, ["ta%d" % q, "tg%d" % q], ["actT"])
            S.barrier()
            A.top = base
            yT = A.alloc([16, nb], F32)
            top3 = A.top
            wd = [A.alloc([44, 256], BF16) for _ in range(2)]

            def loaddn(i):
                DMA(S, "pool", wd[i % 2], W["ffn_w_down"][l][:, i * 256:(i + 1) * 256].rearrange("(k p) c -> p k c", p=128),
                    [], ["wd%d" % (i % 2)])
            loaddn(0)
            bc_ = 0
            for fo2 in range(8):
                if fo2 + 1 < 8:
                    loaddn(fo2 + 1)
                ws = fo2 % 2
                for m in range(2):
                    fo = fo2 * 2 + m
                    b = 4 + bc_ % 4
                    bc_ += 1
                    MM(S, [(ps[b][:, 0:nb], wd[ws][:, k, m * 128:(m + 1) * 128], actT[:, k, 0:nb], k == 0, k == 43) for k in range(44)],
                       ["wd%d" % ws], [psn[b]])
                    CP(S, "act" if fo % 2 else "dve", yT[:, fo, :], ps[b][:, 0:nb], [psn[b]], ["yT%d" % fo])
            S.barrier()
            A.top = top3
            ysb = [A.alloc([D], F32) for _ in range(2)]
            xt = [A.alloc([D], F32) for _ in range(2)]
            junk = A.alloc([D], BF16)
            ss = [A.alloc([1], F32) for _ in range(2)]
            for ti in range(ntl):
                s = ti % 2
                for q4 in range(4):
                    TR(S, [(ps[q4][:, j * 128:(j + 1) * 128], yT[:, q4 * 4 + j, ti * 128:(ti + 1) * 128]) for j in range(4)], identf,
                       [], [psn[q4]])
                    CP(S, "act" if q4 % 2 else "dve", ysb[s][:, q4 * 512:(q4 + 1) * 512], ps[q4][:, :], [psn[q4]], ["fysb%d" % s])
                self.resid_epilogue(ysb[s], "fysb%d" % s, g0 + ti, rowG, (xt, junk, ss), ti, dst="X2")
        S.barrier()
        nrow = T if self.need_ctx else SEQ
        DMA(S, "sp", scr["X"][0:nrow, :], scr["X2"][0:nrow, :], [], [])
        S.barrier()
        A.floor = floor0


_CACHE = {}
SEQ_FULL = 4096
N_CORES = 4


def kernel(**inputs):
    SEQ = SEQ_FULL
    if "nc" not in _CACHE:
        _CACHE["nc"] = Builder(SEQ).build()
        _CACHE["consts"] = host_consts(SEQ)
    nc = _CACHE["nc"]
    consts = _CACHE["consts"]
    f32 = np.float32
    shared = {}
    for name, shp in W_SPECS:
        shared[name] = np.ascontiguousarray(np.asarray(inputs[name], dtype=f32).reshape([DEPTH] + shp))
    shared.update(consts)
    in_maps = []
    for b in range(N_CORES):
        m = dict(shared)
        m["x"] = np.ascontiguousarray(np.asarray(inputs["x"][b], dtype=f32))
        m["ctx"] = np.ascontiguousarray(np.asarray(inputs["ctx"][b], dtype=f32))
        m["cvec"] = np.ascontiguousarray(np.stack([np.asarray(inputs["c"][b], dtype=f32),
                                                   np.asarray(inputs["c_ctx"], dtype=f32)]))
        in_maps.append(m)
    res = run_bass_kernel_spmd(nc, in_maps, core_ids=list(range(N_CORES)))
    out = np.stack([np.asarray(res.results[b]["out"]) for b in range(N_CORES)], axis=0)
    return out.astype(f32)
```

```python
import math
import numpy as np
import ml_dtypes
import concourse.bass as bass
import concourse.mybir as mybir
from concourse.bass_utils import run_bass_kernel_spmd

F32 = mybir.dt.float32
BF16 = mybir.dt.bfloat16
AF = mybir.ActivationFunctionType
ALU = mybir.AluOpType
AX = mybir.AxisListType

D = 2048
DEPTH = 2
CTX = 256
GRID_W = 64
EPS = 1e-6
IN_SIZES = (1024, 1024, 1024, 1024, 2048, 32, 1024, 256, 256, 6144)
IN_OFF = [0]
for _s in IN_SIZES:
    IN_OFF.append(IN_OFF[-1] + _s)
IN_WIDTH = IN_OFF[-1]
D_FF = 5632
NDMA_SEM = 12


class _Op:
    __slots__ = ("eng", "fn", "deps", "sig", "sigval", "dma", "dsem", "dval", "dprev")


class Sched:
    ENGS = ("pe", "act", "dve", "pool", "sp")

    def __init__(self, nc):
        self.nc = nc
        self.ops = []
        self.last_w = {}
        self.readers = {}
        self.last_on = {e: None for e in self.ENGS}
        self.open_dma = {}
        self.dcnt = {e: 0 for e in self.ENGS}

    def op(self, eng, fn, reads=(), writes=(), dma=False):
        o = _Op()
        o.eng, o.fn, o.dma = eng, fn, dma
        o.sig = False
        psr = [r for r in reads if isinstance(r, str) and r.startswith("ps") and r[2:].isdigit()]
        if psr:
            reads = [r for r in reads if r not in psr]
            writes = list(writes) + psr
        deps = set()
        for r in reads:
            w = self.last_w.get(r)
            if w is not None:
                deps.add(w)
        for w_ in writes:
            w = self.last_w.get(w_)
            if w is not None:
                deps.add(w)
            for rd in self.readers.get(w_, ()):
                deps.add(rd)
        for r in reads:
            self.readers.setdefault(r, []).append(o)
        for w_ in writes:
            self.last_w[w_] = o
            self.readers[w_] = []
        deps.discard(o)
        o.deps = deps
        self.ops.append(o)
        if not dma:
            self.last_on[eng] = o
        if dma:
            i = self.dcnt[eng]
            self.dcnt[eng] += 1
            o.dsem = (eng, i % NDMA_SEM)
            o.dval = 16 * (i // NDMA_SEM + 1)
            o.dprev = 16 * (i // NDMA_SEM)
            self.open_dma[o.dsem] = o
        return o

    def barrier(self):
        pend = [o for o in self.last_on.values() if o is not None and not o.dma] + list(self.open_dma.values())
        for e in self.ENGS:
            o = _Op()
            o.eng, o.fn, o.dma, o.sig = e, None, False, False
            o.deps = set(pend)
            self.ops.append(o)
        self.open_dma = {}
        self.last_w = {}
        self.readers = {}

    def emit(self):
        nc = self.nc
        for o in self.ops:
            for d in o.deps:
                if not d.dma:
                    d.sig = True
        cnt = {e: 0 for e in self.ENGS}
        dcnt = self.dcnt
        for o in self.ops:
            if o.dma:
                pass
            elif o.sig:
                cnt[o.eng] += 1
                o.sigval = cnt[o.eng]
        per = {e: [o for o in self.ops if o.eng == e] for e in self.ENGS}
        import contextlib
        with contextlib.ExitStack() as st:
            esem = {e: st.enter_context(nc.semaphore("s_" + e)) for e in self.ENGS}
            dsem = {}
            for e in ("sp", "act", "pool"):
                if dcnt[e]:
                    for i in range(NDMA_SEM):
                        dsem[(e, i)] = st.enter_context(nc.semaphore("d_%s%d" % (e, i)))
            block = st.enter_context(nc.Block())

            def run(ename, eng):
                waited = {}

                def wait(sem_key, sem, val):
                    if waited.get(sem_key, 0) < val:
                        eng.wait_ge(sem, val)
                        waited[sem_key] = val

                for o in per[ename]:
                    for d in o.deps:
                        if d.dma:
                            wait(d.dsem, dsem[d.dsem], d.dval)
                        else:
                            if d.eng == ename and ename == "pe":
                                continue
                            wait(d.eng, esem[d.eng], d.sigval)
                    if o.fn is None:
                        continue
                    if o.dma and o.dprev:
                        wait(o.dsem, dsem[o.dsem], o.dprev)
                    inst = o.fn(eng)
                    if o.dma:
                        inst.then_inc(dsem[o.dsem], 16)
                    elif o.sig:
                        inst.then_inc(esem[ename], 1)

            @block.tensor
            def _(e):
                run("pe", e)

            @block.scalar
            def _(e):
                run("act", e)

            @block.vector
            def _(e):
                run("dve", e)

            @block.gpsimd
            def _(e):
                run("pool", e)

            @block.sync
            def _(e):
                run("sp", e)


ENG_OF = {"act": "act", "dve": "dve", "pool": "pool"}


def MM(S, items, r, w):
    items = list(items)

    def fn(e):
        last = None
        for (out, lhsT, rhs, st, sp) in items:
            last = e.matmul(out, lhsT=lhsT, rhs=rhs, start=st, stop=sp, skip_group_check=True)
        return last
    return S.op("pe", fn, r, w)


def TR(S, items, ident, r, w):
    items = list(items)

    def fn(e):
        last = None
        for (out, in_) in items:
            last = e.transpose(out, in_, ident)
        return last
    return S.op("pe", fn, r, w)


def ACT(S, out, in_, func, r, w, scale=None, bias=None, accum=None):
    kw = {}
    if scale is not None:
        kw["scale"] = scale
    if bias is not None:
        kw["bias"] = bias
    if accum is not None:
        kw["accum_out"] = accum
    return S.op("act", lambda e: e.activation(out=out, in_=in_, func=func, **kw), r, w)


def TS(S, eng, out, in0, s1, s2, op0, op1, r, w):
    if op1 is None:
        return S.op(eng, lambda e: e.tensor_scalar(out, in0, s1, None, op0), r, w)
    return S.op(eng, lambda e: e.tensor_scalar(out, in0, s1, s2, op0, op1), r, w)


def TT(S, eng, out, in0, in1, op, r, w):
    return S.op(eng, lambda e: e.tensor_tensor(out, in0, in1, op), r, w)


def STT(S, out, in0, scalar, in1, op0, op1, r, w):
    return S.op("dve", lambda e: e.scalar_tensor_tensor(out, in0, scalar, in1, op0, op1), r, w)


def CP(S, eng, out, in_, r, w):
    if eng == "act":
        return S.op("act", lambda e: e.copy(out, in_), r, w)
    return S.op(eng, lambda e: e.tensor_copy(out, in_), r, w)


def MSET(S, eng, ap, val, r, w):
    return S.op(eng, lambda e: e.memset(ap, val), r, w)


def RECIP(S, out, in_, r, w):
    return S.op("dve", lambda e: e.reciprocal(out, in_), r, w)


def DMA(S, q, out, in_, r, w):
    return S.op(q, lambda e: e.dma_start(out=out, in_=in_), r, w, dma=True)


class Arena:
    def __init__(self, ap, nwords):
        self.ap = ap
        self.n = nwords
        self.top = 0
        self.floor = 0
        self.uid = 0

    def alloc(self, shape, dtype, name=None):
        nelem = 1
        for s in shape:
            nelem *= s
        nbytes = nelem * (2 if dtype == BF16 else 4)
        nw = (nbytes + 31) // 32 * 8
        assert self.top + nw <= self.n, "arena overflow %s %d+%d>%d" % (name, self.top, nw, self.n)
        a = self.ap[:, self.top:self.top + nw]
        self.top += nw
        if dtype == BF16:
            a = a.bitcast(BF16)
        a = a[:, 0:nelem]
        if len(shape) == 2:
            a = a.rearrange("p (a b) -> p a b", a=shape[0])
        elif len(shape) == 3:
            a = a.rearrange("p (a b c) -> p a b c", a=shape[0], b=shape[1])
        self.uid += 1
        return a

    def keep(self):
        self.floor = self.top

    def reset(self):
        self.top = self.floor


def host_consts(SEQ):
    T = SEQ + CTX
    bf = ml_dtypes.bfloat16
    c = {}
    c["ident_bf"] = np.eye(128, dtype=np.float32).astype(bf)
    c["ident_f"] = np.eye(128, dtype=np.float32)
    pm = np.zeros((128, 128), np.float32)
    for base in (0, 64):
        for i in range(32):
            pm[base + i + 32, base + i] = -1.0
            pm[base + i, base + i + 32] = 1.0
    c["perm_bf"] = pm.astype(bf)
    s_ = np.arange(128)[:, None]
    l_ = np.arange(128)[None, :]
    c["tri_f"] = (s_ <= l_).astype(np.float32)
    c["tri_b"] = (s_ >= l_).astype(np.float32)
    c["ones_f"] = np.ones((128, 128), np.float32)
    c["neg_f"] = np.where(l_ >= s_, 0.0, -30000.0).astype(np.float32)
    c["neg_b"] = np.where(l_ <= s_, 0.0, -30000.0).astype(np.float32)
    c["wm_prev"] = (s_ >= l_).astype(np.float32).astype(bf)
    c["wm_next"] = (s_ <= l_).astype(np.float32).astype(bf)
    t = np.arange(SEQ)
    row = (t // GRID_W).astype(np.float32)
    col = (t % GRID_W).astype(np.float32)
    inv = (10000.0 ** (-np.arange(0, 32, 2, dtype=np.float32) / 32.0)).astype(np.float32)
    ang = np.concatenate([row[:, None] * inv, col[:, None] * inv], axis=-1).astype(np.float32)
    cosT = np.ones((128, T), np.float32)
    sinT = np.zeros((128, T), np.float32)
    for p in range(128):
        cosT[p, :SEQ] = np.cos(ang[:, p % 32])
        sinT[p, :SEQ] = np.sin(ang[:, p % 32])
    c["cosT"] = cosT
    c["sinT"] = sinT
    return c


CONST_SPECS = [("ident_bf", BF16, None), ("ident_f", F32, None), ("perm_bf", BF16, None),
               ("tri_f", F32, None), ("tri_b", F32, None), ("ones_f", F32, None),
               ("neg_f", F32, None), ("neg_b", F32, None), ("wm_prev", BF16, None), ("wm_next", BF16, None)]

W_SPECS = [("w_ada", [D, 6 * D]), ("b_ada", [1, 6 * D]), ("norm_g", [4, D]), ("w_in", [D, IN_WIDTH]),
           ("diff_lambda", [1, 256]), ("diff_norm", [1, 128]), ("ssd_conv_w", [5, 2048]),
           ("ssd_conv_b", [1, 2048]), ("ssd_a_log", [1, 32]), ("ssd_dt_bias", [1, 32]),
           ("ssd_d", [1, 16]), ("ssd_norm", [1, 1024]), ("win_sink", [1, 16]),
           ("w_br_diff", [1024, D]), ("w_br_ssd", [1024, D]), ("w_br_win", [1024, D]),
           ("w_out", [D, D]), ("ffn_w_up", [D, 2 * D_FF]), ("ffn_conv_w", [3, 2 * D_FF]),
           ("ffn_conv_b", [1, 2 * D_FF]), ("ffn_w_down", [D_FF, D])]


class Builder:
    def __init__(self, SEQ, debug=False, nlayers=DEPTH, stop_after=None):
        self.SEQ = SEQ
        self.T = SEQ + CTX
        self.NL = SEQ // 128
        self.NT = self.T // 128
        self.debug = debug
        self.nlayers = nlayers
        self.stop_after = stop_after
        nc = bass.Bass("TRN2", target_bir_lowering=False)
        self.nc = nc
        self.S = Sched(nc)
        T = self.T
        dt = nc.dram_tensor
        self.x_in = dt("x", [SEQ, D], F32, kind="ExternalInput").ap()
        self.ctx_in = dt("ctx", [CTX, D], F32, kind="ExternalInput").ap()
        self.cvec = dt("cvec", [2, D], F32, kind="ExternalInput").ap()
        wspec = dict(W_SPECS)

        class _Lazy(dict):
            def __missing__(d_, name):
                d_[name] = dt(name, [DEPTH] + wspec[name], F32, kind="ExternalInput").ap()
                return d_[name]
        self.W = _Lazy()
        self.C = {}
        for name, dty, _ in CONST_SPECS:
            self.C[name] = dt(name, [128, 128], dty, kind="ExternalInput").ap()
        self.C["cosT"] = dt("cosT", [128, T], F32, kind="ExternalInput").ap()
        self.C["sinT"] = dt("sinT", [128, T], F32, kind="ExternalInput").ap()
        self.out = dt("out", [SEQ, D], F32, kind="ExternalOutput").ap()
        sk = "ExternalOutput" if debug else "Internal"
        self.scr = {}
        for name, shp, dty in [
            ("X", [T, D], F32), ("X2", [T, D], F32), ("MOD", [2, 6 * D], F32),
            ("QDT", [8, 128, T], BF16), ("KDT", [8, 128, T], BF16), ("VD", [T, 1024], BF16),
            ("ZS", [T, 1024], BF16), ("XBCT", [16, 128, T], BF16), ("DT", [T, 32], F32),
            ("WQT", [16, 64, T], BF16), ("WKT", [4, 64, T], BF16), ("WV", [T, 256], BF16),
            ("GT", [48, 128, T], BF16),
            ("UT", [16, 128, T], BF16), ("XSB", [T, 1536], BF16), ("HB", [self.NT, 128, 1024], BF16),
            ("YDT", [8, 128, T], BF16), ("YST", [8, 128, T], BF16), ("YWT", [8, 128, T], BF16),
        ]:
            self.scr[name] = dt("s_" + name, shp, dty, kind=sk).ap()

    def build(self):
        nc, S = self.nc, self.S
        import contextlib
        with contextlib.ExitStack() as st:
            NW = 50000
            arena = st.enter_context(nc.sbuf_tensor("arena", [128, NW], F32))
            self.A = Arena(arena[:, :], NW)
            self.ps = [st.enter_context(nc.psum_tensor("psb%d" % i, [128, 512], F32)) for i in range(8)]
            self.psn = ["ps%d" % i for i in range(8)]
            self.setup_consts()
            for l in range(self.nlayers):
                self.layer(l)
                if self.stop_after is not None and l == self.stop_after[0]:
                    break
            self.finish()
            S.barrier()
            S.emit()
        return nc

    def newphase(self):
        self.S.barrier()
        self.A.reset()

    def setup_consts(self):
        S, A = self.S, self.A
        self.K = {}
        for name, dty, _ in CONST_SPECS:
            t = A.alloc([128], dty)
            DMA(S, "sp", t, self.C[name][:, :], [], ["c_" + name])
            self.K[name] = t
        self.modT = [A.alloc([96], F32), A.alloc([96], F32)]
        self.ngT = A.alloc([64], F32)
        self.colA1 = [A.alloc([16], F32) for _ in range(2)]
        self.colA2 = [A.alloc([16], F32) for _ in range(2)]
        A.keep()
        X = self.scr["X"]
        DMA(S, "sp", X[0:self.SEQ, :], self.x_in[:, :], [], [])
        DMA(S, "sp", X[self.SEQ:self.T, :], self.ctx_in[:, :], [], [])
        S.barrier()

    def finish(self):
        S = self.S
        S.barrier()
        DMA(S, "sp", self.out[:, :], self.scr["X"][0:self.SEQ, :], [], [])

    def layer(self, l):
        self.need_ctx = l < DEPTH - 1
        self.lam_init = 0.8 - 0.6 * math.exp(-0.3 * l)
        sa = self.stop_after[1] if (self.stop_after is not None and self.stop_after[0] == l) else None
        if sa == "setup":
            return
        self.phase_mod(l)
        if sa == "mod":
            return
        self.phase_inproj(l)
        if sa == "inproj":
            return
        self.phase_diff(l)
        if sa == "diff":
            return
        self.phase_win(l)
        if sa == "win":
            return
        self.phase_ssd(l)
        if sa == "ssd":
            return
        self.phase_merge(l)
        if sa == "merge":
            return
        self.phase_ffn(l)

    def phase_mod(self, l):
        S, A, ps, psn, K = self.S, self.A, self.ps, self.psn, self.K
        self.newphase()
        W = self.W
        identf = K["ident_f"]
        cc = A.alloc([D], F32)
        DMA(S, "sp", cc[0:2, :], self.cvec[:, :], [], ["cc"])
        ACT(S, cc[0:2, :], cc[0:2, :], AF.Silu, ["cc"], ["cc"])
        TR(S, [(ps[0][:, 2 * k:2 * k + 2], cc[0:2, k * 128:(k + 1) * 128]) for k in range(16)],
           identf[0:2, 0:2], ["cc", "c_ident_f"], [psn[0]])
        import os
        stage = int(os.environ.get("MODSTAGE", "99"))
        if stage < 1:
            return
        condT = A.alloc([32], F32)
        CP(S, "dve", condT, ps[0][:, 0:32], [psn[0]], ["condT"])
        if stage < 2:
            return
        modrow = A.alloc([6 * D], F32)
        bb = A.alloc([6 * D], F32)
        DMA(S, "sp", bb[0:2, :], W["b_ada"][l][0:1, :].broadcast_to([2, 6 * D]), [], ["bb"])
        wA = [A.alloc([16, 512], F32) for _ in range(2)]
        nchunk = 6 * D // 512

        def load(j):
            DMA(S, "sp", wA[j % 2], W["w_ada"][l][:, j * 512:(j + 1) * 512].rearrange("(k p) c -> p k c", p=128),
                [], ["wA%d" % (j % 2)])
        load(0)
        for j in range(nchunk):
            if j + 1 < nchunk:
                load(j + 1)
            b = 1 + j % 2
            MM(S, [(ps[b][0:2, :], condT[:, 2 * k:2 * k + 2], wA[j % 2][:, k, :], k == 0, k == 15) for k in range(16)],
               ["condT", "wA%d" % (j % 2)], [psn[b]])
            TT(S, "dve", modrow[0:2, j * 512:(j + 1) * 512], ps[b][0:2, :], bb[0:2, j * 512:(j + 1) * 512], ALU.add,
               [psn[b], "bb"], ["modrow"])
        DMA(S, "sp", self.scr["MOD"][:, :], modrow[0:2, :], ["modrow"], [])
        if stage < 3:
            return
        self.newphase()
        MOD = self.scr["MOD"]
        for r in range(2):
            m96 = A.alloc([128], F32)
            DMA(S, "sp", m96[0:96, :], MOD[r, :].rearrange("(c p) -> c p", p=128), [], ["m96_%d" % r])
            TR(S, [(ps[r][:, 0:96], m96[0:96, :])], identf[0:96, 0:96], ["m96_%d" % r, "c_ident_f"], [psn[r]])
            CP(S, "dve", self.modT[r], ps[r][:, 0:96], [psn[r]], ["modT%d" % r])
        n64 = A.alloc([128], F32)
        DMA(S, "sp", n64[0:64, :], W["norm_g"][l].rearrange("i (k p) -> (i k) p", p=128), [], ["n64"])
        TR(S, [(ps[2][:, 0:64], n64[0:64, :])], identf[0:64, 0:64], ["n64", "c_ident_f"], [psn[2]])
        CP(S, "dve", self.ngT, ps[2][:, 0:64], [psn[2]], ["ngT"])
        for r in range(2):
            STT(S, self.colA1[r], self.modT[r][:, 16:32], 1.0, self.ngT[:, 0:16], ALU.add, ALU.mult,
                ["modT%d" % r, "ngT"], ["colA1_%d" % r])
            STT(S, self.colA2[r], self.modT[r][:, 64:80], 1.0, self.ngT[:, 32:48], ALU.add, ALU.mult,
                ["modT%d" % r, "ngT"], ["colA2_%d" % r])
        self.colB1 = [self.modT[r][:, 0:16] for r in range(2)]
        self.colB2 = [self.modT[r][:, 48:64] for r in range(2)]

    def norm_tiles(self, jobs, colA, colB, hT, banks):
        S, A, ps, psn, K = self.S, self.A, self.ps, self.psn, self.K
        X = self.scr["X"]
        xt = [A.alloc([D], F32) for _ in range(2)]
        xn = [A.alloc([D], BF16) for _ in range(2)]
        junk = A.alloc([D], BF16)
        ss = [A.alloc([1], F32) for _ in range(2)]
        rs = [A.alloc([1], F32) for _ in range(2)]
        names = []
        for i, (g, lo, n, dst) in enumerate(jobs):
            s = i % 2
            r = 0 if g < self.NL else 1
            DMA(S, "sp", xt[s], X[g * 128:(g + 1) * 128, :], [], ["xt%d" % s])
            ACT(S, junk, xt[s], AF.Square, ["xt%d" % s], ["junk", "ss%d" % s], accum=ss[s])
            ACT(S, rs[s], ss[s], AF.Sqrt, ["ss%d" % s], ["rs%d" % s], scale=1.0 / D, bias=EPS)
            RECIP(S, rs[s], rs[s], ["rs%d" % s], ["rs%d" % s])
            TS(S, "dve", xn[s], xt[s], rs[s], None, ALU.mult, None, ["xt%d" % s, "rs%d" % s], ["xn%d" % s])
            bp = banks[i % len(banks)]
            for half in range(2):
                b = bp[half]
                pb = ps[b][:].bitcast(BF16)
                TR(S, [(pb[:, kk * 128:(kk + 1) * 128], xn[s][:, (half * 8 + kk) * 128:(half * 8 + kk + 1) * 128])
                       for kk in range(8)], K["ident_bf"], ["xn%d" % s, "c_ident_bf"], [psn[b]])
            nm = "hT_%d" % i
            for half in range(2):
                b = bp[half]
                pb = ps[b][:].bitcast(BF16)
                for kk in range(8):
                    k = half * 8 + kk
                    src = pb[:, kk * 128 + lo:kk * 128 + lo + n]
                    if half == 0:
                        ACT(S, hT[:, k, dst:dst + n], src, AF.Identity, [psn[b]], [nm + "_%d" % k],
                            scale=colA[r][:, k:k + 1], bias=colB[r][:, k:k + 1])
                    else:
                        TS(S, "dve", hT[:, k, dst:dst + n], src, colA[r][:, k:k + 1], colB[r][:, k:k + 1],
                           ALU.mult, ALU.add, [psn[b]], [nm + "_%d" % k])
            names.append([nm + "_%d" % k for k in range(16)])
        return names

    def phase_inproj(self, l):
        S, A, ps, psn, K = self.S, self.A, self.ps, self.psn, self.K
        W = self.W
        scr = self.scr
        NT = self.NT
        nblk = (NT + 16) // 17
        per = (NT + nblk - 1) // nblk
        blocks = [list(range(b * per, min(NT, (b + 1) * per))) for b in range(nblk)]
        for tiles in blocks:
            self.newphase()
            ntb = len(tiles)
            ntok = ntb * 128
            tok0 = tiles[0] * 128
            hT = A.alloc([16, ntok], BF16)
            hnames = self.norm_tiles([(g, 0, 128, i * 128) for i, g in enumerate(tiles)],
                                     self.colA1, self.colB1, hT, [(0, 1), (2, 3), (4, 5), (6, 7)])
            self.newphase_keep(hT)
            cs = A.alloc([ntok], F32)
            sn = A.alloc([ntok], F32)
            DMA(S, "sp", cs, self.C["cosT"][:, tok0:tok0 + ntok], [], ["cs"])
            DMA(S, "sp", sn, self.C["sinT"][:, tok0:tok0 + ntok], [], ["sn"])
            dtb = A.alloc([32], F32)
            DMA(S, "sp", dtb, W["ssd_dt_bias"][l][0:1, :].broadcast_to([128, 32]), [], ["dtb"])
            wt = [A.alloc([16, 512], BF16) for _ in range(2)]
            qsb = [A.alloc([512], BF16) for _ in range(2)]
            t1 = [A.alloc([512], F32) for _ in range(2)]
            t2 = [A.alloc([512], F32) for _ in range(2)]
            obf = [A.alloc([ntok], BF16) for _ in range(2)]
            obt = [A.alloc([512], BF16) for _ in range(3)]
            dts = [A.alloc([32], F32) for _ in range(2)]
            cblocks = []
            import os
            gsel = [int(v) for v in os.environ.get("INPROJ_GROUPS", "0,1,2,3,4,5,6,7,8,9").split(",")]
            for gi in gsel:
                for c0 in range(0, IN_SIZES[gi], 512):
                    cblocks.append((gi, c0, min(512, IN_SIZES[gi] - c0)))
            chunks = [(c, min(512, ntok - c)) for c in range(0, ntok, 512)]
            st = {"bank": 0, "fm": 0, "tm": 0, "rp": 0}

            def loadw(i):
                gi, c0, nco = cblocks[i]
                a = IN_OFF[gi] + c0
                DMA(S, "pool", wt[i % 2][:, :, 0:nco],
                    W["w_in"][l][:, a:a + nco].rearrange("(k p) c -> p k c", p=128), [], ["wt%d" % (i % 2)])

            def hres(c, n):
                out = []
                for ti in range(c // 128, (c + n) // 128):
                    out += hnames[ti]
                return out

            loadw(0)
            for i, (gi, c0, nco) in enumerate(cblocks):
                if i + 1 < len(cblocks):
                    loadw(i + 1)
                ws = i % 2
                wn = "wt%d" % ws
                fm = gi in (0, 1, 4, 6, 7, 9)
                if fm:
                    for m in range(nco // 128):
                        fs = st["fm"] % 2
                        st["fm"] += 1
                        fb = (c0 + m * 128) // 128
                        chunk_names = []
                        for ci, (c, n) in enumerate(chunks):
                            b = st["bank"] % 4
                            st["bank"] += 1
                            MM(S, [(ps[b][:, 0:n], wt[ws][:, k, m * 128:(m + 1) * 128], hT[:, k, c:c + n], k == 0, k == 15)
                                   for k in range(16)], [wn], [psn[b]])
                            on = "obf%d_%d" % (fs, ci)
                            chunk_names.append(on)
                            if gi in (0, 1, 6, 7):
                                rs_ = st["rp"] % 2
                                st["rp"] += 1
                                pb = 4 + rs_
                                CP(S, "act", qsb[rs_][:, 0:n], ps[b][:, 0:n], [psn[b]], ["qsb%d" % rs_])
                                MM(S, [(ps[pb][:, 0:n], K["perm_bf"], qsb[rs_][:, 0:n], True, True)],
                                   ["qsb%d" % rs_], [psn[pb]])
                                TT(S, "dve", t1[rs_][:, 0:n], ps[b][:, 0:n], cs[:, c:c + n], ALU.mult,
                                   [psn[b], "cs"], ["t1_%d" % rs_])
                                TT(S, "dve", t2[rs_][:, 0:n], ps[pb][:, 0:n], sn[:, c:c + n], ALU.mult,
                                   [psn[pb], "sn"], ["t2_%d" % rs_])
                                TT(S, "pool", obf[fs][:, c:c + n], t1[rs_][:, 0:n], t2[rs_][:, 0:n], ALU.add,
                                   ["t1_%d" % rs_, "t2_%d" % rs_], [on])
                            elif gi == 4:
                                CP(S, "act", obf[fs][:, c:c + n], ps[b][:, 0:n], [psn[b]], [on])
                            else:
                                ACT(S, obf[fs][:, c:c + n], ps[b][:, 0:n], AF.Sigmoid, [psn[b]], [on])
                        if gi == 0:
                            dst = scr["QDT"][fb][:, tok0:tok0 + ntok]
                        elif gi == 1:
                            dst = scr["KDT"][fb][:, tok0:tok0 + ntok]
                        elif gi == 4:
                            dst = scr["XBCT"][fb][:, tok0:tok0 + ntok]
                        elif gi == 6:
                            dst = scr["WQT"].rearrange("(a two) d t -> a (two d) t", two=2)[fb][:, tok0:tok0 + ntok]
                        elif gi == 7:
                            dst = scr["WKT"].rearrange("(a two) d t -> a (two d) t", two=2)[fb][:, tok0:tok0 + ntok]
                        else:
                            dst = scr["GT"][fb][:, tok0:tok0 + ntok]
                        DMA(S, "sp", dst, obf[fs], chunk_names, [])
                else:
                    for ti, g in enumerate(tiles):
                        b = st["bank"] % 4
                        st["bank"] += 1
                        MM(S, [(ps[b][:, 0:nco], hT[:, k, ti * 128:(ti + 1) * 128], wt[ws][:, k, 0:nco], k == 0, k == 15)
                               for k in range(16)], [wn], [psn[b]])
                        rows = slice(g * 128, (g + 1) * 128)
                        if gi == 5:
                            d_ = st["tm"] % 2
                            st["tm"] += 1
                            TT(S, "dve", dts[d_], ps[b][:, 0:32], dtb, ALU.add, [psn[b], "dtb"], ["dts%d" % d_])
                            ACT(S, dts[d_], dts[d_], AF.Exp, ["dts%d" % d_], ["dts%d" % d_])
                            ACT(S, dts[d_], dts[d_], AF.Ln, ["dts%d" % d_], ["dts%d" % d_], bias=1.0)
                            DMA(S, "sp", scr["DT"][rows, :], dts[d_], ["dts%d" % d_], [])
                            continue
                        o_ = st["tm"] % 3
                        st["tm"] += 1
                        on = "obt%d" % o_
                        if gi == 3:
                            ACT(S, obt[o_][:, 0:nco], ps[b][:, 0:nco], AF.Silu, [psn[b]], [on])
                            dst = scr["ZS"][rows, c0:c0 + nco]
                        elif gi == 2:
                            CP(S, "act", obt[o_][:, 0:nco], ps[b][:, 0:nco], [psn[b]], [on])
                            dst = scr["VD"][rows, c0:c0 + nco]
                        else:
                            CP(S, "act", obt[o_][:, 0:nco], ps[b][:, 0:nco], [psn[b]], [on])
                            dst = scr["WV"][rows, c0:c0 + nco]
                        DMA(S, "sp", dst, obt[o_][:, 0:nco], [on], [])

    def newphase_keep(self, *_):
        self.S.barrier()

    def phase_diff(self, l):
        S, A, ps, psn, K = self.S, self.A, self.ps, self.psn, self.K
        W, scr = self.W, self.scr
        T, NT, NL, SEQ = self.T, self.NT, self.NL, self.SEQ
        self.newphase()
        dl = A.alloc([256], F32)
        DMA(S, "sp", dl, W["diff_lambda"][l][0:1, :].broadcast_to([128, 256]), [], ["dl"])
        jk = A.alloc([128], F32)
        s12 = A.alloc([2], F32)
        TT(S, "dve", jk[:, 0:64], dl[:, 0:64], dl[:, 64:128], ALU.mult, ["dl"], ["jk"])
        ACT(S, jk[:, 0:64], jk[:, 0:64], AF.Identity, ["jk"], ["jk", "s12a"], accum=s12[:, 0:1])
        TT(S, "dve", jk[:, 64:128], dl[:, 128:192], dl[:, 192:256], ALU.mult, ["dl"], ["jk2"])
        ACT(S, jk[:, 64:128], jk[:, 64:128], AF.Identity, ["jk2"], ["jk2", "s12b"], accum=s12[:, 1:2])
        ACT(S, s12, s12, AF.Exp, ["s12a", "s12b"], ["s12"])
        nlam = A.alloc([1], F32)
        TT(S, "dve", nlam, s12[:, 0:1], s12[:, 1:2], ALU.subtract, ["s12"], ["nlam"])
        TS(S, "dve", nlam, nlam, float(self.lam_init), -1.0, ALU.add, ALU.mult, ["nlam"], ["nlam"])
        dn = A.alloc([1], F32)
        DMA(S, "sp", dn, W["diff_norm"][l].rearrange("o p -> p o"), [], ["dn"])
        TS(S, "dve", dn, dn, float(1.0 - self.lam_init), None, ALU.mult, None, ["dn"], ["dn"])
        KT = [A.alloc([T], BF16) for _ in range(2)]
        V = [A.alloc([NT, 129], BF16) for _ in range(2)]
        for s in range(2):
            MSET(S, "pool", V[s][:, :, 128:129], 1.0, [], ["V%d" % s])
        QT = [A.alloc([512], BF16) for _ in range(2)]
        PT = [[A.alloc([512], BF16) for _ in range(2)] for _ in range(2)]
        rr = A.alloc([2], F32)
        o = A.alloc([128], F32)
        sq = A.alloc([128], BF16)
        ssq = A.alloc([1], F32)
        on = A.alloc([512], BF16)
        ob = [A.alloc([512], BF16) for _ in range(2)]

        def acc(j, qi):
            a = j * 4 + qi
            return ps[4 + a // 3][:, (a % 3) * 129:(a % 3) * 129 + 129], psn[4 + a // 3]

        qc = 0
        for h in range(8):
            hs = h % 2
            DMA(S, "sp", KT[hs], scr["KDT"][h][:, :], [], ["KT%d" % hs])
            DMA(S, "sp", V[hs][:, :, 0:128], scr["VD"][:, h * 128:(h + 1) * 128].rearrange("(t p) c -> p t c", p=128),
                [], ["V%d" % hs])
            qchunks = [(q0, min(512, SEQ - q0), list(range(NT))) for q0 in range(0, SEQ, 512)]
            if self.need_ctx:
                qchunks.append((SEQ, CTX, [NL, NL + 1]))
            for (q0, nq, kts) in qchunks:
                qs = qc % 2
                qc += 1
                nqt = nq // 128
                DMA(S, "sp", QT[qs][:, 0:nq], scr["QDT"][h][:, q0:q0 + nq], [], ["QT%d" % qs])
                for b in (4, 5, 6):
                    MSET(S, "dve", ps[b][:, :], 0.0, [], [psn[b]])
                def emit_s(ki):
                    kt = kts[ki]
                    sb = ki % 2
                    for j in range(2):
                        b = sb * 2 + j
                        MM(S, [(ps[b][:, 0:nq], KT[hs][j * 64:(j + 1) * 64, kt * 128:(kt + 1) * 128],
                                QT[qs][j * 64:(j + 1) * 64, 0:nq], True, True)],
                           ["KT%d" % hs, "QT%d" % qs], [psn[b]])
                        ACT(S, PT[sb][j][:, 0:nq], ps[b][:, 0:nq], AF.Exp, [psn[b]], ["PT%d%d" % (sb, j)], scale=0.125)

                def emit_pv(ki):
                    kt = kts[ki]
                    sb = ki % 2
                    for j in range(2):
                        items = []
                        wb = set()
                        for qi in range(nqt):
                            ap_, bn = acc(j, qi)
                            wb.add(bn)
                            items.append((ap_, PT[sb][j][:, qi * 128:(qi + 1) * 128], V[hs][:, kt, :], False, False))
                        MM(S, items, ["PT%d%d" % (sb, j), "V%d" % hs], sorted(wb))

                emit_s(0)
                for ki in range(len(kts)):
                    if ki + 1 < len(kts):
                        emit_s(ki + 1)
                    emit_pv(ki)
                for qi in range(nqt):
                    a0, b0 = acc(0, qi)
                    a1, b1 = acc(1, qi)
                    RECIP(S, rr[:, 0:1], a0[:, 128:129], [b0], ["rr0"])
                    RECIP(S, rr[:, 1:2], a1[:, 128:129], [b1], ["rr1"])
                    TT(S, "dve", rr[:, 1:2], rr[:, 1:2], nlam, ALU.mult, ["rr1", "nlam"], ["rr1"])
                    TS(S, "dve", o, a0[:, 0:128], rr[:, 0:1], None, ALU.mult, None, [b0, "rr0"], ["o"])
                    STT(S, o, a1[:, 0:128], rr[:, 1:2], o, ALU.mult, ALU.add, [b1, "rr1", "o"], ["o"])
                    ACT(S, sq, o, AF.Square, ["o"], ["sq", "ssq"], accum=ssq)
                    ACT(S, ssq, ssq, AF.Sqrt, ["ssq"], ["ssq"], scale=1.0 / 128, bias=EPS)
                    RECIP(S, ssq, ssq, ["ssq"], ["ssq"])
                    TS(S, "dve", on[:, qi * 128:(qi + 1) * 128], o, ssq, None, ALU.mult, None, ["o", "ssq"], ["on%d" % qi])
                pb = ps[7][:].bitcast(BF16)
                TR(S, [(pb[:, qi * 128:(qi + 1) * 128], on[:, qi * 128:(qi + 1) * 128]) for qi in range(nqt)],
                   K["ident_bf"], ["on%d" % qi for qi in range(nqt)], [psn[7]])
                ACT(S, ob[qs][:, 0:nq], pb[:, 0:nq], AF.Identity, [psn[7], "dn"], ["ob%d" % qs], scale=dn[:, 0:1])
                DMA(S, "sp", scr["YDT"][h][:, q0:q0 + nq], ob[qs][:, 0:nq], ["ob%d" % qs], [])

    def phase_win(self, l):
        S, A, ps, psn, K = self.S, self.A, self.ps, self.psn, self.K
        W, scr = self.W, self.scr
        T, NT, NL, SEQ = self.T, self.NT, self.NL, self.SEQ
        self.newphase()
        KT = A.alloc([4, T], BF16)
        DMA(S, "sp", KT[0:64], scr["WKT"].rearrange("k d t -> d k t"), [], ["KT"])
        V = A.alloc([NT, 4, 65], BF16)
        MSET(S, "pool", V[:, :, :, 64:65], 1.0, [], ["V"])
        for kvh in range(4):
            DMA(S, "sp", V[:, :, kvh, 0:64], scr["WV"][:, kvh * 64:(kvh + 1) * 64].rearrange("(t p) d -> p t d", p=128),
                [], ["V"])
        es = A.alloc([16], F32)
        DMA(S, "sp", es, W["win_sink"][l][0:1, :].broadcast_to([128, 16]), [], ["es"])
        ACT(S, es, es, AF.Exp, ["es"], ["es"])
        QTb = [A.alloc([16, 128], BF16) for _ in range(2)]
        PTw = [A.alloc([512], BF16) for _ in range(3)]
        tmp = [A.alloc([512], BF16) for _ in range(2)]
        dd = A.alloc([4], F32)
        yw = [A.alloc([1024], BF16) for _ in range(2)]
        ob = [A.alloc([8, 128], BF16) for _ in range(2)]
        qbs = list(range(NL)) + ([NL, NL + 1] if self.need_ctx else [])
        cnt = {"s": 0, "p": 0, "t": 0}
        for qi_, qb in enumerate(qbs):
            s = qi_ % 2
            if qb < NL:
                keys = ([(qb - 1, "prev")] if qb > 0 else []) + [(qb, "c")] + \
                       ([(qb + 1, "next")] if qb < NL - 1 else []) + [(NL, "c"), (NL + 1, "c")]
            else:
                keys = [(NL, "c"), (NL + 1, "c")]
            DMA(S, "sp", QTb[s][0:64], scr["WQT"][:, :, qb * 128:(qb + 1) * 128].rearrange("h d t -> d h t"),
                [], ["QTb%d" % s])
            for kvh in range(4):
                ab = 4 + kvh % 2
                MSET(S, "dve", ps[ab][:, 0:260], 0.0, [], [psn[ab]])
                pend = []

                def flush():
                    for (p__, kt__) in pend:
                        MM(S, [(ps[ab][:, g * 65:(g + 1) * 65], PTw[p__][:, g * 128:(g + 1) * 128], V[:, kt__, kvh, :], False, False)
                               for g in range(4)], ["PTw%d" % p__, "V"], [psn[ab]])
                    del pend[:]

                for (kt, kind) in keys:
                    sb = cnt["s"] % 4
                    cnt["s"] += 1
                    p_ = cnt["p"] % 3
                    cnt["p"] += 1
                    MM(S, [(ps[sb][:, :], KT[0:64, kvh, kt * 128:(kt + 1) * 128],
                            QTb[s][0:64, kvh * 4:(kvh + 1) * 4, :], True, True)], ["KT", "QTb%d" % s], [psn[sb]])
                    if kind == "c":
                        ACT(S, PTw[p_], ps[sb][:, :], AF.Exp, [psn[sb]], ["PTw%d" % p_], scale=0.125)
                    else:
                        t_ = cnt["t"] % 2
                        cnt["t"] += 1
                        ACT(S, tmp[t_], ps[sb][:, :], AF.Exp, [psn[sb]], ["tmp%d" % t_], scale=0.125)
                        mk = K["wm_prev"] if kind == "prev" else K["wm_next"]
                        TT(S, "pool", PTw[p_].rearrange("p (g q) -> p g q", g=4),
                           tmp[t_].rearrange("p (g q) -> p g q", g=4),
                           mk.unsqueeze(1).broadcast_to([128, 4, 128]), ALU.mult, ["tmp%d" % t_], ["PTw%d" % p_])
                    flush()
                    pend.append((p_, kt))
                flush()
                av = ps[ab][:, 0:260].rearrange("p (g e) -> p g e", g=4)
                TT(S, "dve", dd, av[:, :, 64], es[:, kvh * 4:(kvh + 1) * 4], ALU.add, [psn[ab], "es"], ["dd"])
                RECIP(S, dd, dd, ["dd"], ["dd"])
                TT(S, "dve", yw[s][:, kvh * 256:(kvh + 1) * 256].rearrange("p (g e) -> p g e", g=4), av[:, :, 0:64],
                   dd.unsqueeze(2).broadcast_to([128, 4, 64]), ALU.mult, [psn[ab], "dd"], ["yw%d_%d" % (s, kvh)])
            pb = ps[7][:].bitcast(BF16)
            TR(S, [(pb[:, k * 128:(k + 1) * 128], yw[s][:, k * 128:(k + 1) * 128]) for k in range(8)], K["ident_bf"],
               ["yw%d_%d" % (s, kvh) for kvh in range(4)], [psn[7]])
            CP(S, "act", ob[s].rearrange("p k t -> p (k t)"), pb[:, 0:1024], [psn[7]], ["ob%d" % s])
            DMA(S, "sp", scr["YWT"][:, :, qb * 128:(qb + 1) * 128].rearrange("k p t -> p k t"), ob[s], ["ob%d" % s], [])

    def phase_ssd(self, l):
        S, A, ps, psn, K = self.S, self.A, self.ps, self.psn, self.K
        W, scr = self.W, self.scr
        T, NT, NL, SEQ = self.T, self.NT, self.NL, self.SEQ
        identf = K["ident_f"]
        self.newphase()
        cw6 = A.alloc([2048], F32)
        DMA(S, "sp", cw6[0:5, :], W["ssd_conv_w"][l][:, :], [], ["cw6"])
        DMA(S, "sp", cw6[5:6, :], W["ssd_conv_b"][l][:, :], [], ["cw6"])
        TR(S, [(ps[0][:, cb * 6:(cb + 1) * 6], cw6[0:6, cb * 128:(cb + 1) * 128]) for cb in range(16)],
           identf[0:6, 0:6], ["cw6"], [psn[0]])
        cw = A.alloc([16, 6], F32)
        CP(S, "dve", cw.rearrange("p a b -> p (a b)"), ps[0][:, 0:96], [psn[0]], ["cw"])
        xin = [A.alloc([T], BF16) for _ in range(2)]
        acc = [A.alloc([T], F32) for _ in range(2)]
        u = [A.alloc([T], BF16) for _ in range(2)]
        stg = [A.alloc([8, 128], BF16) for _ in range(2)]
        tcnt = 0
        for cb in range(16):
            s = cb % 2
            DMA(S, "sp", xin[s], scr["XBCT"][cb][:, :], [], ["xin%d" % s])
            ACT(S, acc[s], xin[s], AF.Identity, ["xin%d" % s, "cw"], ["acc%d" % s], scale=cw[:, cb, 2:3], bias=cw[:, cb, 5:6])
            for j in (0, 1, 3, 4):
                d = j - 2
                for (a, b) in ((0, SEQ), (SEQ, T)):
                    lo, hi = (a - d, b) if d < 0 else (a, b - d)
                    STT(S, acc[s][:, lo:hi], xin[s][:, lo + d:hi + d], cw[:, cb, j:j + 1], acc[s][:, lo:hi],
                        ALU.mult, ALU.add, ["xin%d" % s, "cw", "acc%d" % s], ["acc%d" % s])
            ACT(S, u[s], acc[s], AF.Silu, ["acc%d" % s], ["u%d" % s])
            if cb >= 8:
                DMA(S, "sp", scr["UT"][cb][:, :], u[s], ["u%d" % s], [])
            if cb < 12:
                for t0 in range(0, NT, 8):
                    nt_ = min(8, NT - t0)
                    q = tcnt % 2
                    tcnt += 1
                    pb = ps[1 + q][:].bitcast(BF16)
                    TR(S, [(pb[:, i * 128:(i + 1) * 128], u[s][:, (t0 + i) * 128:(t0 + i + 1) * 128]) for i in range(nt_)],
                       K["ident_bf"], ["u%d" % s], [psn[1 + q]])
                    CP(S, "dve" if q else "act", stg[q].rearrange("p a b -> p (a b)")[:, 0:nt_ * 128], pb[:, 0:nt_ * 128],
                       [psn[1 + q]], ["stg%d" % q])
                    DMA(S, "sp", scr["XSB"][t0 * 128:(t0 + nt_) * 128, cb * 128:(cb + 1) * 128].rearrange("(t p) c -> p t c", p=128),
                        stg[q][:, 0:nt_, :], ["stg%d" % q], [])
        self.newphase()
        abc = A.alloc([32], F32)
        DMA(S, "sp", abc, W["ssd_a_log"][l][0:1, :].broadcast_to([128, 32]), [], ["abc"])
        ACT(S, abc, abc, AF.Exp, ["abc"], ["abc"])
        TS(S, "dve", abc, abc, -1.0, None, ALU.mult, None, ["abc"], ["abc"])
        dsk = A.alloc([16], F32)
        DMA(S, "sp", dsk, W["ssd_d"][l][0:1, :].broadcast_to([128, 16]), [], ["dsk"])
        sn8 = A.alloc([128], F32)
        DMA(S, "sp", sn8[0:8, :], W["ssd_norm"][l].rearrange("o (k p) -> (o k) p", p=128), [], ["sn8"])
        TR(S, [(ps[0][:, 0:8], sn8[0:8, :])], identf[0:8, 0:8], ["sn8"], [psn[0]])
        snc = A.alloc([8], F32)
        CP(S, "dve", snc, ps[0][:, 0:8], [psn[0]], ["snc"])
        xsb = [A.alloc([1536], BF16) for _ in range(2)]
        dtt = [A.alloc([32], F32) for _ in range(2)]
        ad = [A.alloc([32], F32) for _ in range(2)]
        cs = [A.alloc([64], F32) for _ in range(2)]
        dec = [A.alloc([96], F32) for _ in range(2)]
        ncum = [A.alloc([32], F32) for _ in range(2)]
        xdt = [A.alloc([2, 1024], BF16) for _ in range(2)]
        xdtd = [A.alloc([2, 1024], BF16) for _ in range(2)]
        adB = [A.alloc([32, 128], F32) for _ in range(2)]
        adT = [A.alloc([32, 128], F32) for _ in range(2)]
        negones = A.alloc([128], F32)
        MSET(S, "pool", negones, -1.0, [], ["negones"])

        def v16(ap):
            return ap.rearrange("p (h e) -> p h e", e=64)

        def bc(ap, n):
            return ap.unsqueeze(2).broadcast_to([128, n, 64])

        def prep(c, s, dirs, full):
            n_ = "_%d" % s
            DMA(S, "sp", xsb[s], scr["XSB"][c * 128:(c + 1) * 128, :], [], ["xsb" + n_])
            DMA(S, "sp", dtt[s], scr["DT"][c * 128:(c + 1) * 128, :], [], ["dtt" + n_])
            TT(S, "dve", ad[s], dtt[s], abc, ALU.mult, ["dtt" + n_, "abc"], ["ad" + n_])
            MM(S, [(ps[0][:, 0:16], K["tri_f"], ad[s][:, 0:16], True, True),
                   (ps[0][:, 16:32], K["tri_b"], ad[s][:, 16:32], True, True),
                   (ps[0][:, 32:64], K["ones_f"], ad[s][:, 0:32], True, True)], ["ad" + n_], [psn[0]])
            CP(S, "dve", cs[s], ps[0][:, 0:64], [psn[0]], ["cs" + n_])
            TT(S, "dve", dec[s][:, 32:64], cs[s][:, 32:64], cs[s][:, 0:32], ALU.subtract, ["cs" + n_], ["decs" + n_])
            ACT(S, dec[s][:, 32:64], dec[s][:, 32:64], AF.Exp, ["decs" + n_], ["decs" + n_])
            ACT(S, dec[s][:, 0:32], cs[s][:, 0:32], AF.Exp, ["cs" + n_], ["deco" + n_])
            ACT(S, dec[s][:, 64:96], cs[s][:, 32:64], AF.Exp, ["cs" + n_], ["dect" + n_])
            if full:
                CP(S, "pool", adB[s], ad[s].unsqueeze(2).broadcast_to([128, 32, 128]), ["ad" + n_], ["adB" + n_])
                TT(S, "pool", adT[s][:, 0:16, :], adB[s][:, 0:16, :], K["tri_f"].unsqueeze(1).broadcast_to([128, 16, 128]),
                   ALU.mult, ["adB" + n_], ["adT" + n_])
                TT(S, "dve", adT[s][:, 16:32, :], adB[s][:, 16:32, :], K["tri_b"].unsqueeze(1).broadcast_to([128, 16, 128]),
                   ALU.mult, ["adB" + n_], ["adT" + n_])
            for d in dirs:
                e1 = "pool" if d == 0 else "dve"
                TT(S, e1, v16(xdt[s][:, d, :]), v16(xsb[s][:, 0:1024]), bc(dtt[s][:, d * 16:(d + 1) * 16], 16), ALU.mult,
                   ["xsb" + n_, "dtt" + n_], ["xdt%d" % d + n_])
                TT(S, e1, v16(xdtd[s][:, d, :]), v16(xdt[s][:, d, :]), bc(dec[s][:, 32 + d * 16:32 + (d + 1) * 16], 16), ALU.mult,
                   ["xdt%d" % d + n_, "decs" + n_], ["xdtd%d" % d + n_])

        def state_update(H, s, d, hname):
            n_ = "_%d" % s
            for g in range(4):
                b = 2 + g % 2
                MM(S, [(ps[b][:, 0:256], xsb[s][:, 1024 + g * 128:1024 + (g + 1) * 128], xdtd[s][:, d, g * 256:(g + 1) * 256], True, True)],
                   ["xsb" + n_, "xdtd%d" % d + n_], [psn[b]])
                hv = v16(H[:, g * 256:(g + 1) * 256])
                TT(S, "dve", hv, hv, bc(dec[s][:, 64 + d * 16 + g * 4:64 + d * 16 + (g + 1) * 4], 4), ALU.mult,
                   [hname, "dect" + n_], [hname])
                TT(S, "dve", H[:, g * 256:(g + 1) * 256], H[:, g * 256:(g + 1) * 256], ps[b][:, 0:256], ALU.add,
                   [hname, psn[b]], [hname])

        Hb = A.alloc([1024], F32)
        Hbb = [A.alloc([1024], BF16) for _ in range(2)]
        MSET(S, "dve", Hb, 0.0, [], ["Hb"])
        order_b = [NL + 1, NL] + list(range(NL - 1, -1, -1))
        for i, c in enumerate(order_b):
            s = i % 2
            prep(c, s, [1], False)
            CP(S, "act", Hbb[s], Hb, ["Hb"], ["Hbb%d" % s])
            DMA(S, "sp", scr["HB"][c], Hbb[s], ["Hbb%d" % s], [])
            state_update(Hb, s, 1, "Hb")
        self.newphase_keep()
        Hf = A.alloc([1024], F32)
        Hfb = A.alloc([1024], BF16)
        MSET(S, "dve", Hf, 0.0, [], ["Hf"])
        bct = [A.alloc([8, 128], BF16) for _ in range(2)]
        hb = [A.alloc([1024], BF16) for _ in range(2)]
        zs = [A.alloc([1024], BF16) for _ in range(2)]
        gt = [A.alloc([128], BF16) for _ in range(2)]
        lt = [A.alloc([512], BF16) for _ in range(2)]
        mt = [A.alloc([512], BF16) for _ in range(2)]
        yo = [A.alloc([1024], F32) for _ in range(2)]
        xd = A.alloc([1024], F32)
        yt = A.alloc([1024], F32)
        junk = A.alloc([1024], BF16)
        ssq = A.alloc([1], F32)
        yn = A.alloc([1024], BF16)
        ob = [A.alloc([8, 128], BF16) for _ in range(2)]
        order_f = [NL, NL + 1] + list(range(NL))
        kk = 0
        for i, c in enumerate(order_f):
            s = i % 2
            n_ = "_%d" % s
            want_out = (c < NL) or self.need_ctx
            prep(c, s, [0, 1] if want_out else [0], want_out)
            if want_out:
                DMA(S, "sp", bct[s], scr["UT"][8:16, :, c * 128:(c + 1) * 128].rearrange("k p t -> p k t"), [], ["bct" + n_])
                DMA(S, "sp", hb[s], scr["HB"][c], [], ["hb" + n_])
                DMA(S, "sp", zs[s], scr["ZS"][c * 128:(c + 1) * 128, :], [], ["zs" + n_])
                CP(S, "act", Hfb, Hf, ["Hf"], ["Hfb"])
                MSET(S, "dve", ps[4][:, :], 0.0, [], [psn[4]])
                MSET(S, "dve", ps[5][:, :], 0.0, [], [psn[5]])
                for g in range(4):
                    MM(S, [(ps[1][:, 0:128], bct[s][:, g, :], bct[s][:, 4 + g, :], True, True)], ["bct" + n_], [psn[1]])
                    CP(S, "act", gt[g % 2], ps[1][:, 0:128], [psn[1]], ["gt%d" % (g % 2)])
                    for d in range(2):
                        q = kk % 2
                        kk += 1
                        b = 2 + q
                        items = []
                        for hh in range(4):
                            c_ = d * 16 + g * 4 + hh
                            o_ = ps[b][:, hh * 128:(hh + 1) * 128]
                            items += [(o_, adB[s][:, c_, :], K["tri_f"] if d == 0 else K["tri_b"], True, False),
                                      (o_, adT[s][:, c_, :], negones, False, False),
                                      (o_, identf, K["neg_f"] if d == 0 else K["neg_b"], False, True)]
                        MM(S, items, ["adB" + n_, "adT" + n_, "negones"], [psn[b]])
                        ACT(S, lt[q], ps[b][:, :], AF.Exp, [psn[b]], ["lt%d" % q])
                        TT(S, "pool" if q else "dve", mt[q].rearrange("p (a b) -> p a b", a=4), lt[q].rearrange("p (a b) -> p a b", a=4),
                           gt[g % 2].unsqueeze(1).broadcast_to([128, 4, 128]), ALU.mult, ["lt%d" % q, "gt%d" % (g % 2)], ["mt%d" % q])
                        yb = 4 + (g * 4) // 8
                        MM(S, [(ps[yb][:, ((g * 4 + hh) % 8) * 64:((g * 4 + hh) % 8 + 1) * 64], mt[q][:, hh * 128:(hh + 1) * 128],
                                xdt[s][:, d, (g * 4 + hh) * 64:(g * 4 + hh + 1) * 64], False, False) for hh in range(4)],
                           ["mt%d" % q, "xdt%d" % d + n_], [psn[yb]])
                for d in range(2):
                    hsrc = Hfb if d == 0 else hb[s]
                    hn = "Hfb" if d == 0 else "hb" + n_
                    for g in range(4):
                        MM(S, [(ps[6 + g // 2][:, (g % 2) * 256:(g % 2 + 1) * 256], bct[s][:, 4 + g, :], hsrc[:, g * 256:(g + 1) * 256], True, True)],
                           ["bct" + n_, hn], [psn[6 + g // 2]])
                    for half in range(2):
                        TT(S, "dve", yo[d][:, half * 512:(half + 1) * 512].rearrange("p (h e) -> p h e", e=64),
                           ps[6 + half][:, :].rearrange("p (h e) -> p h e", e=64),
                           bc(dec[s][:, d * 16 + half * 8:d * 16 + half * 8 + 8], 8), ALU.mult,
                           [psn[6 + half], "deco" + n_], ["yo%d_%d" % (d, half)])
                TT(S, "pool", yo[0], yo[0], yo[1], ALU.add, ["yo0_0", "yo0_1", "yo1_0", "yo1_1"], ["yo0_0", "yo0_1"])
                TT(S, "pool", v16(xd), v16(xsb[s][:, 0:1024]), bc(dsk, 16), ALU.mult, ["xsb" + n_, "dsk"], ["xd"])
                TT(S, "pool", yo[0], yo[0], xd, ALU.add, ["yo0_0", "yo0_1", "xd"], ["yo0_0", "yo0_1"])
                for half in range(2):
                    TT(S, "dve", yt[:, half * 512:(half + 1) * 512], ps[4 + half][:, :], yo[0][:, half * 512:(half + 1) * 512], ALU.add,
                       [psn[4 + half], "yo0_0", "yo0_1"], ["yt%d" % half])
                TT(S, "dve", yt, yt, zs[s], ALU.mult, ["yt0", "yt1", "zs" + n_], ["yt0", "yt1"])
                ACT(S, junk, yt, AF.Square, ["yt0", "yt1"], ["junk", "ssq"], accum=ssq)
                ACT(S, ssq, ssq, AF.Sqrt, ["ssq"], ["ssq"], scale=1.0 / 1024, bias=EPS)
                RECIP(S, ssq, ssq, ["ssq"], ["ssq"])
                TS(S, "dve", yn, yt, ssq, None, ALU.mult, None, ["yt0", "yt1", "ssq"], ["yn"])
                pb = ps[1][:].bitcast(BF16)
                TR(S, [(pb[:, k * 128:(k + 1) * 128], yn[:, k * 128:(k + 1) * 128]) for k in range(8)], K["ident_bf"], ["yn"], [psn[1]])
                for k in range(8):
                    ACT(S, ob[s][:, k, :], pb[:, k * 128:(k + 1) * 128], AF.Identity, [psn[1], "snc"], ["ob" + n_], scale=snc[:, k:k + 1])
                DMA(S, "sp", scr["YST"][:, :, c * 128:(c + 1) * 128].rearrange("k p t -> p k t"), ob[s], ["ob" + n_], [])
            state_update(Hf, s, 0, "Hf")

    def load_rowG(self, l, which, ngi):
        S, A = self.S, self.A
        rowG = [A.alloc([D], F32) for _ in range(2)]
        ngb = A.alloc([D], F32)
        DMA(S, "sp", ngb, self.W["norm_g"][l][ngi:ngi + 1, :].broadcast_to([128, D]), [], ["ngb"])
        for r in range(2):
            DMA(S, "sp", rowG[r], self.scr["MOD"][r:r + 1, which * D:(which + 1) * D].broadcast_to([128, D]), [], ["rowG%d" % r])
            TT(S, "dve", rowG[r], rowG[r], ngb, ALU.mult, ["rowG%d" % r, "ngb"], ["rowG%d" % r])
        return rowG

    def resid_epilogue(self, y, yname, g, rowG, bufs, i, dst="X"):
        S = self.S
        xt, junk, ss = bufs
        s = i % 2
        r = 0 if g < self.NL else 1
        X = self.scr["X"]
        rows = slice(g * 128, (g + 1) * 128)
        DMA(S, "sp", xt[s], X[rows, :], [], ["ext%d" % s])
        ACT(S, junk, y, AF.Square, [yname], ["ejunk", "ess%d" % s], accum=ss[s])
        ACT(S, ss[s], ss[s], AF.Sqrt, ["ess%d" % s], ["ess%d" % s], scale=1.0 / D, bias=EPS)
        RECIP(S, ss[s], ss[s], ["ess%d" % s], ["ess%d" % s])
        STT(S, y, y, ss[s], rowG[r], ALU.mult, ALU.mult, [yname, "ess%d" % s, "rowG%d" % r], [yname])
        TT(S, "pool", xt[s], xt[s], y, ALU.add, ["ext%d" % s, yname], ["ext%d" % s])
        DMA(S, "sp", self.scr[dst][rows, :], xt[s], ["ext%d" % s], [])

    def phase_merge(self, l):
        S, A, ps, psn, K = self.S, self.A, self.ps, self.psn, self.K
        W, scr = self.W, self.scr
        T, NT, NL, SEQ = self.T, self.NT, self.NL, self.SEQ
        self.newphase()
        rowG = self.load_rowG(l, 2, 1)
        blocks = [list(range(t, min(NL, t + 2))) for t in range(0, NL, 2)]
        if self.need_ctx:
            blocks.append([NL, NL + 1])
        yin = [[A.alloc([8, 256], BF16) for _ in range(3)] for _ in range(2)]
        wb = [[A.alloc([8, 512], BF16) for _ in range(3)] for _ in range(2)]
        gg = [[A.alloc([4, 256], BF16) for _ in range(3)] for _ in range(2)]
        mT = A.alloc([16, 256], BF16)
        tt_ = [A.alloc([256], F32) for _ in range(3)]
        wo = [A.alloc([16, 512], BF16) for _ in range(2)]
        ysb = A.alloc([2, D], F32)
        xt = [A.alloc([D], F32) for _ in range(2)]
        junk = A.alloc([D], BF16)
        ss = [A.alloc([1], F32) for _ in range(2)]
        names = ["YDT", "YST", "YWT"]
        wn = ["w_br_diff", "w_br_ssd", "w_br_win"]
        wc = 0
        oc = 0
        ec = 0
        for bi, tiles in enumerate(blocks):
            bs = bi % 2
            tok0 = tiles[0] * 128
            ntok = len(tiles) * 128
            for br in range(3):
                DMA(S, "sp", yin[bs][br][:, :, 0:ntok], scr[names[br]][:, :, tok0:tok0 + ntok].rearrange("k p t -> p k t"),
                    [], ["yin%d%d" % (bs, br)])
            for f4 in range(4):
                ws = wc % 2
                wc += 1
                for br in range(3):
                    DMA(S, "pool", wb[ws][br], W[wn[br]][l][:, f4 * 512:(f4 + 1) * 512].rearrange("(k p) c -> p k c", p=128),
                        [], ["wb%d%d" % (ws, br)])
                    DMA(S, "sp", gg[ws][br][:, :, 0:ntok],
                        scr["GT"][br * 16 + f4 * 4:br * 16 + f4 * 4 + 4, :, tok0:tok0 + ntok].rearrange("k p t -> p k t"),
                        [], ["gg%d%d" % (ws, br)])
                for m in range(4):
                    f = f4 * 4 + m
                    for br in range(3):
                        MM(S, [(ps[br][:, 0:ntok], wb[ws][br][:, k, m * 128:(m + 1) * 128], yin[bs][br][:, k, 0:ntok], k == 0, k == 7)
                               for k in range(8)], ["wb%d%d" % (ws, br), "yin%d%d" % (bs, br)], [psn[br]])
                        TT(S, "dve", tt_[br][:, 0:ntok], ps[br][:, 0:ntok], gg[ws][br][:, m, 0:ntok], ALU.mult,
                           [psn[br], "gg%d%d" % (ws, br)], ["tt%d" % br])
                    TT(S, "pool", tt_[0][:, 0:ntok], tt_[0][:, 0:ntok], tt_[1][:, 0:ntok], ALU.add, ["tt0", "tt1"], ["tt0"])
                    TT(S, "pool", mT[:, f, 0:ntok], tt_[0][:, 0:ntok], tt_[2][:, 0:ntok], ALU.add, ["tt0", "tt2"], ["mT%d" % f])
            for nch in range(4):
                os_ = oc % 2
                oc += 1
                DMA(S, "pool", wo[os_], W["w_out"][l][:, nch * 512:(nch + 1) * 512].rearrange("(k p) c -> p k c", p=128),
                    [], ["wo%d" % os_])
                for ti in range(len(tiles)):
                    b = 3 + (ti + nch) % 4
                    MM(S, [(ps[b][:, :], mT[:, k, ti * 128:(ti + 1) * 128], wo[os_][:, k, :], k == 0, k == 15) for k in range(16)],
                       ["wo%d" % os_] + ["mT%d" % f for f in range(16)], [psn[b]])
                    CP(S, "act", ysb[:, ti, nch * 512:(nch + 1) * 512], ps[b][:, :], [psn[b]], ["ysbm%d" % ti])
            for ti, g in enumerate(tiles):
                yname = "ysbm%d" % ti
                self.resid_epilogue(ysb[:, ti, :], yname, g, rowG, (xt, junk, ss), ec)
                ec += 1

    def phase_ffn(self, l):
        S, A, ps, psn, K = self.S, self.A, self.ps, self.psn, self.K
        W, scr = self.W, self.scr
        T, NT, NL, SEQ = self.T, self.NT, self.NL, self.SEQ
        identf = K["ident_f"]
        self.newphase()
        floor0 = A.floor
        rowG = self.load_rowG(l, 5, 3)
        fcT = A.alloc([88, 4], F32)
        fc4 = A.alloc([2816], F32)
        for pc in range(4):
            DMA(S, "sp", fc4[0:3, :], W["ffn_conv_w"][l][:, pc * 2816:(pc + 1) * 2816], [], ["fc4"])
            DMA(S, "sp", fc4[3:4, :], W["ffn_conv_b"][l][:, pc * 2816:(pc + 1) * 2816], [], ["fc4"])
            TR(S, [(ps[0][:, j * 4:(j + 1) * 4], fc4[0:4, j * 128:(j + 1) * 128]) for j in range(22)], identf[0:4, 0:4],
               ["fc4"], [psn[0]])
            CP(S, "dve", fcT[:, pc * 22:(pc + 1) * 22, :].rearrange("p a b -> p (a b)"), ps[0][:, 0:88], [psn[0]], ["fcT"])
        S.barrier()
        A.keep()
        NB = 512
        blocks = [(t0, min(NB, SEQ - t0), t0 > 0, t0 + NB < SEQ) for t0 in range(0, SEQ, NB)]
        if self.need_ctx:
            blocks.append((SEQ, CTX, False, False))
        for (t0, nb, lv, rv) in blocks:
            self.newphase()
            g0 = t0 // 128
            ntl = nb // 128
            actT = A.alloc([44, nb], BF16)
            base = A.top
            h2T = A.alloc([16, nb + 256], BF16)
            jobs = [(g0 + i, 0, 128, 128 + i * 128) for i in range(ntl)]
            if lv:
                jobs.append((g0 - 1, 0, 128, 0))
            if rv:
                jobs.append((g0 + ntl, 0, 128, 128 + nb))
            top1 = A.top
            self.norm_tiles(jobs, self.colA2, self.colB2, h2T, [(0, 1), (2, 3), (4, 5), (6, 7)])
            S.barrier()
            A.top = top1
            wa = [A.alloc([16, 256], BF16) for _ in range(2)]
            wg = [A.alloc([16, 256], BF16) for _ in range(2)]
            ta = [A.alloc([256], F32) for _ in range(2)]
            tg = [A.alloc([256], F32) for _ in range(2)]
            chunks = [(c, min(256, nb - c)) for c in range(0, nb, 256)]
            cc_ = 0

            def loadup(i):
                DMA(S, "pool", wa[i % 2], W["ffn_w_up"][l][:, i * 256:(i + 1) * 256].rearrange("(k p) c -> p k c", p=128),
                    [], ["wa%d" % (i % 2)])
                DMA(S, "pool", wg[i % 2], W["ffn_w_up"][l][:, D_FF + i * 256:D_FF + (i + 1) * 256].rearrange("(k p) c -> p k c", p=128),
                    [], ["wg%d" % (i % 2)])
            loadup(0)
            for fb2 in range(22):
                if fb2 + 1 < 22:
                    loadup(fb2 + 1)
                ws = fb2 % 2
                for m in range(2):
                    fb = fb2 * 2 + m
                    for (c, n) in chunks:
                        q = cc_ % 2
                        cc_ += 1
                        first, last = (c == 0), (c + n == nb)
                        lo = 1 if (first and not lv) else 0
                        hi = n - 1 if (last and not rv) else n
                        for (wt_, wname, bnk, tx, txn, fidx) in ((wa[ws], "wa%d" % ws, q * 2, ta[q], "ta%d" % q, fb),
                                                               (wg[ws], "wg%d" % ws, q * 2 + 1, tg[q], "tg%d" % q, 44 + fb)):
                            MM(S, [(ps[bnk][:, 0:n + 2], wt_[:, k, m * 128:(m + 1) * 128], h2T[:, k, 127 + c:127 + c + n + 2], k == 0, k == 15)
                                   for k in range(16)], [wname], [psn[bnk]])
                            ACT(S, tx[:, 0:n], ps[bnk][:, 1:n + 1], AF.Identity, [psn[bnk]], [txn],
                                scale=fcT[:, fidx, 1:2], bias=fcT[:, fidx, 3:4])
                            STT(S, tx[:, lo:n], ps[bnk][:, lo:n], fcT[:, fidx, 0:1], tx[:, lo:n], ALU.mult, ALU.add,
                                [psn[bnk], txn], [txn])
                            STT(S, tx[:, 0:hi], ps[bnk][:, 2:2 + hi], fcT[:, fidx, 2:3], tx[:, 0:hi], ALU.mult, ALU.add,
                                [psn[bnk], txn], [txn])
                        ACT(S, ta[q][:, 0:n], ta[q][:, 0:n], AF.Silu, ["ta%d" % q], ["ta%d" % q])
                        TT(S, "pool", actT[:, fb, c:c + n], ta[q][:, 0:n], tg[q][:, 0:n], ALU.mult, ["ta%d" % q, "tg%d" % q], ["actT"])
            S.barrier()
            A.top = base
            yT = A.alloc([16, nb], F32)
            top3 = A.top
            wd = [A.alloc([44, 256], BF16) for _ in range(2)]

            def loaddn(i):
                DMA(S, "pool", wd[i % 2], W["ffn_w_down"][l][:, i * 256:(i + 1) * 256].rearrange("(k p) c -> p k c", p=128),
                    [], ["wd%d" % (i % 2)])
            loaddn(0)
            bc_ = 0
            for fo2 in range(8):
                if fo2 + 1 < 8:
                    loaddn(fo2 + 1)
                ws = fo2 % 2
                for m in range(2):
                    fo = fo2 * 2 + m
                    b = 4 + bc_ % 4
                    bc_ += 1
                    MM(S, [(ps[b][:, 0:nb], wd[ws][:, k, m * 128:(m + 1) * 128], actT[:, k, 0:nb], k == 0, k == 43) for k in range(44)],
                       ["wd%d" % ws], [psn[b]])
                    CP(S, "act" if fo % 2 else "dve", yT[:, fo, :], ps[b][:, 0:nb], [psn[b]], ["yT%d" % fo])
            S.barrier()
            A.top = top3
            ysb = [A.alloc([D], F32) for _ in range(2)]
            xt = [A.alloc([D], F32) for _ in range(2)]
            junk = A.alloc([D], BF16)
            ss = [A.alloc([1], F32) for _ in range(2)]
            for ti in range(ntl):
                s = ti % 2
                for q4 in range(4):
                    TR(S, [(ps[q4][:, j * 128:(j + 1) * 128], yT[:, q4 * 4 + j, ti * 128:(ti + 1) * 128]) for j in range(4)], identf,
                       [], [psn[q4]])
                    CP(S, "act" if q4 % 2 else "dve", ysb[s][:, q4 * 512:(q4 + 1) * 512], ps[q4][:, :], [psn[q4]], ["fysb%d" % s])
                self.resid_epilogue(ysb[s], "fysb%d" % s, g0 + ti, rowG, (xt, junk, ss), ti, dst="X2")
        S.barrier()
        nrow = T if self.need_ctx else SEQ
        DMA(S, "sp", scr["X"][0:nrow, :], scr["X2"][0:nrow, :], [], [])
        S.barrier()
        A.floor = floor0


_CACHE = {}
SEQ_FULL = 4096
N_CORES = 4


def kernel(**inputs):
    SEQ = SEQ_FULL
    if "nc" not in _CACHE:
        _CACHE["nc"] = Builder(SEQ).build()
        _CACHE["consts"] = host_consts(SEQ)
    nc = _CACHE["nc"]
    consts = _CACHE["consts"]
    f32 = np.float32
    shared = {}
    for name, shp in W_SPECS:
        shared[name] = np.ascontiguousarray(np.asarray(inputs[name], dtype=f32).reshape([DEPTH] + shp))
    shared.update(consts)
    in_maps = []
    for b in range(N_CORES):
        m = dict(shared)
        m["x"] = np.ascontiguousarray(np.asarray(inputs["x"][b], dtype=f32))
        m["ctx"] = np.ascontiguousarray(np.asarray(inputs["ctx"][b], dtype=f32))
        m["cvec"] = np.ascontiguousarray(np.stack([np.asarray(inputs["c"][b], dtype=f32),
                                                   np.asarray(inputs["c_ctx"], dtype=f32)]))
        in_maps.append(m)
    res = run_bass_kernel_spmd(nc, in_maps, core_ids=list(range(N_CORES)))
    out = np.stack([np.asarray(res.results[b]["out"]) for b in range(N_CORES)], axis=0)
    return out.astype(f32)
```

```python
import math
import numpy as np
import ml_dtypes
import concourse.bass as bass
import concourse.mybir as mybir
from concourse.bass_utils import run_bass_kernel_spmd

F32 = mybir.dt.float32
BF16 = mybir.dt.bfloat16
AF = mybir.ActivationFunctionType
ALU = mybir.AluOpType
AX = mybir.AxisListType

D = 2048
DEPTH = 2
CTX = 256
GRID_W = 64
EPS = 1e-6
IN_SIZES = (1024, 1024, 1024, 1024, 2048, 32, 1024, 256, 256, 6144)
IN_OFF = [0]
for _s in IN_SIZES:
    IN_OFF.append(IN_OFF[-1] + _s)
IN_WIDTH = IN_OFF[-1]
D_FF = 5632
NDMA_SEM = 12


class _Op:
    __slots__ = ("eng", "fn", "deps", "sig", "sigval", "dma", "dsem", "dval", "dprev")


class Sched:
    ENGS = ("pe", "act", "dve", "pool", "sp")

    def __init__(self, nc):
        self.nc = nc
        self.ops = []
        self.last_w = {}
        self.readers = {}
        self.last_on = {e: None for e in self.ENGS}
        self.open_dma = {}
        self.dcnt = {e: 0 for e in self.ENGS}

    def op(self, eng, fn, reads=(), writes=(), dma=False):
        o = _Op()
        o.eng, o.fn, o.dma = eng, fn, dma
        o.sig = False
        psr = [r for r in reads if isinstance(r, str) and r.startswith("ps") and r[2:].isdigit()]
        if psr:
            reads = [r for r in reads if r not in psr]
            writes = list(writes) + psr
        deps = set()
        for r in reads:
            w = self.last_w.get(r)
            if w is not None:
                deps.add(w)
        for w_ in writes:
            w = self.last_w.get(w_)
            if w is not None:
                deps.add(w)
            for rd in self.readers.get(w_, ()):
                deps.add(rd)
        for r in reads:
            self.readers.setdefault(r, []).append(o)
        for w_ in writes:
            self.last_w[w_] = o
            self.readers[w_] = []
        deps.discard(o)
        o.deps = deps
        self.ops.append(o)
        if not dma:
            self.last_on[eng] = o
        if dma:
            i = self.dcnt[eng]
            self.dcnt[eng] += 1
            o.dsem = (eng, i % NDMA_SEM)
            o.dval = 16 * (i // NDMA_SEM + 1)
            o.dprev = 16 * (i // NDMA_SEM)
            self.open_dma[o.dsem] = o
        return o

    def barrier(self):
        pend = [o for o in self.last_on.values() if o is not None and not o.dma] + list(self.open_dma.values())
        for e in self.ENGS:
            o = _Op()
            o.eng, o.fn, o.dma, o.sig = e, None, False, False
            o.deps = set(pend)
            self.ops.append(o)
        self.open_dma = {}
        self.last_w = {}
        self.readers = {}

    def emit(self):
        nc = self.nc
        for o in self.ops:
            for d in o.deps:
                if not d.dma:
                    d.sig = True
        cnt = {e: 0 for e in self.ENGS}
        dcnt = self.dcnt
        for o in self.ops:
            if o.dma:
                pass
            elif o.sig:
                cnt[o.eng] += 1
                o.sigval = cnt[o.eng]
        per = {e: [o for o in self.ops if o.eng == e] for e in self.ENGS}
        import contextlib
        with contextlib.ExitStack() as st:
            esem = {e: st.enter_context(nc.semaphore("s_" + e)) for e in self.ENGS}
            dsem = {}
            for e in ("sp", "act", "pool"):
                if dcnt[e]:
                    for i in range(NDMA_SEM):
                        dsem[(e, i)] = st.enter_context(nc.semaphore("d_%s%d" % (e, i)))
            block = st.enter_context(nc.Block())

            def run(ename, eng):
                waited = {}

                def wait(sem_key, sem, val):
                    if waited.get(sem_key, 0) < val:
                        eng.wait_ge(sem, val)
                        waited[sem_key] = val

                for o in per[ename]:
                    for d in o.deps:
                        if d.dma:
                            wait(d.dsem, dsem[d.dsem], d.dval)
                        else:
                            if d.eng == ename and ename == "pe":
                                continue
                            wait(d.eng, esem[d.eng], d.sigval)
                    if o.fn is None:
                        continue
                    if o.dma and o.dprev:
                        wait(o.dsem, dsem[o.dsem], o.dprev)
                    inst = o.fn(eng)
                    if o.dma:
                        inst.then_inc(dsem[o.dsem], 16)
                    elif o.sig:
                        inst.then_inc(esem[ename], 1)

            @block.tensor
            def _(e):
                run("pe", e)

            @block.scalar
            def _(e):
                run("act", e)

            @block.vector
            def _(e):
                run("dve", e)

            @block.gpsimd
            def _(e):
                run("pool", e)

            @block.sync
            def _(e):
                run("sp", e)


ENG_OF = {"act": "act", "dve": "dve", "pool": "pool"}


def MM(S, items, r, w):
    items = list(items)

    def fn(e):
        last = None
        for (out, lhsT, rhs, st, sp) in items:
            last = e.matmul(out, lhsT=lhsT, rhs=rhs, start=st, stop=sp, skip_group_check=True)
        return last
    return S.op("pe", fn, r, w)


def TR(S, items, ident, r, w):
    items = list(items)

    def fn(e):
        last = None
        for (out, in_) in items:
            last = e.transpose(out, in_, ident)
        return last
    return S.op("pe", fn, r, w)


def ACT(S, out, in_, func, r, w, scale=None, bias=None, accum=None):
    kw = {}
    if scale is not None:
        kw["scale"] = scale
    if bias is not None:
        kw["bias"] = bias
    if accum is not None:
        kw["accum_out"] = accum
    return S.op("act", lambda e: e.activation(out=out, in_=in_, func=func, **kw), r, w)


def TS(S, eng, out, in0, s1, s2, op0, op1, r, w):
    if op1 is None:
        return S.op(eng, lambda e: e.tensor_scalar(out, in0, s1, None, op0), r, w)
    return S.op(eng, lambda e: e.tensor_scalar(out, in0, s1, s2, op0, op1), r, w)


def TT(S, eng, out, in0, in1, op, r, w):
    return S.op(eng, lambda e: e.tensor_tensor(out, in0, in1, op), r, w)


def STT(S, out, in0, scalar, in1, op0, op1, r, w):
    return S.op("dve", lambda e: e.scalar_tensor_tensor(out, in0, scalar, in1, op0, op1), r, w)


def CP(S, eng, out, in_, r, w):
    if eng == "act":
        return S.op("act", lambda e: e.copy(out, in_), r, w)
    return S.op(eng, lambda e: e.tensor_copy(out, in_), r, w)


def MSET(S, eng, ap, val, r, w):
    return S.op(eng, lambda e: e.memset(ap, val), r, w)


def RECIP(S, out, in_, r, w):
    return S.op("dve", lambda e: e.reciprocal(out, in_), r, w)


def DMA(S, q, out, in_, r, w):
    return S.op(q, lambda e: e.dma_start(out=out, in_=in_), r, w, dma=True)


class Arena:
    def __init__(self, ap, nwords):
        self.ap = ap
        self.n = nwords
        self.top = 0
        self.floor = 0
        self.uid = 0

    def alloc(self, shape, dtype, name=None):
        nelem = 1
        for s in shape:
            nelem *= s
        nbytes = nelem * (2 if dtype == BF16 else 4)
        nw = (nbytes + 31) // 32 * 8
        assert self.top + nw <= self.n, "arena overflow %s %d+%d>%d" % (name, self.top, nw, self.n)
        a = self.ap[:, self.top:self.top + nw]
        self.top += nw
        if dtype == BF16:
            a = a.bitcast(BF16)
        a = a[:, 0:nelem]
        if len(shape) == 2:
            a = a.rearrange("p (a b) -> p a b", a=shape[0])
        elif len(shape) == 3:
            a = a.rearrange("p (a b c) -> p a b c", a=shape[0], b=shape[1])
        self.uid += 1
        return a

    def keep(self):
        self.floor = self.top

    def reset(self):
        self.top = self.floor


def host_consts(SEQ):
    T = SEQ + CTX
    bf = ml_dtypes.bfloat16
    c = {}
    c["ident_bf"] = np.eye(128, dtype=np.float32).astype(bf)
    c["ident_f"] = np.eye(128, dtype=np.float32)
    pm = np.zeros((128, 128), np.float32)
    for base in (0, 64):
        for i in range(32):
            pm[base + i + 32, base + i] = -1.0
            pm[base + i, base + i + 32] = 1.0
    c["perm_bf"] = pm.astype(bf)
    s_ = np.arange(128)[:, None]
    l_ = np.arange(128)[None, :]
    c["tri_f"] = (s_ <= l_).astype(np.float32)
    c["tri_b"] = (s_ >= l_).astype(np.float32)
    c["ones_f"] = np.ones((128, 128), np.float32)
    c["neg_f"] = np.where(l_ >= s_, 0.0, -30000.0).astype(np.float32)
    c["neg_b"] = np.where(l_ <= s_, 0.0, -30000.0).astype(np.float32)
    c["wm_prev"] = (s_ >= l_).astype(np.float32).astype(bf)
    c["wm_next"] = (s_ <= l_).astype(np.float32).astype(bf)
    t = np.arange(SEQ)
    row = (t // GRID_W).astype(np.float32)
    col = (t % GRID_W).astype(np.float32)
    inv = (10000.0 ** (-np.arange(0, 32, 2, dtype=np.float32) / 32.0)).astype(np.float32)
    ang = np.concatenate([row[:, None] * inv, col[:, None] * inv], axis=-1).astype(np.float32)
    cosT = np.ones((128, T), np.float32)
    sinT = np.zeros((128, T), np.float32)
    for p in range(128):
        cosT[p, :SEQ] = np.cos(ang[:, p % 32])
        sinT[p, :SEQ] = np.sin(ang[:, p % 32])
    c["cosT"] = cosT
    c["sinT"] = sinT
    return c


CONST_SPECS = [("ident_bf", BF16, None), ("ident_f", F32, None), ("perm_bf", BF16, None),
               ("tri_f", F32, None), ("tri_b", F32, None), ("ones_f", F32, None),
               ("neg_f", F32, None), ("neg_b", F32, None), ("wm_prev", BF16, None), ("wm_next", BF16, None)]

W_SPECS = [("w_ada", [D, 6 * D]), ("b_ada", [1, 6 * D]), ("norm_g", [4, D]), ("w_in", [D, IN_WIDTH]),
           ("diff_lambda", [1, 256]), ("diff_norm", [1, 128]), ("ssd_conv_w", [5, 2048]),
           ("ssd_conv_b", [1, 2048]), ("ssd_a_log", [1, 32]), ("ssd_dt_bias", [1, 32]),
           ("ssd_d", [1, 16]), ("ssd_norm", [1, 1024]), ("win_sink", [1, 16]),
           ("w_br_diff", [1024, D]), ("w_br_ssd", [1024, D]), ("w_br_win", [1024, D]),
           ("w_out", [D, D]), ("ffn_w_up", [D, 2 * D_FF]), ("ffn_conv_w", [3, 2 * D_FF]),
           ("ffn_conv_b", [1, 2 * D_FF]), ("ffn_w_down", [D_FF, D])]


class Builder:
    def __init__(self, SEQ, debug=False, nlayers=DEPTH, stop_after=None):
        self.SEQ = SEQ
        self.T = SEQ + CTX
        self.NL = SEQ // 128
        self.NT = self.T // 128
        self.debug = debug
        self.nlayers = nlayers
        self.stop_after = stop_after
        nc = bass.Bass("TRN2", target_bir_lowering=False)
        self.nc = nc
        self.S = Sched(nc)
        T = self.T
        dt = nc.dram_tensor
        self.x_in = dt("x", [SEQ, D], F32, kind="ExternalInput").ap()
        self.ctx_in = dt("ctx", [CTX, D], F32, kind="ExternalInput").ap()
        self.cvec = dt("cvec", [2, D], F32, kind="ExternalInput").ap()
        wspec = dict(W_SPECS)

        class _Lazy(dict):
            def __missing__(d_, name):
                d_[name] = dt(name, [DEPTH] + wspec[name], F32, kind="ExternalInput").ap()
                return d_[name]
        self.W = _Lazy()
        self.C = {}
        for name, dty, _ in CONST_SPECS:
            self.C[name] = dt(name, [128, 128], dty, kind="ExternalInput").ap()
        self.C["cosT"] = dt("cosT", [128, T], F32, kind="ExternalInput").ap()
        self.C["sinT"] = dt("sinT", [128, T], F32, kind="ExternalInput").ap()
        self.out = dt("out", [SEQ, D], F32, kind="ExternalOutput").ap()
        sk = "ExternalOutput" if debug else "Internal"
        self.scr = {}
        for name, shp, dty in [
            ("X", [T, D], F32), ("X2", [T, D], F32), ("MOD", [2, 6 * D], F32),
            ("QDT", [8, 128, T], BF16), ("KDT", [8, 128, T], BF16), ("VD", [T, 1024], BF16),
            ("ZS", [T, 1024], BF16), ("XBCT", [16, 128, T], BF16), ("DT", [T, 32], F32),
            ("WQT", [16, 64, T], BF16), ("WKT", [4, 64, T], BF16), ("WV", [T, 256], BF16),
            ("GT", [48, 128, T], BF16),
            ("UT", [16, 128, T], BF16), ("XSB", [T, 1536], BF16), ("HB", [self.NT, 128, 1024], BF16),
            ("YDT", [8, 128, T], BF16), ("YST", [8, 128, T], BF16), ("YWT", [8, 128, T], BF16),
            ("B_w_br_diff", [1024, D], BF16), ("B_w_br_ssd", [1024, D], BF16), ("B_w_br_win", [1024, D], BF16),
            ("B_w_out", [D, D], BF16), ("B_ffn_w_up", [D, 2 * D_FF], BF16), ("B_ffn_w_down", [D_FF, D], BF16),
        ]:
            self.scr[name] = dt("s_" + name, shp, dty, kind=sk).ap()

    def build(self):
        nc, S = self.nc, self.S
        import contextlib
        with contextlib.ExitStack() as st:
            NW = 50000
            arena = st.enter_context(nc.sbuf_tensor("arena", [128, NW], F32))
            self.A = Arena(arena[:, :], NW)
            self.ps = [st.enter_context(nc.psum_tensor("psb%d" % i, [128, 512], F32)) for i in range(8)]
            self.psn = ["ps%d" % i for i in range(8)]
            self.setup_consts()
            for l in range(self.nlayers):
                self.layer(l)
                if self.stop_after is not None and l == self.stop_after[0]:
                    break
            self.finish()
            S.barrier()
            S.emit()
        return nc

    def newphase(self):
        self.S.barrier()
        self.A.reset()

    def setup_consts(self):
        S, A = self.S, self.A
        self.K = {}
        for name, dty, _ in CONST_SPECS:
            t = A.alloc([128], dty)
            DMA(S, "sp", t, self.C[name][:, :], [], ["c_" + name])
            self.K[name] = t
        self.modT = [A.alloc([96], F32), A.alloc([96], F32)]
        self.ngT = A.alloc([64], F32)
        self.colA1 = [A.alloc([16], F32) for _ in range(2)]
        self.colA2 = [A.alloc([16], F32) for _ in range(2)]
        A.keep()
        X = self.scr["X"]
        DMA(S, "sp", X[0:self.SEQ, :], self.x_in[:, :], [], [])
        DMA(S, "sp", X[self.SEQ:self.T, :], self.ctx_in[:, :], [], [])
        S.barrier()

    def finish(self):
        S = self.S
        S.barrier()
        DMA(S, "sp", self.out[:, :], self.scr["X"][0:self.SEQ, :], [], [])

    def layer(self, l):
        self.need_ctx = l < DEPTH - 1
        self.lam_init = 0.8 - 0.6 * math.exp(-0.3 * l)
        sa = self.stop_after[1] if (self.stop_after is not None and self.stop_after[0] == l) else None
        if sa == "setup":
            return
        self.phase_mod(l)
        if sa == "mod":
            return
        self.phase_inproj(l)
        if sa == "inproj":
            return
        self.phase_diff(l)
        if sa == "diff":
            return
        self.phase_win(l)
        if sa == "win":
            return
        self.phase_ssd(l)
        if sa == "ssd":
            return
        self.phase_merge(l)
        if sa == "merge":
            return
        self.phase_ffn(l)

    def phase_mod(self, l):
        S, A, ps, psn, K = self.S, self.A, self.ps, self.psn, self.K
        self.newphase()
        W = self.W
        identf = K["ident_f"]
        cc = A.alloc([D], F32)
        DMA(S, "sp", cc[0:2, :], self.cvec[:, :], [], ["cc"])
        ACT(S, cc[0:2, :], cc[0:2, :], AF.Silu, ["cc"], ["cc"])
        TR(S, [(ps[0][:, 2 * k:2 * k + 2], cc[0:2, k * 128:(k + 1) * 128]) for k in range(16)],
           identf[0:2, 0:2], ["cc", "c_ident_f"], [psn[0]])
        import os
        stage = int(os.environ.get("MODSTAGE", "99"))
        if stage < 1:
            return
        condT = A.alloc([32], F32)
        CP(S, "dve", condT, ps[0][:, 0:32], [psn[0]], ["condT"])
        if stage < 2:
            return
        modrow = A.alloc([6 * D], F32)
        bb = A.alloc([6 * D], F32)
        DMA(S, "sp", bb[0:2, :], W["b_ada"][l][0:1, :].broadcast_to([2, 6 * D]), [], ["bb"])
        wA = [A.alloc([16, 512], F32) for _ in range(2)]
        nchunk = 6 * D // 512

        def load(j):
            DMA(S, "sp", wA[j % 2], W["w_ada"][l][:, j * 512:(j + 1) * 512].rearrange("(k p) c -> p k c", p=128),
                [], ["wA%d" % (j % 2)])
        load(0)
        for j in range(nchunk):
            if j + 1 < nchunk:
                load(j + 1)
            b = 1 + j % 2
            MM(S, [(ps[b][0:2, :], condT[:, 2 * k:2 * k + 2], wA[j % 2][:, k, :], k == 0, k == 15) for k in range(16)],
               ["condT", "wA%d" % (j % 2)], [psn[b]])
            TT(S, "dve", modrow[0:2, j * 512:(j + 1) * 512], ps[b][0:2, :], bb[0:2, j * 512:(j + 1) * 512], ALU.add,
               [psn[b], "bb"], ["modrow"])
        DMA(S, "sp", self.scr["MOD"][:, :], modrow[0:2, :], ["modrow"], [])
        if stage < 3:
            return
        self.newphase()
        MOD = self.scr["MOD"]
        for r in range(2):
            m96 = A.alloc([128], F32)
            DMA(S, "sp", m96[0:96, :], MOD[r, :].rearrange("(c p) -> c p", p=128), [], ["m96_%d" % r])
            TR(S, [(ps[r][:, 0:96], m96[0:96, :])], identf[0:96, 0:96], ["m96_%d" % r, "c_ident_f"], [psn[r]])
            CP(S, "dve", self.modT[r], ps[r][:, 0:96], [psn[r]], ["modT%d" % r])
        n64 = A.alloc([128], F32)
        DMA(S, "sp", n64[0:64, :], W["norm_g"][l].rearrange("i (k p) -> (i k) p", p=128), [], ["n64"])
        TR(S, [(ps[2][:, 0:64], n64[0:64, :])], identf[0:64, 0:64], ["n64", "c_ident_f"], [psn[2]])
        CP(S, "dve", self.ngT, ps[2][:, 0:64], [psn[2]], ["ngT"])
        for r in range(2):
            STT(S, self.colA1[r], self.modT[r][:, 16:32], 1.0, self.ngT[:, 0:16], ALU.add, ALU.mult,
                ["modT%d" % r, "ngT"], ["colA1_%d" % r])
            STT(S, self.colA2[r], self.modT[r][:, 64:80], 1.0, self.ngT[:, 32:48], ALU.add, ALU.mult,
                ["modT%d" % r, "ngT"], ["colA2_%d" % r])
        self.colB1 = [self.modT[r][:, 0:16] for r in range(2)]
        self.colB2 = [self.modT[r][:, 48:64] for r in range(2)]

    def norm_tiles(self, jobs, colA, colB, hT, banks):
        S, A, ps, psn, K = self.S, self.A, self.ps, self.psn, self.K
        X = self.scr["X"]
        xt = [A.alloc([D], F32) for _ in range(2)]
        xn = [A.alloc([D], BF16) for _ in range(2)]
        junk = A.alloc([D], BF16)
        ss = [A.alloc([1], F32) for _ in range(2)]
        rs = [A.alloc([1], F32) for _ in range(2)]
        names = []
        for i, (g, lo, n, dst) in enumerate(jobs):
            s = i % 2
            r = 0 if g < self.NL else 1
            DMA(S, "sp", xt[s], X[g * 128:(g + 1) * 128, :], [], ["xt%d" % s])
            ACT(S, junk, xt[s], AF.Square, ["xt%d" % s], ["junk", "ss%d" % s], accum=ss[s])
            ACT(S, rs[s], ss[s], AF.Sqrt, ["ss%d" % s], ["rs%d" % s], scale=1.0 / D, bias=EPS)
            RECIP(S, rs[s], rs[s], ["rs%d" % s], ["rs%d" % s])
            TS(S, "dve", xn[s], xt[s], rs[s], None, ALU.mult, None, ["xt%d" % s, "rs%d" % s], ["xn%d" % s])
            bp = banks[i % len(banks)]
            for half in range(2):
                b = bp[half]
                pb = ps[b][:].bitcast(BF16)
                TR(S, [(pb[:, kk * 128:(kk + 1) * 128], xn[s][:, (half * 8 + kk) * 128:(half * 8 + kk + 1) * 128])
                       for kk in range(8)], K["ident_bf"], ["xn%d" % s, "c_ident_bf"], [psn[b]])
            nm = "hT_%d" % i
            for half in range(2):
                b = bp[half]
                pb = ps[b][:].bitcast(BF16)
                for kk in range(8):
                    k = half * 8 + kk
                    src = pb[:, kk * 128 + lo:kk * 128 + lo + n]
                    if half == 0:
                        ACT(S, hT[:, k, dst:dst + n], src, AF.Identity, [psn[b]], [nm + "_%d" % k],
                            scale=colA[r][:, k:k + 1], bias=colB[r][:, k:k + 1])
                    else:
                        TS(S, "dve", hT[:, k, dst:dst + n], src, colA[r][:, k:k + 1], colB[r][:, k:k + 1],
                           ALU.mult, ALU.add, [psn[b]], [nm + "_%d" % k])
            names.append([nm + "_%d" % k for k in range(16)])
        return names

    def phase_inproj(self, l):
        S, A, ps, psn, K = self.S, self.A, self.ps, self.psn, self.K
        W = self.W
        scr = self.scr
        NT = self.NT
        nblk = (NT + 16) // 17
        per = (NT + nblk - 1) // nblk
        blocks = [list(range(b * per, min(NT, (b + 1) * per))) for b in range(nblk)]
        for tiles in blocks:
            self.newphase()
            ntb = len(tiles)
            ntok = ntb * 128
            tok0 = tiles[0] * 128
            hT = A.alloc([16, ntok], BF16)
            hnames = self.norm_tiles([(g, 0, 128, i * 128) for i, g in enumerate(tiles)],
                                     self.colA1, self.colB1, hT, [(0, 1), (2, 3), (4, 5), (6, 7)])
            self.newphase_keep(hT)
            cs = A.alloc([ntok], F32)
            sn = A.alloc([ntok], F32)
            DMA(S, "sp", cs, self.C["cosT"][:, tok0:tok0 + ntok], [], ["cs"])
            DMA(S, "sp", sn, self.C["sinT"][:, tok0:tok0 + ntok], [], ["sn"])
            dtb = A.alloc([32], F32)
            DMA(S, "sp", dtb, W["ssd_dt_bias"][l][0:1, :].broadcast_to([128, 32]), [], ["dtb"])
            wt = [A.alloc([16, 512], BF16) for _ in range(2)]
            qsb = [A.alloc([512], BF16) for _ in range(2)]
            t1 = [A.alloc([512], F32) for _ in range(2)]
            t2 = [A.alloc([512], F32) for _ in range(2)]
            obf = [A.alloc([ntok], BF16) for _ in range(2)]
            obt = [A.alloc([512], BF16) for _ in range(3)]
            dts = [A.alloc([32], F32) for _ in range(2)]
            cblocks = []
            import os
            gsel = [int(v) for v in os.environ.get("INPROJ_GROUPS", "0,1,2,3,4,5,6,7,8,9").split(",")]
            for gi in gsel:
                for c0 in range(0, IN_SIZES[gi], 512):
                    cblocks.append((gi, c0, min(512, IN_SIZES[gi] - c0)))
            chunks = [(c, min(512, ntok - c)) for c in range(0, ntok, 512)]
            st = {"bank": 0, "fm": 0, "tm": 0, "rp": 0}

            def loadw(i):
                gi, c0, nco = cblocks[i]
                a = IN_OFF[gi] + c0
                DMA(S, "pool", wt[i % 2][:, :, 0:nco],
                    W["w_in"][l][:, a:a + nco].rearrange("(k p) c -> p k c", p=128), [], ["wt%d" % (i % 2)])

            def hres(c, n):
                out = []
                for ti in range(c // 128, (c + n) // 128):
                    out += hnames[ti]
                return out

            loadw(0)
            for i, (gi, c0, nco) in enumerate(cblocks):
                if i + 1 < len(cblocks):
                    loadw(i + 1)
                ws = i % 2
                wn = "wt%d" % ws
                fm = gi in (0, 1, 4, 6, 7, 9)
                if fm:
                    for m in range(nco // 128):
                        fs = st["fm"] % 2
                        st["fm"] += 1
                        fb = (c0 + m * 128) // 128
                        chunk_names = []
                        for ci, (c, n) in enumerate(chunks):
                            b = st["bank"] % 4
                            st["bank"] += 1
                            MM(S, [(ps[b][:, 0:n], wt[ws][:, k, m * 128:(m + 1) * 128], hT[:, k, c:c + n], k == 0, k == 15)
                                   for k in range(16)], [wn], [psn[b]])
                            on = "obf%d_%d" % (fs, ci)
                            chunk_names.append(on)
                            if gi in (0, 1, 6, 7):
                                rs_ = st["rp"] % 2
                                st["rp"] += 1
                                pb = 4 + rs_
                                CP(S, "act", qsb[rs_][:, 0:n], ps[b][:, 0:n], [psn[b]], ["qsb%d" % rs_])
                                MM(S, [(ps[pb][:, 0:n], K["perm_bf"], qsb[rs_][:, 0:n], True, True)],
                                   ["qsb%d" % rs_], [psn[pb]])
                                TT(S, "dve", t1[rs_][:, 0:n], ps[b][:, 0:n], cs[:, c:c + n], ALU.mult,
                                   [psn[b], "cs"], ["t1_%d" % rs_])
                                TT(S, "dve", t2[rs_][:, 0:n], ps[pb][:, 0:n], sn[:, c:c + n], ALU.mult,
                                   [psn[pb], "sn"], ["t2_%d" % rs_])
                                TT(S, "pool", obf[fs][:, c:c + n], t1[rs_][:, 0:n], t2[rs_][:, 0:n], ALU.add,
                                   ["t1_%d" % rs_, "t2_%d" % rs_], [on])
                            elif gi == 4:
                                CP(S, "act", obf[fs][:, c:c + n], ps[b][:, 0:n], [psn[b]], [on])
                            else:
                                ACT(S, obf[fs][:, c:c + n], ps[b][:, 0:n], AF.Sigmoid, [psn[b]], [on])
                        if gi == 0:
                            dst = scr["QDT"][fb][:, tok0:tok0 + ntok]
                        elif gi == 1:
                            dst = scr["KDT"][fb][:, tok0:tok0 + ntok]
                        elif gi == 4:
                            dst = scr["XBCT"][fb][:, tok0:tok0 + ntok]
                        elif gi == 6:
                            dst = scr["WQT"].rearrange("(a two) d t -> a (two d) t", two=2)[fb][:, tok0:tok0 + ntok]
                        elif gi == 7:
                            dst = scr["WKT"].rearrange("(a two) d t -> a (two d) t", two=2)[fb][:, tok0:tok0 + ntok]
                        else:
                            dst = scr["GT"][fb][:, tok0:tok0 + ntok]
                        DMA(S, "sp", dst, obf[fs], chunk_names, [])
                else:
                    for ti, g in enumerate(tiles):
                        b = st["bank"] % 4
                        st["bank"] += 1
                        MM(S, [(ps[b][:, 0:nco], hT[:, k, ti * 128:(ti + 1) * 128], wt[ws][:, k, 0:nco], k == 0, k == 15)
                               for k in range(16)], [wn], [psn[b]])
                        rows = slice(g * 128, (g + 1) * 128)
                        if gi == 5:
                            d_ = st["tm"] % 2
                            st["tm"] += 1
                            TT(S, "dve", dts[d_], ps[b][:, 0:32], dtb, ALU.add, [psn[b], "dtb"], ["dts%d" % d_])
                            ACT(S, dts[d_], dts[d_], AF.Exp, ["dts%d" % d_], ["dts%d" % d_])
                            ACT(S, dts[d_], dts[d_], AF.Ln, ["dts%d" % d_], ["dts%d" % d_], bias=1.0)
                            DMA(S, "sp", scr["DT"][rows, :], dts[d_], ["dts%d" % d_], [])
                            continue
                        o_ = st["tm"] % 3
                        st["tm"] += 1
                        on = "obt%d" % o_
                        if gi == 3:
                            ACT(S, obt[o_][:, 0:nco], ps[b][:, 0:nco], AF.Silu, [psn[b]], [on])
                            dst = scr["ZS"][rows, c0:c0 + nco]
                        elif gi == 2:
                            CP(S, "act", obt[o_][:, 0:nco], ps[b][:, 0:nco], [psn[b]], [on])
                            dst = scr["VD"][rows, c0:c0 + nco]
                        else:
                            CP(S, "act", obt[o_][:, 0:nco], ps[b][:, 0:nco], [psn[b]], [on])
                            dst = scr["WV"][rows, c0:c0 + nco]
                        DMA(S, "sp", dst, obt[o_][:, 0:nco], [on], [])

    def newphase_keep(self, *_):
        self.S.barrier()

    def phase_diff(self, l):
        S, A, ps, psn, K = self.S, self.A, self.ps, self.psn, self.K
        W, scr = self.W, self.scr
        T, NT, NL, SEQ = self.T, self.NT, self.NL, self.SEQ
        self.newphase()
        dl = A.alloc([256], F32)
        DMA(S, "sp", dl, W["diff_lambda"][l][0:1, :].broadcast_to([128, 256]), [], ["dl"])
        jk = A.alloc([128], F32)
        s12 = A.alloc([2], F32)
        TT(S, "dve", jk[:, 0:64], dl[:, 0:64], dl[:, 64:128], ALU.mult, ["dl"], ["jk"])
        ACT(S, jk[:, 0:64], jk[:, 0:64], AF.Identity, ["jk"], ["jk", "s12a"], accum=s12[:, 0:1])
        TT(S, "dve", jk[:, 64:128], dl[:, 128:192], dl[:, 192:256], ALU.mult, ["dl"], ["jk2"])
        ACT(S, jk[:, 64:128], jk[:, 64:128], AF.Identity, ["jk2"], ["jk2", "s12b"], accum=s12[:, 1:2])
        ACT(S, s12, s12, AF.Exp, ["s12a", "s12b"], ["s12"])
        nlam = A.alloc([1], F32)
        TT(S, "dve", nlam, s12[:, 0:1], s12[:, 1:2], ALU.subtract, ["s12"], ["nlam"])
        TS(S, "dve", nlam, nlam, float(self.lam_init), -1.0, ALU.add, ALU.mult, ["nlam"], ["nlam"])
        dn = A.alloc([1], F32)
        DMA(S, "sp", dn, W["diff_norm"][l].rearrange("o p -> p o"), [], ["dn"])
        TS(S, "dve", dn, dn, float(1.0 - self.lam_init), None, ALU.mult, None, ["dn"], ["dn"])
        KT = [A.alloc([T], BF16) for _ in range(2)]
        V = [A.alloc([NT, 129], BF16) for _ in range(2)]
        for s in range(2):
            MSET(S, "pool", V[s][:, :, 128:129], 1.0, [], ["V%d" % s])
        QT = [A.alloc([512], BF16) for _ in range(2)]
        PT = [[A.alloc([512], BF16) for _ in range(2)] for _ in range(2)]
        rr = A.alloc([2], F32)
        o = A.alloc([128], F32)
        sq = A.alloc([128], BF16)
        ssq = A.alloc([1], F32)
        on = A.alloc([512], BF16)
        ob = [A.alloc([512], BF16) for _ in range(2)]

        def acc(j, qi):
            a = j * 4 + qi
            return ps[4 + a // 3][:, (a % 3) * 129:(a % 3) * 129 + 129], psn[4 + a // 3]

        cv = [A.alloc([11264], BF16) for _ in range(2)]
        cjobs = []
        for nm in ("w_br_diff", "w_br_ssd", "w_br_win"):
            cjobs += [(nm, 8, c0, 512) for c0 in range(0, D, 512)]
        cjobs += [("w_out", 16, c0, 512) for c0 in range(0, D, 512)]
        cjobs += [("ffn_w_up", 16, c0, 512) for c0 in range(0, 2 * D_FF, 512)]
        cjobs += [("ffn_w_down", 44, c0, 256) for c0 in range(0, D, 256)]
        cstate = {"i": 0}

        def convert_some(n):
            for _ in range(n):
                i = cstate["i"]
                if i >= len(cjobs):
                    return
                cstate["i"] += 1
                nm, kc, c0, ncol = cjobs[i]
                sl = i % 2
                tile_ = cv[sl][:, 0:kc * ncol].rearrange("p (k c) -> p k c", k=kc)
                DMA(S, "pool", tile_, W[nm][l][:, c0:c0 + ncol].rearrange("(k p) c -> p k c", p=128), [], ["cv%d" % sl])
                DMA(S, "sp", scr["B_" + nm][:, c0:c0 + ncol].rearrange("(k p) c -> p k c", p=128), tile_, ["cv%d" % sl], [])

        qc = 0
        for h in range(8):
            hs = h % 2
            DMA(S, "sp", KT[hs], scr["KDT"][h][:, :], [], ["KT%d" % hs])
            DMA(S, "sp", V[hs][:, :, 0:128], scr["VD"][:, h * 128:(h + 1) * 128].rearrange("(t p) c -> p t c", p=128),
                [], ["V%d" % hs])
            qchunks = [(q0, min(512, SEQ - q0), list(range(NT))) for q0 in range(0, SEQ, 512)]
            if self.need_ctx:
                qchunks.append((SEQ, CTX, [NL, NL + 1]))
            for (q0, nq, kts) in qchunks:
                qs = qc % 2
                qc += 1
                nqt = nq // 128
                DMA(S, "sp", QT[qs][:, 0:nq], scr["QDT"][h][:, q0:q0 + nq], [], ["QT%d" % qs])
                convert_some(1)
                for b in (4, 5, 6):
                    MSET(S, "dve", ps[b][:, :], 0.0, [], [psn[b]])
                def emit_s(ki):
                    kt = kts[ki]
                    sb = ki % 2
                    for j in range(2):
                        b = sb * 2 + j
                        MM(S, [(ps[b][:, 0:nq], KT[hs][j * 64:(j + 1) * 64, kt * 128:(kt + 1) * 128],
                                QT[qs][j * 64:(j + 1) * 64, 0:nq], True, True)],
                           ["KT%d" % hs, "QT%d" % qs], [psn[b]])
                        ACT(S, PT[sb][j][:, 0:nq], ps[b][:, 0:nq], AF.Exp, [psn[b]], ["PT%d%d" % (sb, j)], scale=0.125)

                def emit_pv(ki):
                    kt = kts[ki]
                    sb = ki % 2
                    for j in range(2):
                        items = []
                        wb = set()
                        for qi in range(nqt):
                            ap_, bn = acc(j, qi)
                            wb.add(bn)
                            items.append((ap_, PT[sb][j][:, qi * 128:(qi + 1) * 128], V[hs][:, kt, :], False, False))
                        MM(S, items, ["PT%d%d" % (sb, j), "V%d" % hs], sorted(wb))

                emit_s(0)
                for ki in range(len(kts)):
                    if ki + 1 < len(kts):
                        emit_s(ki + 1)
                    emit_pv(ki)
                for qi in range(nqt):
                    a0, b0 = acc(0, qi)
                    a1, b1 = acc(1, qi)
                    RECIP(S, rr[:, 0:1], a0[:, 128:129], [b0], ["rr0"])
                    RECIP(S, rr[:, 1:2], a1[:, 128:129], [b1], ["rr1"])
                    TT(S, "dve", rr[:, 1:2], rr[:, 1:2], nlam, ALU.mult, ["rr1", "nlam"], ["rr1"])
                    TS(S, "dve", o, a0[:, 0:128], rr[:, 0:1], None, ALU.mult, None, [b0, "rr0"], ["o"])
                    STT(S, o, a1[:, 0:128], rr[:, 1:2], o, ALU.mult, ALU.add, [b1, "rr1", "o"], ["o"])
                    ACT(S, sq, o, AF.Square, ["o"], ["sq", "ssq"], accum=ssq)
                    ACT(S, ssq, ssq, AF.Sqrt, ["ssq"], ["ssq"], scale=1.0 / 128, bias=EPS)
                    RECIP(S, ssq, ssq, ["ssq"], ["ssq"])
                    TS(S, "dve", on[:, qi * 128:(qi + 1) * 128], o, ssq, None, ALU.mult, None, ["o", "ssq"], ["on%d" % qi])
                pb = ps[7][:].bitcast(BF16)
                TR(S, [(pb[:, qi * 128:(qi + 1) * 128], on[:, qi * 128:(qi + 1) * 128]) for qi in range(nqt)],
                   K["ident_bf"], ["on%d" % qi for qi in range(nqt)], [psn[7]])
                ACT(S, ob[qs][:, 0:nq], pb[:, 0:nq], AF.Identity, [psn[7], "dn"], ["ob%d" % qs], scale=dn[:, 0:1])
                DMA(S, "sp", scr["YDT"][h][:, q0:q0 + nq], ob[qs][:, 0:nq], ["ob%d" % qs], [])
        convert_some(len(cjobs))

    def phase_win(self, l):
        S, A, ps, psn, K = self.S, self.A, self.ps, self.psn, self.K
        W, scr = self.W, self.scr
        T, NT, NL, SEQ = self.T, self.NT, self.NL, self.SEQ
        self.newphase()
        KT = A.alloc([4, T], BF16)
        DMA(S, "sp", KT[0:64], scr["WKT"].rearrange("k d t -> d k t"), [], ["KT"])
        V = A.alloc([NT, 4, 65], BF16)
        MSET(S, "pool", V[:, :, :, 64:65], 1.0, [], ["V"])
        for kvh in range(4):
            DMA(S, "sp", V[:, :, kvh, 0:64], scr["WV"][:, kvh * 64:(kvh + 1) * 64].rearrange("(t p) d -> p t d", p=128),
                [], ["V"])
        es = A.alloc([16], F32)
        DMA(S, "sp", es, W["win_sink"][l][0:1, :].broadcast_to([128, 16]), [], ["es"])
        ACT(S, es, es, AF.Exp, ["es"], ["es"])
        QTb = [A.alloc([16, 128], BF16) for _ in range(2)]
        PTw = [A.alloc([512], BF16) for _ in range(3)]
        tmp = [A.alloc([512], BF16) for _ in range(2)]
        dd = A.alloc([4], F32)
        yw = [A.alloc([1024], BF16) for _ in range(2)]
        ob = [A.alloc([8, 128], BF16) for _ in range(2)]
        qbs = list(range(NL)) + ([NL, NL + 1] if self.need_ctx else [])
        cnt = {"s": 0, "p": 0, "t": 0}
        for qi_, qb in enumerate(qbs):
            s = qi_ % 2
            if qb < NL:
                keys = ([(qb - 1, "prev")] if qb > 0 else []) + [(qb, "c")] + \
                       ([(qb + 1, "next")] if qb < NL - 1 else []) + [(NL, "c"), (NL + 1, "c")]
            else:
                keys = [(NL, "c"), (NL + 1, "c")]
            DMA(S, "sp", QTb[s][0:64], scr["WQT"][:, :, qb * 128:(qb + 1) * 128].rearrange("h d t -> d h t"),
                [], ["QTb%d" % s])
            for kvh in range(4):
                ab = 4 + kvh % 2
                MSET(S, "dve", ps[ab][:, 0:260], 0.0, [], [psn[ab]])
                pend = []

                def flush():
                    for (p__, kt__) in pend:
                        MM(S, [(ps[ab][:, g * 65:(g + 1) * 65], PTw[p__][:, g * 128:(g + 1) * 128], V[:, kt__, kvh, :], False, False)
                               for g in range(4)], ["PTw%d" % p__, "V"], [psn[ab]])
                    del pend[:]

                for (kt, kind) in keys:
                    sb = cnt["s"] % 4
                    cnt["s"] += 1
                    p_ = cnt["p"] % 3
                    cnt["p"] += 1
                    MM(S, [(ps[sb][:, :], KT[0:64, kvh, kt * 128:(kt + 1) * 128],
                            QTb[s][0:64, kvh * 4:(kvh + 1) * 4, :], True, True)], ["KT", "QTb%d" % s], [psn[sb]])
                    if kind == "c":
                        ACT(S, PTw[p_], ps[sb][:, :], AF.Exp, [psn[sb]], ["PTw%d" % p_], scale=0.125)
                    else:
                        t_ = cnt["t"] % 2
                        cnt["t"] += 1
                        ACT(S, tmp[t_], ps[sb][:, :], AF.Exp, [psn[sb]], ["tmp%d" % t_], scale=0.125)
                        mk = K["wm_prev"] if kind == "prev" else K["wm_next"]
                        TT(S, "pool", PTw[p_].rearrange("p (g q) -> p g q", g=4),
                           tmp[t_].rearrange("p (g q) -> p g q", g=4),
                           mk.unsqueeze(1).broadcast_to([128, 4, 128]), ALU.mult, ["tmp%d" % t_], ["PTw%d" % p_])
                    flush()
                    pend.append((p_, kt))
                flush()
                av = ps[ab][:, 0:260].rearrange("p (g e) -> p g e", g=4)
                TT(S, "dve", dd, av[:, :, 64], es[:, kvh * 4:(kvh + 1) * 4], ALU.add, [psn[ab], "es"], ["dd"])
                RECIP(S, dd, dd, ["dd"], ["dd"])
                TT(S, "dve", yw[s][:, kvh * 256:(kvh + 1) * 256].rearrange("p (g e) -> p g e", g=4), av[:, :, 0:64],
                   dd.unsqueeze(2).broadcast_to([128, 4, 64]), ALU.mult, [psn[ab], "dd"], ["yw%d_%d" % (s, kvh)])
            pb = ps[7][:].bitcast(BF16)
            TR(S, [(pb[:, k * 128:(k + 1) * 128], yw[s][:, k * 128:(k + 1) * 128]) for k in range(8)], K["ident_bf"],
               ["yw%d_%d" % (s, kvh) for kvh in range(4)], [psn[7]])
            CP(S, "act", ob[s].rearrange("p k t -> p (k t)"), pb[:, 0:1024], [psn[7]], ["ob%d" % s])
            DMA(S, "sp", scr["YWT"][:, :, qb * 128:(qb + 1) * 128].rearrange("k p t -> p k t"), ob[s], ["ob%d" % s], [])

    def phase_ssd(self, l):
        S, A, ps, psn, K = self.S, self.A, self.ps, self.psn, self.K
        W, scr = self.W, self.scr
        T, NT, NL, SEQ = self.T, self.NT, self.NL, self.SEQ
        identf = K["ident_f"]
        self.newphase()
        cw6 = A.alloc([2048], F32)
        DMA(S, "sp", cw6[0:5, :], W["ssd_conv_w"][l][:, :], [], ["cw6"])
        DMA(S, "sp", cw6[5:6, :], W["ssd_conv_b"][l][:, :], [], ["cw6"])
        TR(S, [(ps[0][:, cb * 6:(cb + 1) * 6], cw6[0:6, cb * 128:(cb + 1) * 128]) for cb in range(16)],
           identf[0:6, 0:6], ["cw6"], [psn[0]])
        cw = A.alloc([16, 6], F32)
        CP(S, "dve", cw.rearrange("p a b -> p (a b)"), ps[0][:, 0:96], [psn[0]], ["cw"])
        xin = [A.alloc([T], BF16) for _ in range(2)]
        acc = [A.alloc([T], F32) for _ in range(2)]
        u = [A.alloc([T], BF16) for _ in range(2)]
        stg = [A.alloc([8, 128], BF16) for _ in range(2)]
        tcnt = 0
        for cb in range(16):
            s = cb % 2
            DMA(S, "sp", xin[s], scr["XBCT"][cb][:, :], [], ["xin%d" % s])
            ACT(S, acc[s], xin[s], AF.Identity, ["xin%d" % s, "cw"], ["acc%d" % s], scale=cw[:, cb, 2:3], bias=cw[:, cb, 5:6])
            for j in (0, 1, 3, 4):
                d = j - 2
                for (a, b) in ((0, SEQ), (SEQ, T)):
                    lo, hi = (a - d, b) if d < 0 else (a, b - d)
                    STT(S, acc[s][:, lo:hi], xin[s][:, lo + d:hi + d], cw[:, cb, j:j + 1], acc[s][:, lo:hi],
                        ALU.mult, ALU.add, ["xin%d" % s, "cw", "acc%d" % s], ["acc%d" % s])
            ACT(S, u[s], acc[s], AF.Silu, ["acc%d" % s], ["u%d" % s])
            if cb >= 8:
                DMA(S, "sp", scr["UT"][cb][:, :], u[s], ["u%d" % s], [])
            if cb < 12:
                for t0 in range(0, NT, 8):
                    nt_ = min(8, NT - t0)
                    q = tcnt % 2
                    tcnt += 1
                    pb = ps[1 + q][:].bitcast(BF16)
                    TR(S, [(pb[:, i * 128:(i + 1) * 128], u[s][:, (t0 + i) * 128:(t0 + i + 1) * 128]) for i in range(nt_)],
                       K["ident_bf"], ["u%d" % s], [psn[1 + q]])
                    CP(S, "dve" if q else "act", stg[q].rearrange("p a b -> p (a b)")[:, 0:nt_ * 128], pb[:, 0:nt_ * 128],
                       [psn[1 + q]], ["stg%d" % q])
                    DMA(S, "sp", scr["XSB"][t0 * 128:(t0 + nt_) * 128, cb * 128:(cb + 1) * 128].rearrange("(t p) c -> p t c", p=128),
                        stg[q][:, 0:nt_, :], ["stg%d" % q], [])
        self.newphase()
        abc = A.alloc([32], F32)
        DMA(S, "sp", abc, W["ssd_a_log"][l][0:1, :].broadcast_to([128, 32]), [], ["abc"])
        ACT(S, abc, abc, AF.Exp, ["abc"], ["abc"])
        TS(S, "dve", abc, abc, -1.0, None, ALU.mult, None, ["abc"], ["abc"])
        dsk = A.alloc([16], F32)
        DMA(S, "sp", dsk, W["ssd_d"][l][0:1, :].broadcast_to([128, 16]), [], ["dsk"])
        sn8 = A.alloc([128], F32)
        DMA(S, "sp", sn8[0:8, :], W["ssd_norm"][l].rearrange("o (k p) -> (o k) p", p=128), [], ["sn8"])
        TR(S, [(ps[0][:, 0:8], sn8[0:8, :])], identf[0:8, 0:8], ["sn8"], [psn[0]])
        snc = A.alloc([8], F32)
        CP(S, "dve", snc, ps[0][:, 0:8], [psn[0]], ["snc"])
        xsb = [A.alloc([1536], BF16) for _ in range(2)]
        dtt = [A.alloc([32], F32) for _ in range(2)]
        ad = [A.alloc([32], F32) for _ in range(2)]
        cs = [A.alloc([64], F32) for _ in range(2)]
        dec = [A.alloc([96], F32) for _ in range(2)]
        ncum = [A.alloc([32], F32) for _ in range(2)]
        xdt = [A.alloc([2, 1024], BF16) for _ in range(2)]
        xdtd = [A.alloc([2, 1024], BF16) for _ in range(2)]
        adB = [A.alloc([32, 128], F32) for _ in range(2)]
        adT = [A.alloc([32, 128], F32) for _ in range(2)]
        negones = A.alloc([128], F32)
        MSET(S, "pool", negones, -1.0, [], ["negones"])

        def v16(ap):
            return ap.rearrange("p (h e) -> p h e", e=64)

        def bc(ap, n):
            return ap.unsqueeze(2).broadcast_to([128, n, 64])

        def prep(c, s, dirs, full):
            n_ = "_%d" % s
            DMA(S, "sp", xsb[s], scr["XSB"][c * 128:(c + 1) * 128, :], [], ["xsb" + n_])
            DMA(S, "sp", dtt[s], scr["DT"][c * 128:(c + 1) * 128, :], [], ["dtt" + n_])
            TT(S, "dve", ad[s], dtt[s], abc, ALU.mult, ["dtt" + n_, "abc"], ["ad" + n_])
            MM(S, [(ps[0][:, 0:16], K["tri_f"], ad[s][:, 0:16], True, True),
                   (ps[0][:, 16:32], K["tri_b"], ad[s][:, 16:32], True, True),
                   (ps[0][:, 32:64], K["ones_f"], ad[s][:, 0:32], True, True)], ["ad" + n_], [psn[0]])
            CP(S, "dve", cs[s], ps[0][:, 0:64], [psn[0]], ["cs" + n_])
            TT(S, "dve", dec[s][:, 32:64], cs[s][:, 32:64], cs[s][:, 0:32], ALU.subtract, ["cs" + n_], ["decs" + n_])
            ACT(S, dec[s][:, 32:64], dec[s][:, 32:64], AF.Exp, ["decs" + n_], ["decs" + n_])
            ACT(S, dec[s][:, 0:32], cs[s][:, 0:32], AF.Exp, ["cs" + n_], ["deco" + n_])
            ACT(S, dec[s][:, 64:96], cs[s][:, 32:64], AF.Exp, ["cs" + n_], ["dect" + n_])
            if full:
                CP(S, "pool", adB[s], ad[s].unsqueeze(2).broadcast_to([128, 32, 128]), ["ad" + n_], ["adB" + n_])
                TT(S, "pool", adT[s][:, 0:16, :], adB[s][:, 0:16, :], K["tri_f"].unsqueeze(1).broadcast_to([128, 16, 128]),
                   ALU.mult, ["adB" + n_], ["adT" + n_])
                TT(S, "dve", adT[s][:, 16:32, :], adB[s][:, 16:32, :], K["tri_b"].unsqueeze(1).broadcast_to([128, 16, 128]),
                   ALU.mult, ["adB" + n_], ["adT" + n_])
            for d in dirs:
                e1 = "pool" if d == 0 else "dve"
                TT(S, e1, v16(xdt[s][:, d, :]), v16(xsb[s][:, 0:1024]), bc(dtt[s][:, d * 16:(d + 1) * 16], 16), ALU.mult,
                   ["xsb" + n_, "dtt" + n_], ["xdt%d" % d + n_])
                TT(S, e1, v16(xdtd[s][:, d, :]), v16(xdt[s][:, d, :]), bc(dec[s][:, 32 + d * 16:32 + (d + 1) * 16], 16), ALU.mult,
                   ["xdt%d" % d + n_, "decs" + n_], ["xdtd%d" % d + n_])

        def state_update(H, s, d, hname):
            n_ = "_%d" % s
            for g in range(4):
                b = 2 + g % 2
                MM(S, [(ps[b][:, 0:256], xsb[s][:, 1024 + g * 128:1024 + (g + 1) * 128], xdtd[s][:, d, g * 256:(g + 1) * 256], True, True)],
                   ["xsb" + n_, "xdtd%d" % d + n_], [psn[b]])
                hv = v16(H[:, g * 256:(g + 1) * 256])
                TT(S, "dve", hv, hv, bc(dec[s][:, 64 + d * 16 + g * 4:64 + d * 16 + (g + 1) * 4], 4), ALU.mult,
                   [hname, "dect" + n_], [hname])
                TT(S, "dve", H[:, g * 256:(g + 1) * 256], H[:, g * 256:(g + 1) * 256], ps[b][:, 0:256], ALU.add,
                   [hname, psn[b]], [hname])

        Hb = A.alloc([1024], F32)
        Hbb = [A.alloc([1024], BF16) for _ in range(2)]
        MSET(S, "dve", Hb, 0.0, [], ["Hb"])
        order_b = [NL + 1, NL] + list(range(NL - 1, -1, -1))
        for i, c in enumerate(order_b):
            s = i % 2
            prep(c, s, [1], False)
            CP(S, "act", Hbb[s], Hb, ["Hb"], ["Hbb%d" % s])
            DMA(S, "sp", scr["HB"][c], Hbb[s], ["Hbb%d" % s], [])
            state_update(Hb, s, 1, "Hb")
        self.newphase_keep()
        Hf = A.alloc([1024], F32)
        Hfb = A.alloc([1024], BF16)
        MSET(S, "dve", Hf, 0.0, [], ["Hf"])
        bct = [A.alloc([8, 128], BF16) for _ in range(2)]
        hb = [A.alloc([1024], BF16) for _ in range(2)]
        zs = [A.alloc([1024], BF16) for _ in range(2)]
        gt = [A.alloc([128], BF16) for _ in range(2)]
        lt = [A.alloc([512], BF16) for _ in range(2)]
        mt = [A.alloc([512], BF16) for _ in range(2)]
        yo = [A.alloc([1024], F32) for _ in range(2)]
        xd = A.alloc([1024], F32)
        yt = A.alloc([1024], F32)
        junk = A.alloc([1024], BF16)
        ssq = A.alloc([1], F32)
        yn = A.alloc([1024], BF16)
        ob = [A.alloc([8, 128], BF16) for _ in range(2)]
        order_f = [NL, NL + 1] + list(range(NL))
        kk = 0
        for i, c in enumerate(order_f):
            s = i % 2
            n_ = "_%d" % s
            want_out = (c < NL) or self.need_ctx
            prep(c, s, [0, 1] if want_out else [0], want_out)
            if want_out:
                DMA(S, "sp", bct[s], scr["UT"][8:16, :, c * 128:(c + 1) * 128].rearrange("k p t -> p k t"), [], ["bct" + n_])
                DMA(S, "sp", hb[s], scr["HB"][c], [], ["hb" + n_])
                DMA(S, "sp", zs[s], scr["ZS"][c * 128:(c + 1) * 128, :], [], ["zs" + n_])
                CP(S, "act", Hfb, Hf, ["Hf"], ["Hfb"])
                MSET(S, "dve", ps[4][:, :], 0.0, [], [psn[4]])
                MSET(S, "dve", ps[5][:, :], 0.0, [], [psn[5]])
                for g in range(4):
                    MM(S, [(ps[1][:, 0:128], bct[s][:, g, :], bct[s][:, 4 + g, :], True, True)], ["bct" + n_], [psn[1]])
                    CP(S, "act", gt[g % 2], ps[1][:, 0:128], [psn[1]], ["gt%d" % (g % 2)])
                    for d in range(2):
                        q = kk % 2
                        kk += 1
                        b = 2 + q
                        items = []
                        for hh in range(4):
                            c_ = d * 16 + g * 4 + hh
                            o_ = ps[b][:, hh * 128:(hh + 1) * 128]
                            items += [(o_, adB[s][:, c_, :], K["tri_f"] if d == 0 else K["tri_b"], True, False),
                                      (o_, adT[s][:, c_, :], negones, False, False),
                                      (o_, identf, K["neg_f"] if d == 0 else K["neg_b"], False, True)]
                        MM(S, items, ["adB" + n_, "adT" + n_, "negones"], [psn[b]])
                        ACT(S, lt[q], ps[b][:, :], AF.Exp, [psn[b]], ["lt%d" % q])
                        TT(S, "pool" if q else "dve", mt[q].rearrange("p (a b) -> p a b", a=4), lt[q].rearrange("p (a b) -> p a b", a=4),
                           gt[g % 2].unsqueeze(1).broadcast_to([128, 4, 128]), ALU.mult, ["lt%d" % q, "gt%d" % (g % 2)], ["mt%d" % q])
                        yb = 4 + (g * 4) // 8
                        MM(S, [(ps[yb][:, ((g * 4 + hh) % 8) * 64:((g * 4 + hh) % 8 + 1) * 64], mt[q][:, hh * 128:(hh + 1) * 128],
                                xdt[s][:, d, (g * 4 + hh) * 64:(g * 4 + hh + 1) * 64], False, False) for hh in range(4)],
                           ["mt%d" % q, "xdt%d" % d + n_], [psn[yb]])
                for d in range(2):
                    hsrc = Hfb if d == 0 else hb[s]
                    hn = "Hfb" if d == 0 else "hb" + n_
                    for g in range(4):
                        MM(S, [(ps[6 + g // 2][:, (g % 2) * 256:(g % 2 + 1) * 256], bct[s][:, 4 + g, :], hsrc[:, g * 256:(g + 1) * 256], True, True)],
                           ["bct" + n_, hn], [psn[6 + g // 2]])
                    for half in range(2):
                        TT(S, "dve", yo[d][:, half * 512:(half + 1) * 512].rearrange("p (h e) -> p h e", e=64),
                           ps[6 + half][:, :].rearrange("p (h e) -> p h e", e=64),
                           bc(dec[s][:, d * 16 + half * 8:d * 16 + half * 8 + 8], 8), ALU.mult,
                           [psn[6 + half], "deco" + n_], ["yo%d_%d" % (d, half)])
                TT(S, "pool", yo[0], yo[0], yo[1], ALU.add, ["yo0_0", "yo0_1", "yo1_0", "yo1_1"], ["yo0_0", "yo0_1"])
                TT(S, "pool", v16(xd), v16(xsb[s][:, 0:1024]), bc(dsk, 16), ALU.mult, ["xsb" + n_, "dsk"], ["xd"])
                TT(S, "pool", yo[0], yo[0], xd, ALU.add, ["yo0_0", "yo0_1", "xd"], ["yo0_0", "yo0_1"])
                for half in range(2):
                    TT(S, "dve", yt[:, half * 512:(half + 1) * 512], ps[4 + half][:, :], yo[0][:, half * 512:(half + 1) * 512], ALU.add,
                       [psn[4 + half], "yo0_0", "yo0_1"], ["yt%d" % half])
                TT(S, "dve", yt, yt, zs[s], ALU.mult, ["yt0", "yt1", "zs" + n_], ["yt0", "yt1"])
                ACT(S, junk, yt, AF.Square, ["yt0", "yt1"], ["junk", "ssq"], accum=ssq)
                ACT(S, ssq, ssq, AF.Sqrt, ["ssq"], ["ssq"], scale=1.0 / 1024, bias=EPS)
                RECIP(S, ssq, ssq, ["ssq"], ["ssq"])
                TS(S, "dve", yn, yt, ssq, None, ALU.mult, None, ["yt0", "yt1", "ssq"], ["yn"])
                pb = ps[1][:].bitcast(BF16)
                TR(S, [(pb[:, k * 128:(k + 1) * 128], yn[:, k * 128:(k + 1) * 128]) for k in range(8)], K["ident_bf"], ["yn"], [psn[1]])
                for k in range(8):
                    ACT(S, ob[s][:, k, :], pb[:, k * 128:(k + 1) * 128], AF.Identity, [psn[1], "snc"], ["ob" + n_], scale=snc[:, k:k + 1])
                DMA(S, "sp", scr["YST"][:, :, c * 128:(c + 1) * 128].rearrange("k p t -> p k t"), ob[s], ["ob" + n_], [])
            state_update(Hf, s, 0, "Hf")

    def load_rowG(self, l, which, ngi):
        S, A = self.S, self.A
        rowG = [A.alloc([D], F32) for _ in range(2)]
        ngb = A.alloc([D], F32)
        DMA(S, "sp", ngb, self.W["norm_g"][l][ngi:ngi + 1, :].broadcast_to([128, D]), [], ["ngb"])
        for r in range(2):
            DMA(S, "sp", rowG[r], self.scr["MOD"][r:r + 1, which * D:(which + 1) * D].broadcast_to([128, D]), [], ["rowG%d" % r])
            TT(S, "dve", rowG[r], rowG[r], ngb, ALU.mult, ["rowG%d" % r, "ngb"], ["rowG%d" % r])
        return rowG

    def resid_epilogue(self, y, yname, g, rowG, bufs, i, dst="X"):
        S = self.S
        xt, junk, ss = bufs
        s = i % 2
        r = 0 if g < self.NL else 1
        X = self.scr["X"]
        rows = slice(g * 128, (g + 1) * 128)
        DMA(S, "sp", xt[s], X[rows, :], [], ["ext%d" % s])
        ACT(S, junk, y, AF.Square, [yname], ["ejunk", "ess%d" % s], accum=ss[s])
        ACT(S, ss[s], ss[s], AF.Sqrt, ["ess%d" % s], ["ess%d" % s], scale=1.0 / D, bias=EPS)
        RECIP(S, ss[s], ss[s], ["ess%d" % s], ["ess%d" % s])
        STT(S, y, y, ss[s], rowG[r], ALU.mult, ALU.mult, [yname, "ess%d" % s, "rowG%d" % r], [yname])
        TT(S, "pool", xt[s], xt[s], y, ALU.add, ["ext%d" % s, yname], ["ext%d" % s])
        DMA(S, "sp", self.scr[dst][rows, :], xt[s], ["ext%d" % s], [])

    def phase_merge(self, l):
        S, A, ps, psn, K = self.S, self.A, self.ps, self.psn, self.K
        W, scr = self.W, self.scr
        T, NT, NL, SEQ = self.T, self.NT, self.NL, self.SEQ
        self.newphase()
        rowG = self.load_rowG(l, 2, 1)
        blocks = [list(range(t, min(NL, t + 2))) for t in range(0, NL, 2)]
        if self.need_ctx:
            blocks.append([NL, NL + 1])
        yin = [[A.alloc([8, 256], BF16) for _ in range(3)] for _ in range(2)]
        wb = [[A.alloc([8, 512], BF16) for _ in range(3)] for _ in range(2)]
        gg = [[A.alloc([4, 256], BF16) for _ in range(3)] for _ in range(2)]
        mT = A.alloc([16, 256], BF16)
        tt_ = [A.alloc([256], F32) for _ in range(3)]
        wo = [A.alloc([16, 512], BF16) for _ in range(2)]
        ysb = A.alloc([2, D], F32)
        xt = [A.alloc([D], F32) for _ in range(2)]
        junk = A.alloc([D], BF16)
        ss = [A.alloc([1], F32) for _ in range(2)]
        names = ["YDT", "YST", "YWT"]
        wn = ["w_br_diff", "w_br_ssd", "w_br_win"]
        wc = 0
        oc = 0
        ec = 0
        for bi, tiles in enumerate(blocks):
            bs = bi % 2
            tok0 = tiles[0] * 128
            ntok = len(tiles) * 128
            for br in range(3):
                DMA(S, "sp", yin[bs][br][:, :, 0:ntok], scr[names[br]][:, :, tok0:tok0 + ntok].rearrange("k p t -> p k t"),
                    [], ["yin%d%d" % (bs, br)])
            for f4 in range(4):
                ws = wc % 2
                wc += 1
                for br in range(3):
                    DMA(S, "pool", wb[ws][br], scr["B_" + wn[br]][:, f4 * 512:(f4 + 1) * 512].rearrange("(k p) c -> p k c", p=128),
                        [], ["wb%d%d" % (ws, br)])
                    DMA(S, "sp", gg[ws][br][:, :, 0:ntok],
                        scr["GT"][br * 16 + f4 * 4:br * 16 + f4 * 4 + 4, :, tok0:tok0 + ntok].rearrange("k p t -> p k t"),
                        [], ["gg%d%d" % (ws, br)])
                for m in range(4):
                    f = f4 * 4 + m
                    for br in range(3):
                        MM(S, [(ps[br][:, 0:ntok], wb[ws][br][:, k, m * 128:(m + 1) * 128], yin[bs][br][:, k, 0:ntok], k == 0, k == 7)
                               for k in range(8)], ["wb%d%d" % (ws, br), "yin%d%d" % (bs, br)], [psn[br]])
                        TT(S, "dve", tt_[br][:, 0:ntok], ps[br][:, 0:ntok], gg[ws][br][:, m, 0:ntok], ALU.mult,
                           [psn[br], "gg%d%d" % (ws, br)], ["tt%d" % br])
                    TT(S, "pool", tt_[0][:, 0:ntok], tt_[0][:, 0:ntok], tt_[1][:, 0:ntok], ALU.add, ["tt0", "tt1"], ["tt0"])
                    TT(S, "pool", mT[:, f, 0:ntok], tt_[0][:, 0:ntok], tt_[2][:, 0:ntok], ALU.add, ["tt0", "tt2"], ["mT%d" % f])
            for nch in range(4):
                os_ = oc % 2
                oc += 1
                DMA(S, "pool", wo[os_], scr["B_w_out"][:, nch * 512:(nch + 1) * 512].rearrange("(k p) c -> p k c", p=128),
                    [], ["wo%d" % os_])
                for ti in range(len(tiles)):
                    b = 3 + (ti + nch) % 4
                    MM(S, [(ps[b][:, :], mT[:, k, ti * 128:(ti + 1) * 128], wo[os_][:, k, :], k == 0, k == 15) for k in range(16)],
                       ["wo%d" % os_] + ["mT%d" % f for f in range(16)], [psn[b]])
                    CP(S, "act", ysb[:, ti, nch * 512:(nch + 1) * 512], ps[b][:, :], [psn[b]], ["ysbm%d" % ti])
            for ti, g in enumerate(tiles):
                yname = "ysbm%d" % ti
                self.resid_epilogue(ysb[:, ti, :], yname, g, rowG, (xt, junk, ss), ec)
                ec += 1

    def phase_ffn(self, l):
        S, A, ps, psn, K = self.S, self.A, self.ps, self.psn, self.K
        W, scr = self.W, self.scr
        T, NT, NL, SEQ = self.T, self.NT, self.NL, self.SEQ
        identf = K["ident_f"]
        self.newphase()
        floor0 = A.floor
        rowG = self.load_rowG(l, 5, 3)
        fcT = A.alloc([88, 4], F32)
        fc4 = A.alloc([2816], F32)
        for pc in range(4):
            DMA(S, "sp", fc4[0:3, :], W["ffn_conv_w"][l][:, pc * 2816:(pc + 1) * 2816], [], ["fc4"])
            DMA(S, "sp", fc4[3:4, :], W["ffn_conv_b"][l][:, pc * 2816:(pc + 1) * 2816], [], ["fc4"])
            TR(S, [(ps[0][:, j * 4:(j + 1) * 4], fc4[0:4, j * 128:(j + 1) * 128]) for j in range(22)], identf[0:4, 0:4],
               ["fc4"], [psn[0]])
            CP(S, "dve", fcT[:, pc * 22:(pc + 1) * 22, :].rearrange("p a b -> p (a b)"), ps[0][:, 0:88], [psn[0]], ["fcT"])
        S.barrier()
        A.keep()
        NB = 512
        blocks = [(t0, min(NB, SEQ - t0), t0 > 0, t0 + NB < SEQ) for t0 in range(0, SEQ, NB)]
        if self.need_ctx:
            blocks.append((SEQ, CTX, False, False))
        for (t0, nb, lv, rv) in blocks:
            self.newphase()
            g0 = t0 // 128
            ntl = nb // 128
            actT = A.alloc([44, nb], BF16)
            base = A.top
            h2T = A.alloc([16, nb + 256], BF16)
            jobs = [(g0 + i, 0, 128, 128 + i * 128) for i in range(ntl)]
            if lv:
                jobs.append((g0 - 1, 0, 128, 0))
            if rv:
                jobs.append((g0 + ntl, 0, 128, 128 + nb))
            top1 = A.top
            self.norm_tiles(jobs, self.colA2, self.colB2, h2T, [(0, 1), (2, 3), (4, 5), (6, 7)])
            S.barrier()
            A.top = top1
            wa = [A.alloc([16, 256], BF16) for _ in range(2)]
            wg = [A.alloc([16, 256], BF16) for _ in range(2)]
            ta = [A.alloc([256], F32) for _ in range(2)]
            tg = [A.alloc([256], F32) for _ in range(2)]
            chunks = [(c, min(256, nb - c)) for c in range(0, nb, 256)]
            cc_ = 0

            def loadup(i):
                DMA(S, "pool", wa[i % 2], scr["B_ffn_w_up"][:, i * 256:(i + 1) * 256].rearrange("(k p) c -> p k c", p=128),
                    [], ["wa%d" % (i % 2)])
                DMA(S, "pool", wg[i % 2], scr["B_ffn_w_up"][:, D_FF + i * 256:D_FF + (i + 1) * 256].rearrange("(k p) c -> p k c", p=128),
                    [], ["wg%d" % (i % 2)])
            loadup(0)
            for fb2 in range(22):
                if fb2 + 1 < 22:
                    loadup(fb2 + 1)
                ws = fb2 % 2
                for m in range(2):
                    fb = fb2 * 2 + m
                    for (c, n) in chunks:
                        q = cc_ % 2
                        cc_ += 1
                        first, last = (c == 0), (c + n == nb)
                        lo = 1 if (first and not lv) else 0
                        hi = n - 1 if (last and not rv) else n
                        for (wt_, wname, bnk, tx, txn, fidx) in ((wa[ws], "wa%d" % ws, q * 2, ta[q], "ta%d" % q, fb),
                                                               (wg[ws], "wg%d" % ws, q * 2 + 1, tg[q], "tg%d" % q, 44 + fb)):
                            MM(S, [(ps[bnk][:, 0:n + 2], wt_[:, k, m * 128:(m + 1) * 128], h2T[:, k, 127 + c:127 + c + n + 2], k == 0, k == 15)
                                   for k in range(16)], [wname], [psn[bnk]])
                            ACT(S, tx[:, 0:n], ps[bnk][:, 1:n + 1], AF.Identity, [psn[bnk]], [txn],
                                scale=fcT[:, fidx, 1:2], bias=fcT[:, fidx, 3:4])
                            STT(S, tx[:, lo:n], ps[bnk][:, lo:n], fcT[:, fidx, 0:1], tx[:, lo:n], ALU.mult, ALU.add,
                                [psn[bnk], txn], [txn])
                            STT(S, tx[:, 0:hi], ps[bnk][:, 2:2 + hi], fcT[:, fidx, 2:3], tx[:, 0:hi], ALU.mult, ALU.add,
                                [psn[bnk], txn], [txn])
                        ACT(S, ta[q][:, 0:n], ta[q][:, 0:n], AF.Silu, ["ta%d" % q], ["ta%d" % q])
                        TT(S, "pool", actT[:, fb, c:c + n], ta[q][:, 0:n], tg[q][:, 0:n], ALU.mult, ["ta%d" % q, "tg%d" % q], ["actT"])
            S.barrier()
            A.top = base
            yT = A.alloc([16, nb], F32)
            top3 = A.top
            wd = [A.alloc([44, 256], BF16) for _ in range(2)]

            def loaddn(i):
                DMA(S, "pool", wd[i % 2], scr["B_ffn_w_down"][:, i * 256:(i + 1) * 256].rearrange("(k p) c -> p k c", p=128),
                    [], ["wd%d" % (i % 2)])
            loaddn(0)
            bc_ = 0
            for fo2 in range(8):
                if fo2 + 1 < 8:
                    loaddn(fo2 + 1)
                ws = fo2 % 2
                for m in range(2):
                    fo = fo2 * 2 + m
                    b = 4 + bc_ % 4
                    bc_ += 1
                    MM(S, [(ps[b][:, 0:nb], wd[ws][:, k, m * 128:(m + 1) * 128], actT[:, k, 0:nb], k == 0, k == 43) for k in range(44)],
                       ["wd%d" % ws], [psn[b]])
                    CP(S, "act" if fo % 2 else "dve", yT[:, fo, :], ps[b][:, 0:nb], [psn[b]], ["yT%d" % fo])
            S.barrier()
            A.top = top3
            ysb = [A.alloc([D], F32) for _ in range(2)]
            xt = [A.alloc([D], F32) for _ in range(2)]
            junk = A.alloc([D], BF16)
            ss = [A.alloc([1], F32) for _ in range(2)]
            for ti in range(ntl):
                s = ti % 2
                for q4 in range(4):
                    TR(S, [(ps[q4][:, j * 128:(j + 1) * 128], yT[:, q4 * 4 + j, ti * 128:(ti + 1) * 128]) for j in range(4)], identf,
                       [], [psn[q4]])
                    CP(S, "act" if q4 % 2 else "dve", ysb[s][:, q4 * 512:(q4 + 1) * 512], ps[q4][:, :], [psn[q4]], ["fysb%d" % s])
                self.resid_epilogue(ysb[s], "fysb%d" % s, g0 + ti, rowG, (xt, junk, ss), ti, dst="X2")
        S.barrier()
        nrow = T if self.need_ctx else SEQ
        DMA(S, "sp", scr["X"][0:nrow, :], scr["X2"][0:nrow, :], [], [])
        S.barrier()
        A.floor = floor0


_CACHE = {}
SEQ_FULL = 4096
N_CORES = 4


def kernel(**inputs):
    SEQ = SEQ_FULL
    if "nc" not in _CACHE:
        _CACHE["nc"] = Builder(SEQ).build()
        _CACHE["consts"] = host_consts(SEQ)
    nc = _CACHE["nc"]
    consts = _CACHE["consts"]
    f32 = np.float32
    shared = {}
    for name, shp in W_SPECS:
        shared[name] = np.ascontiguousarray(np.asarray(inputs[name], dtype=f32).reshape([DEPTH] + shp))
    shared.update(consts)
    in_maps = []
    for b in range(N_CORES):
        m = dict(shared)
        m["x"] = np.ascontiguousarray(np.asarray(inputs["x"][b], dtype=f32))
        m["ctx"] = np.ascontiguousarray(np.asarray(inputs["ctx"][b], dtype=f32))
        m["cvec"] = np.ascontiguousarray(np.stack([np.asarray(inputs["c"][b], dtype=f32),
                                                   np.asarray(inputs["c_ctx"], dtype=f32)]))
        in_maps.append(m)
    res = run_bass_kernel_spmd(nc, in_maps, core_ids=list(range(N_CORES)))
    out = np.stack([np.asarray(res.results[b]["out"]) for b in range(N_CORES)], axis=0)
    return out.astype(f32)
```

```python
import math
import numpy as np
import ml_dtypes
import concourse.bass as bass
import concourse.mybir as mybir
from concourse.bass_utils import run_bass_kernel_spmd

F32 = mybir.dt.float32
BF16 = mybir.dt.bfloat16
AF = mybir.ActivationFunctionType
ALU = mybir.AluOpType
AX = mybir.AxisListType

D = 2048
DEPTH = 2
CTX = 256
GRID_W = 64
EPS = 1e-6
IN_SIZES = (1024, 1024, 1024, 1024, 2048, 32, 1024, 256, 256, 6144)
IN_OFF = [0]
for _s in IN_SIZES:
    IN_OFF.append(IN_OFF[-1] + _s)
IN_WIDTH = IN_OFF[-1]
D_FF = 5632
NDMA_SEM = 12


class _Op:
    __slots__ = ("eng", "fn", "deps", "sig", "sigval", "dma", "dsem", "dval", "dprev")


class Sched:
    ENGS = ("pe", "act", "dve", "pool", "sp")

    def __init__(self, nc):
        self.nc = nc
        self.ops = []
        self.last_w = {}
        self.readers = {}
        self.last_on = {e: None for e in self.ENGS}
        self.open_dma = {}
        self.dcnt = {e: 0 for e in self.ENGS}

    def op(self, eng, fn, reads=(), writes=(), dma=False):
        o = _Op()
        o.eng, o.fn, o.dma = eng, fn, dma
        o.sig = False
        psr = [r for r in reads if isinstance(r, str) and r.startswith("ps") and r[2:].isdigit()]
        if psr:
            reads = [r for r in reads if r not in psr]
            writes = list(writes) + psr
        deps = set()
        for r in reads:
            w = self.last_w.get(r)
            if w is not None:
                deps.add(w)
        for w_ in writes:
            w = self.last_w.get(w_)
            if w is not None:
                deps.add(w)
            for rd in self.readers.get(w_, ()):
                deps.add(rd)
        for r in reads:
            self.readers.setdefault(r, []).append(o)
        for w_ in writes:
            self.last_w[w_] = o
            self.readers[w_] = []
        deps.discard(o)
        o.deps = deps
        self.ops.append(o)
        if not dma:
            self.last_on[eng] = o
        if dma:
            i = self.dcnt[eng]
            self.dcnt[eng] += 1
            o.dsem = (eng, i % NDMA_SEM)
            o.dval = 16 * (i // NDMA_SEM + 1)
            o.dprev = 16 * (i // NDMA_SEM)
            self.open_dma[o.dsem] = o
        return o

    def barrier(self):
        pend = [o for o in self.last_on.values() if o is not None and not o.dma] + list(self.open_dma.values())
        for e in self.ENGS:
            o = _Op()
            o.eng, o.fn, o.dma, o.sig = e, None, False, False
            o.deps = set(pend)
            self.ops.append(o)
        self.open_dma = {}
        self.last_w = {}
        self.readers = {}

    def emit(self):
        nc = self.nc
        for o in self.ops:
            for d in o.deps:
                if not d.dma:
                    d.sig = True
        cnt = {e: 0 for e in self.ENGS}
        dcnt = self.dcnt
        for o in self.ops:
            if o.dma:
                pass
            elif o.sig:
                cnt[o.eng] += 1
                o.sigval = cnt[o.eng]
        per = {e: [o for o in self.ops if o.eng == e] for e in self.ENGS}
        import contextlib
        with contextlib.ExitStack() as st:
            esem = {e: st.enter_context(nc.semaphore("s_" + e)) for e in self.ENGS}
            dsem = {}
            for e in ("sp", "act", "pool"):
                if dcnt[e]:
                    for i in range(NDMA_SEM):
                        dsem[(e, i)] = st.enter_context(nc.semaphore("d_%s%d" % (e, i)))
            block = st.enter_context(nc.Block())

            def run(ename, eng):
                waited = {}

                def wait(sem_key, sem, val):
                    if waited.get(sem_key, 0) < val:
                        eng.wait_ge(sem, val)
                        waited[sem_key] = val

                for o in per[ename]:
                    for d in o.deps:
                        if d.dma:
                            wait(d.dsem, dsem[d.dsem], d.dval)
                        else:
                            if d.eng == ename and ename == "pe":
                                continue
                            wait(d.eng, esem[d.eng], d.sigval)
                    if o.fn is None:
                        continue
                    if o.dma and o.dprev:
                        wait(o.dsem, dsem[o.dsem], o.dprev)
                    inst = o.fn(eng)
                    if o.dma:
                        inst.then_inc(dsem[o.dsem], 16)
                    elif o.sig:
                        inst.then_inc(esem[ename], 1)

            @block.tensor
            def _(e):
                run("pe", e)

            @block.scalar
            def _(e):
                run("act", e)

            @block.vector
            def _(e):
                run("dve", e)

            @block.gpsimd
            def _(e):
                run("pool", e)

            @block.sync
            def _(e):
                run("sp", e)


ENG_OF = {"act": "act", "dve": "dve", "pool": "pool"}


def MM(S, items, r, w):
    items = list(items)

    def fn(e):
        last = None
        for (out, lhsT, rhs, st, sp) in items:
            last = e.matmul(out, lhsT=lhsT, rhs=rhs, start=st, stop=sp, skip_group_check=True)
        return last
    return S.op("pe", fn, r, w)


def TR(S, items, ident, r, w):
    items = list(items)

    def fn(e):
        last = None
        for (out, in_) in items:
            last = e.transpose(out, in_, ident)
        return last
    return S.op("pe", fn, r, w)


def ACT(S, out, in_, func, r, w, scale=None, bias=None, accum=None):
    kw = {}
    if scale is not None:
        kw["scale"] = scale
    if bias is not None:
        kw["bias"] = bias
    if accum is not None:
        kw["accum_out"] = accum
    return S.op("act", lambda e: e.activation(out=out, in_=in_, func=func, **kw), r, w)


def TS(S, eng, out, in0, s1, s2, op0, op1, r, w):
    if op1 is None:
        return S.op(eng, lambda e: e.tensor_scalar(out, in0, s1, None, op0), r, w)
    return S.op(eng, lambda e: e.tensor_scalar(out, in0, s1, s2, op0, op1), r, w)


def TT(S, eng, out, in0, in1, op, r, w):
    return S.op(eng, lambda e: e.tensor_tensor(out, in0, in1, op), r, w)


def STT(S, out, in0, scalar, in1, op0, op1, r, w):
    return S.op("dve", lambda e: e.scalar_tensor_tensor(out, in0, scalar, in1, op0, op1), r, w)


def STTA(S, out, in0, scalar, in1, op0, op1, accum, r, w):
    return S.op("dve", lambda e: e.scalar_tensor_tensor(out, in0, scalar, in1, op0, op1, accum_out=accum), r, w)


def CP(S, eng, out, in_, r, w):
    if eng == "act":
        return S.op("act", lambda e: e.copy(out, in_), r, w)
    return S.op(eng, lambda e: e.tensor_copy(out, in_), r, w)


def MSET(S, eng, ap, val, r, w):
    return S.op(eng, lambda e: e.memset(ap, val), r, w)


def RECIP(S, out, in_, r, w):
    return S.op("dve", lambda e: e.reciprocal(out, in_), r, w)


def DMA(S, q, out, in_, r, w):
    return S.op(q, lambda e: e.dma_start(out=out, in_=in_), r, w, dma=True)


class Arena:
    def __init__(self, ap, nwords):
        self.ap = ap
        self.n = nwords
        self.top = 0
        self.floor = 0
        self.uid = 0

    def alloc(self, shape, dtype, name=None):
        nelem = 1
        for s in shape:
            nelem *= s
        nbytes = nelem * (2 if dtype == BF16 else 4)
        nw = (nbytes + 31) // 32 * 8
        assert self.top + nw <= self.n, "arena overflow %s %d+%d>%d" % (name, self.top, nw, self.n)
        a = self.ap[:, self.top:self.top + nw]
        self.top += nw
        if dtype == BF16:
            a = a.bitcast(BF16)
        a = a[:, 0:nelem]
        if len(shape) == 2:
            a = a.rearrange("p (a b) -> p a b", a=shape[0])
        elif len(shape) == 3:
            a = a.rearrange("p (a b c) -> p a b c", a=shape[0], b=shape[1])
        self.uid += 1
        return a

    def keep(self):
        self.floor = self.top

    def reset(self):
        self.top = self.floor


def host_consts(SEQ):
    T = SEQ + CTX
    bf = ml_dtypes.bfloat16
    c = {}
    c["ident_bf"] = np.eye(128, dtype=np.float32).astype(bf)
    c["ident_f"] = np.eye(128, dtype=np.float32)
    pm = np.zeros((128, 128), np.float32)
    for base in (0, 64):
        for i in range(32):
            pm[base + i + 32, base + i] = -1.0
            pm[base + i, base + i + 32] = 1.0
    c["perm_bf"] = pm.astype(bf)
    s_ = np.arange(128)[:, None]
    l_ = np.arange(128)[None, :]
    c["tri_f"] = (s_ <= l_).astype(np.float32)
    c["tri_b"] = (s_ >= l_).astype(np.float32)
    c["ones_f"] = np.ones((128, 128), np.float32)
    c["neg_f"] = np.where(l_ >= s_, 0.0, -30000.0).astype(np.float32)
    c["neg_b"] = np.where(l_ <= s_, 0.0, -30000.0).astype(np.float32)
    c["wm_prev"] = (s_ >= l_).astype(np.float32).astype(bf)
    c["wm_next"] = (s_ <= l_).astype(np.float32).astype(bf)
    t = np.arange(SEQ)
    row = (t // GRID_W).astype(np.float32)
    col = (t % GRID_W).astype(np.float32)
    inv = (10000.0 ** (-np.arange(0, 32, 2, dtype=np.float32) / 32.0)).astype(np.float32)
    ang = np.concatenate([row[:, None] * inv, col[:, None] * inv], axis=-1).astype(np.float32)
    cosT = np.ones((128, T), np.float32)
    sinT = np.zeros((128, T), np.float32)
    for p in range(128):
        cosT[p, :SEQ] = np.cos(ang[:, p % 32])
        sinT[p, :SEQ] = np.sin(ang[:, p % 32])
    c["cosT"] = cosT
    c["sinT"] = sinT
    return c


CONST_SPECS = [("ident_bf", BF16, None), ("ident_f", F32, None), ("perm_bf", BF16, None),
               ("tri_f", F32, None), ("tri_b", F32, None), ("ones_f", F32, None),
               ("neg_f", F32, None), ("neg_b", F32, None), ("wm_prev", BF16, None), ("wm_next", BF16, None)]

W_SPECS = [("w_ada", [D, 6 * D]), ("b_ada", [1, 6 * D]), ("norm_g", [4, D]), ("w_in", [D, IN_WIDTH]),
           ("diff_lambda", [1, 256]), ("diff_norm", [1, 128]), ("ssd_conv_w", [5, 2048]),
           ("ssd_conv_b", [1, 2048]), ("ssd_a_log", [1, 32]), ("ssd_dt_bias", [1, 32]),
           ("ssd_d", [1, 16]), ("ssd_norm", [1, 1024]), ("win_sink", [1, 16]),
           ("w_br_diff", [1024, D]), ("w_br_ssd", [1024, D]), ("w_br_win", [1024, D]),
           ("w_out", [D, D]), ("ffn_w_up", [D, 2 * D_FF]), ("ffn_conv_w", [3, 2 * D_FF]),
           ("ffn_conv_b", [1, 2 * D_FF]), ("ffn_w_down", [D_FF, D])]


class Builder:
    def __init__(self, SEQ, debug=False, nlayers=DEPTH, stop_after=None):
        self.SEQ = SEQ
        self.T = SEQ + CTX
        self.NL = SEQ // 128
        self.NT = self.T // 128
        self.debug = debug
        self.nlayers = nlayers
        self.stop_after = stop_after
        nc = bass.Bass("TRN2", target_bir_lowering=False)
        self.nc = nc
        self.S = Sched(nc)
        T = self.T
        dt = nc.dram_tensor
        self.x_in = dt("x", [SEQ, D], F32, kind="ExternalInput").ap()
        self.ctx_in = dt("ctx", [CTX, D], F32, kind="ExternalInput").ap()
        self.cvec = dt("cvec", [2, D], F32, kind="ExternalInput").ap()
        wspec = dict(W_SPECS)

        class _Lazy(dict):
            def __missing__(d_, name):
                d_[name] = dt(name, [DEPTH] + wspec[name], F32, kind="ExternalInput").ap()
                return d_[name]
        self.W = _Lazy()
        self.C = {}
        for name, dty, _ in CONST_SPECS:
            self.C[name] = dt(name, [128, 128], dty, kind="ExternalInput").ap()
        self.C["cosT"] = dt("cosT", [128, T], F32, kind="ExternalInput").ap()
        self.C["sinT"] = dt("sinT", [128, T], F32, kind="ExternalInput").ap()
        self.out = dt("out", [SEQ, D], F32, kind="ExternalOutput").ap()
        sk = "ExternalOutput" if debug else "Internal"
        self.scr = {}
        for name, shp, dty in [
            ("X", [T, D], F32), ("X2", [T, D], F32), ("MOD", [2, 6 * D], F32),
            ("QDT", [8, 128, T], BF16), ("KDT", [8, 128, T], BF16), ("VD", [T, 1024], BF16),
            ("ZS", [T, 1024], BF16), ("XBCT", [16, 128, T], BF16), ("DT", [T, 32], F32),
            ("WQT", [16, 64, T], BF16), ("WKT", [4, 64, T], BF16), ("WV", [T, 256], BF16),
            ("GT", [48, 128, T], BF16),
            ("UT", [16, 128, T], BF16), ("XSB", [T, 1536], BF16), ("HB", [self.NT, 128, 1024], BF16),
            ("YDT", [8, 128, T], BF16), ("YST", [8, 128, T], BF16), ("YWT", [8, 128, T], BF16),
            ("B_w_br_diff", [1024, D], BF16), ("B_w_br_ssd", [1024, D], BF16), ("B_w_br_win", [1024, D], BF16),
            ("B_w_out", [D, D], BF16), ("B_ffn_w_up", [D, 2 * D_FF], BF16), ("B_ffn_w_down", [D_FF, D], BF16),
        ]:
            self.scr[name] = dt("s_" + name, shp, dty, kind=sk).ap()

    def build(self):
        nc, S = self.nc, self.S
        import contextlib
        with contextlib.ExitStack() as st:
            NW = 50000
            arena = st.enter_context(nc.sbuf_tensor("arena", [128, NW], F32))
            self.A = Arena(arena[:, :], NW)
            self.ps = [st.enter_context(nc.psum_tensor("psb%d" % i, [128, 512], F32)) for i in range(8)]
            self.psn = ["ps%d" % i for i in range(8)]
            self.setup_consts()
            for l in range(self.nlayers):
                self.layer(l)
                if self.stop_after is not None and l == self.stop_after[0]:
                    break
            self.finish()
            S.barrier()
            S.emit()
        return nc

    def newphase(self):
        self.S.barrier()
        self.A.reset()

    def setup_consts(self):
        S, A = self.S, self.A
        self.K = {}
        for name, dty, _ in CONST_SPECS:
            t = A.alloc([128], dty)
            DMA(S, "sp", t, self.C[name][:, :], [], ["c_" + name])
            self.K[name] = t
        self.modT = [A.alloc([96], F32), A.alloc([96], F32)]
        self.ngT = A.alloc([64], F32)
        self.colA1 = [A.alloc([16], F32) for _ in range(2)]
        self.colA2 = [A.alloc([16], F32) for _ in range(2)]
        A.keep()
        X = self.scr["X"]
        DMA(S, "sp", X[0:self.SEQ, :], self.x_in[:, :], [], [])
        DMA(S, "sp", X[self.SEQ:self.T, :], self.ctx_in[:, :], [], [])
        S.barrier()

    def finish(self):
        S = self.S
        S.barrier()
        DMA(S, "sp", self.out[:, :], self.scr["X"][0:self.SEQ, :], [], [])

    def layer(self, l):
        self.need_ctx = l < DEPTH - 1
        self.lam_init = 0.8 - 0.6 * math.exp(-0.3 * l)
        sa = self.stop_after[1] if (self.stop_after is not None and self.stop_after[0] == l) else None
        if sa == "setup":
            return
        self.phase_mod(l)
        if sa == "mod":
            return
        self.phase_inproj(l)
        if sa == "inproj":
            return
        self.phase_diff(l)
        if sa == "diff":
            return
        self.phase_win(l)
        if sa == "win":
            return
        self.phase_ssd(l)
        if sa == "ssd":
            return
        self.phase_merge(l)
        if sa == "merge":
            return
        self.phase_ffn(l)

    def phase_mod(self, l):
        S, A, ps, psn, K = self.S, self.A, self.ps, self.psn, self.K
        self.newphase()
        W = self.W
        identf = K["ident_f"]
        cc = A.alloc([D], F32)
        DMA(S, "sp", cc[0:2, :], self.cvec[:, :], [], ["cc"])
        ACT(S, cc[0:2, :], cc[0:2, :], AF.Silu, ["cc"], ["cc"])
        TR(S, [(ps[0][:, 2 * k:2 * k + 2], cc[0:2, k * 128:(k + 1) * 128]) for k in range(16)],
           identf[0:2, 0:2], ["cc", "c_ident_f"], [psn[0]])
        import os
        stage = int(os.environ.get("MODSTAGE", "99"))
        if stage < 1:
            return
        condT = A.alloc([32], F32)
        CP(S, "dve", condT, ps[0][:, 0:32], [psn[0]], ["condT"])
        if stage < 2:
            return
        modrow = A.alloc([6 * D], F32)
        bb = A.alloc([6 * D], F32)
        DMA(S, "sp", bb[0:2, :], W["b_ada"][l][0:1, :].broadcast_to([2, 6 * D]), [], ["bb"])
        wA = [A.alloc([16, 512], F32) for _ in range(2)]
        nchunk = 6 * D // 512

        def load(j):
            DMA(S, "sp", wA[j % 2], W["w_ada"][l][:, j * 512:(j + 1) * 512].rearrange("(k p) c -> p k c", p=128),
                [], ["wA%d" % (j % 2)])
        load(0)
        for j in range(nchunk):
            if j + 1 < nchunk:
                load(j + 1)
            b = 1 + j % 2
            MM(S, [(ps[b][0:2, :], condT[:, 2 * k:2 * k + 2], wA[j % 2][:, k, :], k == 0, k == 15) for k in range(16)],
               ["condT", "wA%d" % (j % 2)], [psn[b]])
            TT(S, "dve", modrow[0:2, j * 512:(j + 1) * 512], ps[b][0:2, :], bb[0:2, j * 512:(j + 1) * 512], ALU.add,
               [psn[b], "bb"], ["modrow"])
        DMA(S, "sp", self.scr["MOD"][:, :], modrow[0:2, :], ["modrow"], [])
        if stage < 3:
            return
        self.newphase()
        MOD = self.scr["MOD"]
        for r in range(2):
            m96 = A.alloc([128], F32)
            DMA(S, "sp", m96[0:96, :], MOD[r, :].rearrange("(c p) -> c p", p=128), [], ["m96_%d" % r])
            TR(S, [(ps[r][:, 0:96], m96[0:96, :])], identf[0:96, 0:96], ["m96_%d" % r, "c_ident_f"], [psn[r]])
            CP(S, "dve", self.modT[r], ps[r][:, 0:96], [psn[r]], ["modT%d" % r])
        n64 = A.alloc([128], F32)
        DMA(S, "sp", n64[0:64, :], W["norm_g"][l].rearrange("i (k p) -> (i k) p", p=128), [], ["n64"])
        TR(S, [(ps[2][:, 0:64], n64[0:64, :])], identf[0:64, 0:64], ["n64", "c_ident_f"], [psn[2]])
        CP(S, "dve", self.ngT, ps[2][:, 0:64], [psn[2]], ["ngT"])
        for r in range(2):
            STT(S, self.colA1[r], self.modT[r][:, 16:32], 1.0, self.ngT[:, 0:16], ALU.add, ALU.mult,
                ["modT%d" % r, "ngT"], ["colA1_%d" % r])
            STT(S, self.colA2[r], self.modT[r][:, 64:80], 1.0, self.ngT[:, 32:48], ALU.add, ALU.mult,
                ["modT%d" % r, "ngT"], ["colA2_%d" % r])
        self.colB1 = [self.modT[r][:, 0:16] for r in range(2)]
        self.colB2 = [self.modT[r][:, 48:64] for r in range(2)]

    def norm_tiles(self, jobs, colA, colB, hT, banks):
        S, A, ps, psn, K = self.S, self.A, self.ps, self.psn, self.K
        X = self.scr["X"]
        xt = [A.alloc([D], F32) for _ in range(2)]
        xn = [A.alloc([D], BF16) for _ in range(2)]
        junk = A.alloc([D], BF16)
        ss = [A.alloc([1], F32) for _ in range(2)]
        rs = [A.alloc([1], F32) for _ in range(2)]
        names = []
        for i, (g, lo, n, dst) in enumerate(jobs):
            s = i % 2
            r = 0 if g < self.NL else 1
            DMA(S, "sp", xt[s], X[g * 128:(g + 1) * 128, :], [], ["xt%d" % s])
            ACT(S, junk, xt[s], AF.Square, ["xt%d" % s], ["junk", "ss%d" % s], accum=ss[s])
            ACT(S, rs[s], ss[s], AF.Sqrt, ["ss%d" % s], ["rs%d" % s], scale=1.0 / D, bias=EPS)
            RECIP(S, rs[s], rs[s], ["rs%d" % s], ["rs%d" % s])
            TS(S, "dve", xn[s], xt[s], rs[s], None, ALU.mult, None, ["xt%d" % s, "rs%d" % s], ["xn%d" % s])
            bp = banks[i % len(banks)]
            for half in range(2):
                b = bp[half]
                pb = ps[b][:].bitcast(BF16)
                TR(S, [(pb[:, kk * 128:(kk + 1) * 128], xn[s][:, (half * 8 + kk) * 128:(half * 8 + kk + 1) * 128])
                       for kk in range(8)], K["ident_bf"], ["xn%d" % s, "c_ident_bf"], [psn[b]])
            nm = "hT_%d" % i
            for half in range(2):
                b = bp[half]
                pb = ps[b][:].bitcast(BF16)
                for kk in range(8):
                    k = half * 8 + kk
                    src = pb[:, kk * 128 + lo:kk * 128 + lo + n]
                    if half == 0:
                        ACT(S, hT[:, k, dst:dst + n], src, AF.Identity, [psn[b]], [nm + "_%d" % k],
                            scale=colA[r][:, k:k + 1], bias=colB[r][:, k:k + 1])
                    else:
                        TS(S, "dve", hT[:, k, dst:dst + n], src, colA[r][:, k:k + 1], colB[r][:, k:k + 1],
                           ALU.mult, ALU.add, [psn[b]], [nm + "_%d" % k])
            names.append([nm + "_%d" % k for k in range(16)])
        return names

    def phase_inproj(self, l):
        S, A, ps, psn, K = self.S, self.A, self.ps, self.psn, self.K
        W = self.W
        scr = self.scr
        NT = self.NT
        nblk = (NT + 16) // 17
        per = (NT + nblk - 1) // nblk
        blocks = [list(range(b * per, min(NT, (b + 1) * per))) for b in range(nblk)]
        for tiles in blocks:
            self.newphase()
            ntb = len(tiles)
            ntok = ntb * 128
            tok0 = tiles[0] * 128
            hT = A.alloc([16, ntok], BF16)
            hnames = self.norm_tiles([(g, 0, 128, i * 128) for i, g in enumerate(tiles)],
                                     self.colA1, self.colB1, hT, [(0, 1), (2, 3), (4, 5), (6, 7)])
            self.newphase_keep(hT)
            cs = A.alloc([ntok], F32)
            sn = A.alloc([ntok], F32)
            DMA(S, "sp", cs, self.C["cosT"][:, tok0:tok0 + ntok], [], ["cs"])
            DMA(S, "sp", sn, self.C["sinT"][:, tok0:tok0 + ntok], [], ["sn"])
            dtb = A.alloc([32], F32)
            DMA(S, "sp", dtb, W["ssd_dt_bias"][l][0:1, :].broadcast_to([128, 32]), [], ["dtb"])
            wt = [A.alloc([16, 512], BF16) for _ in range(2)]
            qsb = [A.alloc([512], BF16) for _ in range(2)]
            t1 = [A.alloc([512], F32) for _ in range(2)]
            t2 = [A.alloc([512], F32) for _ in range(2)]
            obf = [A.alloc([ntok], BF16) for _ in range(2)]
            obt = [A.alloc([512], BF16) for _ in range(3)]
            dts = [A.alloc([32], F32) for _ in range(2)]
            cblocks = []
            import os
            gsel = [int(v) for v in os.environ.get("INPROJ_GROUPS", "0,1,2,3,4,5,6,7,8,9").split(",")]
            for gi in gsel:
                for c0 in range(0, IN_SIZES[gi], 512):
                    cblocks.append((gi, c0, min(512, IN_SIZES[gi] - c0)))
            chunks = [(c, min(512, ntok - c)) for c in range(0, ntok, 512)]
            st = {"bank": 0, "fm": 0, "tm": 0, "rp": 0}

            def loadw(i):
                gi, c0, nco = cblocks[i]
                a = IN_OFF[gi] + c0
                DMA(S, "pool", wt[i % 2][:, :, 0:nco],
                    W["w_in"][l][:, a:a + nco].rearrange("(k p) c -> p k c", p=128), [], ["wt%d" % (i % 2)])

            def hres(c, n):
                out = []
                for ti in range(c // 128, (c + n) // 128):
                    out += hnames[ti]
                return out

            loadw(0)
            for i, (gi, c0, nco) in enumerate(cblocks):
                if i + 1 < len(cblocks):
                    loadw(i + 1)
                ws = i % 2
                wn = "wt%d" % ws
                fm = gi in (0, 1, 4, 6, 7, 9)
                if fm:
                    for m in range(nco // 128):
                        fs = st["fm"] % 2
                        st["fm"] += 1
                        fb = (c0 + m * 128) // 128
                        chunk_names = []
                        for ci, (c, n) in enumerate(chunks):
                            b = st["bank"] % 4
                            st["bank"] += 1
                            MM(S, [(ps[b][:, 0:n], wt[ws][:, k, m * 128:(m + 1) * 128], hT[:, k, c:c + n], k == 0, k == 15)
                                   for k in range(16)], [wn], [psn[b]])
                            on = "obf%d_%d" % (fs, ci)
                            chunk_names.append(on)
                            if gi in (0, 1, 6, 7):
                                rs_ = st["rp"] % 2
                                st["rp"] += 1
                                pb = 4 + rs_
                                CP(S, "act", qsb[rs_][:, 0:n], ps[b][:, 0:n], [psn[b]], ["qsb%d" % rs_])
                                MM(S, [(ps[pb][:, 0:n], K["perm_bf"], qsb[rs_][:, 0:n], True, True)],
                                   ["qsb%d" % rs_], [psn[pb]])
                                TT(S, "dve", t1[rs_][:, 0:n], ps[b][:, 0:n], cs[:, c:c + n], ALU.mult,
                                   [psn[b], "cs"], ["t1_%d" % rs_])
                                TT(S, "dve", t2[rs_][:, 0:n], ps[pb][:, 0:n], sn[:, c:c + n], ALU.mult,
                                   [psn[pb], "sn"], ["t2_%d" % rs_])
                                TT(S, "pool", obf[fs][:, c:c + n], t1[rs_][:, 0:n], t2[rs_][:, 0:n], ALU.add,
                                   ["t1_%d" % rs_, "t2_%d" % rs_], [on])
                            elif gi == 4:
                                CP(S, "act", obf[fs][:, c:c + n], ps[b][:, 0:n], [psn[b]], [on])
                            else:
                                ACT(S, obf[fs][:, c:c + n], ps[b][:, 0:n], AF.Sigmoid, [psn[b]], [on])
                        if gi == 0:
                            dst = scr["QDT"][fb][:, tok0:tok0 + ntok]
                        elif gi == 1:
                            dst = scr["KDT"][fb][:, tok0:tok0 + ntok]
                        elif gi == 4:
                            dst = scr["XBCT"][fb][:, tok0:tok0 + ntok]
                        elif gi == 6:
                            dst = scr["WQT"].rearrange("(a two) d t -> a (two d) t", two=2)[fb][:, tok0:tok0 + ntok]
                        elif gi == 7:
                            dst = scr["WKT"].rearrange("(a two) d t -> a (two d) t", two=2)[fb][:, tok0:tok0 + ntok]
                        else:
                            dst = scr["GT"][fb][:, tok0:tok0 + ntok]
                        DMA(S, "sp", dst, obf[fs], chunk_names, [])
                else:
                    for ti, g in enumerate(tiles):
                        b = st["bank"] % 4
                        st["bank"] += 1
                        MM(S, [(ps[b][:, 0:nco], hT[:, k, ti * 128:(ti + 1) * 128], wt[ws][:, k, 0:nco], k == 0, k == 15)
                               for k in range(16)], [wn], [psn[b]])
                        rows = slice(g * 128, (g + 1) * 128)
                        if gi == 5:
                            d_ = st["tm"] % 2
                            st["tm"] += 1
                            TT(S, "dve", dts[d_], ps[b][:, 0:32], dtb, ALU.add, [psn[b], "dtb"], ["dts%d" % d_])
                            ACT(S, dts[d_], dts[d_], AF.Exp, ["dts%d" % d_], ["dts%d" % d_])
                            ACT(S, dts[d_], dts[d_], AF.Ln, ["dts%d" % d_], ["dts%d" % d_], bias=1.0)
                            DMA(S, "sp", scr["DT"][rows, :], dts[d_], ["dts%d" % d_], [])
                            continue
                        o_ = st["tm"] % 3
                        st["tm"] += 1
                        on = "obt%d" % o_
                        if gi == 3:
                            ACT(S, obt[o_][:, 0:nco], ps[b][:, 0:nco], AF.Silu, [psn[b]], [on])
                            dst = scr["ZS"][rows, c0:c0 + nco]
                        elif gi == 2:
                            CP(S, "act", obt[o_][:, 0:nco], ps[b][:, 0:nco], [psn[b]], [on])
                            dst = scr["VD"][rows, c0:c0 + nco]
                        else:
                            CP(S, "act", obt[o_][:, 0:nco], ps[b][:, 0:nco], [psn[b]], [on])
                            dst = scr["WV"][rows, c0:c0 + nco]
                        DMA(S, "sp", dst, obt[o_][:, 0:nco], [on], [])

    def newphase_keep(self, *_):
        self.S.barrier()

    def phase_diff(self, l):
        S, A, ps, psn, K = self.S, self.A, self.ps, self.psn, self.K
        W, scr = self.W, self.scr
        T, NT, NL, SEQ = self.T, self.NT, self.NL, self.SEQ
        self.newphase()
        dl = A.alloc([256], F32)
        DMA(S, "sp", dl, W["diff_lambda"][l][0:1, :].broadcast_to([128, 256]), [], ["dl"])
        jk = A.alloc([128], F32)
        s12 = A.alloc([2], F32)
        TT(S, "dve", jk[:, 0:64], dl[:, 0:64], dl[:, 64:128], ALU.mult, ["dl"], ["jk"])
        ACT(S, jk[:, 0:64], jk[:, 0:64], AF.Identity, ["jk"], ["jk", "s12a"], accum=s12[:, 0:1])
        TT(S, "dve", jk[:, 64:128], dl[:, 128:192], dl[:, 192:256], ALU.mult, ["dl"], ["jk2"])
        ACT(S, jk[:, 64:128], jk[:, 64:128], AF.Identity, ["jk2"], ["jk2", "s12b"], accum=s12[:, 1:2])
        ACT(S, s12, s12, AF.Exp, ["s12a", "s12b"], ["s12"])
        nlam = A.alloc([1], F32)
        TT(S, "dve", nlam, s12[:, 0:1], s12[:, 1:2], ALU.subtract, ["s12"], ["nlam"])
        TS(S, "dve", nlam, nlam, float(self.lam_init), -1.0, ALU.add, ALU.mult, ["nlam"], ["nlam"])
        dn = A.alloc([1], F32)
        DMA(S, "sp", dn, W["diff_norm"][l].rearrange("o p -> p o"), [], ["dn"])
        TS(S, "dve", dn, dn, float(1.0 - self.lam_init), None, ALU.mult, None, ["dn"], ["dn"])
        KT = [A.alloc([T], BF16) for _ in range(2)]
        V = [A.alloc([NT, 129], BF16) for _ in range(2)]
        for s in range(2):
            MSET(S, "pool", V[s][:, :, 128:129], 1.0, [], ["V%d" % s])
        QT = [A.alloc([512], BF16) for _ in range(2)]
        PT = [[A.alloc([512], BF16) for _ in range(2)] for _ in range(2)]
        rr = A.alloc([2], F32)
        o = A.alloc([128], F32)
        sq = A.alloc([128], BF16)
        ssq = A.alloc([1], F32)
        on = A.alloc([512], BF16)
        ob = [A.alloc([512], BF16) for _ in range(2)]

        def acc(j, qi):
            a = j * 4 + qi
            return ps[4 + a // 3][:, (a % 3) * 129:(a % 3) * 129 + 129], psn[4 + a // 3]

        cv = [A.alloc([11264], BF16) for _ in range(2)]
        cjobs = []
        for nm in ("w_br_diff", "w_br_ssd", "w_br_win"):
            cjobs += [(nm, 8, c0, 512) for c0 in range(0, D, 512)]
        cjobs += [("w_out", 16, c0, 512) for c0 in range(0, D, 512)]
        cjobs += [("ffn_w_up", 16, c0, 512) for c0 in range(0, 2 * D_FF, 512)]
        cjobs += [("ffn_w_down", 44, c0, 256) for c0 in range(0, D, 256)]
        cstate = {"i": 0}

        def convert_some(n):
            for _ in range(n):
                i = cstate["i"]
                if i >= len(cjobs):
                    return
                cstate["i"] += 1
                nm, kc, c0, ncol = cjobs[i]
                sl = i % 2
                tile_ = cv[sl][:, 0:kc * ncol].rearrange("p (k c) -> p k c", k=kc)
                DMA(S, "pool", tile_, W[nm][l][:, c0:c0 + ncol].rearrange("(k p) c -> p k c", p=128), [], ["cv%d" % sl])
                DMA(S, "sp", scr["B_" + nm][:, c0:c0 + ncol].rearrange("(k p) c -> p k c", p=128), tile_, ["cv%d" % sl], [])

        accS = [A.alloc([1536], F32) for _ in range(2)]
        on2 = [A.alloc([512], BF16) for _ in range(2)]
        sqj = A.alloc([128], F32)
        mhalf = A.alloc([1], F32)
        MSET(S, "pool", mhalf, -0.5, [], ["mhalf"])
        pend = []

        def accs(qs_, j, qi):
            a = j * 4 + qi
            o0 = (a // 3) * 512 + (a % 3) * 129
            return accS[qs_][:, o0:o0 + 129]

        def epi_a(job):
            qs_, h_, q0_, nq_, nqt_ = job
            an = "accS%d" % qs_
            for qi in range(nqt_):
                a0 = accs(qs_, 0, qi)
                a1 = accs(qs_, 1, qi)
                RECIP(S, rr[:, 0:1], a0[:, 128:129], [an], ["rr0"])
                RECIP(S, rr[:, 1:2], a1[:, 128:129], [an], ["rr1"])
                TT(S, "dve", rr[:, 1:2], rr[:, 1:2], nlam, ALU.mult, ["rr1", "nlam"], ["rr1"])
                TS(S, "dve", o, a0[:, 0:128], rr[:, 0:1], None, ALU.mult, None, [an, "rr0"], ["o"])
                STT(S, o, a1[:, 0:128], rr[:, 1:2], o, ALU.mult, ALU.add, [an, "rr1", "o"], ["o"])
                STTA(S, sqj, o, 1.0, o, ALU.mult, ALU.mult, ssq, ["o"], ["sqj", "ssq"])
                TS(S, "dve", ssq, ssq, 1.0 / 128, EPS, ALU.mult, ALU.add, ["ssq"], ["ssq"])
                TT(S, "pool", ssq, ssq, mhalf, ALU.pow, ["ssq", "mhalf"], ["ssq"])
                TS(S, "dve", on2[qs_][:, qi * 128:(qi + 1) * 128], o, ssq, None, ALU.mult, None, ["o", "ssq"], ["on%d_%d" % (qs_, qi)])

        def epi_b(job):
            qs_, h_, q0_, nq_, nqt_ = job
            pb = ps[7][:].bitcast(BF16)
            TR(S, [(pb[:, qi * 128:(qi + 1) * 128], on2[qs_][:, qi * 128:(qi + 1) * 128]) for qi in range(nqt_)],
               K["ident_bf"], ["on%d_%d" % (qs_, qi) for qi in range(nqt_)], [psn[7]])
            TS(S, "dve", ob[qs_][:, 0:nq_], pb[:, 0:nq_], dn[:, 0:1], None, ALU.mult, None, [psn[7], "dn"], ["ob%d" % qs_])
            DMA(S, "sp", scr["YDT"][h_][:, q0_:q0_ + nq_], ob[qs_][:, 0:nq_], ["ob%d" % qs_], [])

        qc = 0
        for h in range(8):
            hs = h % 2
            DMA(S, "sp", KT[hs], scr["KDT"][h][:, :], [], ["KT%d" % hs])
            DMA(S, "sp", V[hs][:, :, 0:128], scr["VD"][:, h * 128:(h + 1) * 128].rearrange("(t p) c -> p t c", p=128),
                [], ["V%d" % hs])
            qchunks = [(q0, min(512, SEQ - q0), list(range(NT))) for q0 in range(0, SEQ, 512)]
            if self.need_ctx:
                qchunks.append((SEQ, CTX, [NL, NL + 1]))
            for (q0, nq, kts) in qchunks:
                qs = qc % 2
                qc += 1
                nqt = nq // 128
                DMA(S, "sp", QT[qs][:, 0:nq], scr["QDT"][h][:, q0:q0 + nq], [], ["QT%d" % qs])
                convert_some(1)
                for b in (4, 5, 6):
                    MSET(S, "dve", ps[b][:, :], 0.0, [], [psn[b]])
                def emit_s(ki):
                    kt = kts[ki]
                    sb = ki % 2
                    for j in range(2):
                        b = sb * 2 + j
                        MM(S, [(ps[b][:, 0:nq], KT[hs][j * 64:(j + 1) * 64, kt * 128:(kt + 1) * 128],
                                QT[qs][j * 64:(j + 1) * 64, 0:nq], True, True)],
                           ["KT%d" % hs, "QT%d" % qs], [psn[b]])
                        ACT(S, PT[sb][j][:, 0:nq], ps[b][:, 0:nq], AF.Exp, [psn[b]], ["PT%d%d" % (sb, j)], scale=0.125)

                def emit_pv(ki):
                    kt = kts[ki]
                    sb = ki % 2
                    for j in range(2):
                        items = []
                        wb = set()
                        for qi in range(nqt):
                            ap_, bn = acc(j, qi)
                            wb.add(bn)
                            items.append((ap_, PT[sb][j][:, qi * 128:(qi + 1) * 128], V[hs][:, kt, :], False, False))
                        MM(S, items, ["PT%d%d" % (sb, j), "V%d" % hs], sorted(wb))

                emit_s(0)
                nk = len(kts)
                for ki in range(nk):
                    if ki + 1 < nk:
                        emit_s(ki + 1)
                    emit_pv(ki)
                    if pend and ki == min(1, nk - 1):
                        epi_a(pend[0])
                    if pend and ki == min(20, nk - 1):
                        epi_b(pend.pop(0))
                for b in range(3):
                    CP(S, "dve", accS[qs][:, b * 512:(b + 1) * 512], ps[4 + b][:, :], [psn[4 + b]], ["accS%d" % qs])
                pend.append((qs, h, q0, nq, nqt))
        while pend:
            epi_a(pend[0])
            epi_b(pend.pop(0))
        convert_some(len(cjobs))

    def phase_win(self, l):
        S, A, ps, psn, K = self.S, self.A, self.ps, self.psn, self.K
        W, scr = self.W, self.scr
        T, NT, NL, SEQ = self.T, self.NT, self.NL, self.SEQ
        self.newphase()
        KT = A.alloc([4, T], BF16)
        DMA(S, "sp", KT[0:64], scr["WKT"].rearrange("k d t -> d k t"), [], ["KT"])
        V = A.alloc([NT, 4, 65], BF16)
        MSET(S, "pool", V[:, :, :, 64:65], 1.0, [], ["V"])
        for kvh in range(4):
            DMA(S, "sp", V[:, :, kvh, 0:64], scr["WV"][:, kvh * 64:(kvh + 1) * 64].rearrange("(t p) d -> p t d", p=128),
                [], ["V"])
        es = A.alloc([16], F32)
        DMA(S, "sp", es, W["win_sink"][l][0:1, :].broadcast_to([128, 16]), [], ["es"])
        ACT(S, es, es, AF.Exp, ["es"], ["es"])
        QTb = [A.alloc([16, 128], BF16) for _ in range(2)]
        PTw = [A.alloc([512], BF16) for _ in range(3)]
        tmp = [A.alloc([512], BF16) for _ in range(2)]
        dd = A.alloc([4], F32)
        yw = [A.alloc([1024], BF16) for _ in range(2)]
        ob = [A.alloc([8, 128], BF16) for _ in range(2)]
        qbs = list(range(NL)) + ([NL, NL + 1] if self.need_ctx else [])
        cnt = {"s": 0, "p": 0, "t": 0}
        for qi_, qb in enumerate(qbs):
            s = qi_ % 2
            if qb < NL:
                keys = ([(qb - 1, "prev")] if qb > 0 else []) + [(qb, "c")] + \
                       ([(qb + 1, "next")] if qb < NL - 1 else []) + [(NL, "c"), (NL + 1, "c")]
            else:
                keys = [(NL, "c"), (NL + 1, "c")]
            DMA(S, "sp", QTb[s][0:64], scr["WQT"][:, :, qb * 128:(qb + 1) * 128].rearrange("h d t -> d h t"),
                [], ["QTb%d" % s])
            for kvh in range(4):
                ab = 4 + kvh % 2
                MSET(S, "dve", ps[ab][:, 0:260], 0.0, [], [psn[ab]])
                pend = []

                def flush():
                    for (p__, kt__) in pend:
                        MM(S, [(ps[ab][:, g * 65:(g + 1) * 65], PTw[p__][:, g * 128:(g + 1) * 128], V[:, kt__, kvh, :], False, False)
                               for g in range(4)], ["PTw%d" % p__, "V"], [psn[ab]])
                    del pend[:]

                for (kt, kind) in keys:
                    sb = cnt["s"] % 4
                    cnt["s"] += 1
                    p_ = cnt["p"] % 3
                    cnt["p"] += 1
                    MM(S, [(ps[sb][:, :], KT[0:64, kvh, kt * 128:(kt + 1) * 128],
                            QTb[s][0:64, kvh * 4:(kvh + 1) * 4, :], True, True)], ["KT", "QTb%d" % s], [psn[sb]])
                    if kind == "c":
                        ACT(S, PTw[p_], ps[sb][:, :], AF.Exp, [psn[sb]], ["PTw%d" % p_], scale=0.125)
                    else:
                        t_ = cnt["t"] % 2
                        cnt["t"] += 1
                        ACT(S, tmp[t_], ps[sb][:, :], AF.Exp, [psn[sb]], ["tmp%d" % t_], scale=0.125)
                        mk = K["wm_prev"] if kind == "prev" else K["wm_next"]
                        TT(S, "pool", PTw[p_].rearrange("p (g q) -> p g q", g=4),
                           tmp[t_].rearrange("p (g q) -> p g q", g=4),
                           mk.unsqueeze(1).broadcast_to([128, 4, 128]), ALU.mult, ["tmp%d" % t_], ["PTw%d" % p_])
                    flush()
                    pend.append((p_, kt))
                flush()
                av = ps[ab][:, 0:260].rearrange("p (g e) -> p g e", g=4)
                TT(S, "dve", dd, av[:, :, 64], es[:, kvh * 4:(kvh + 1) * 4], ALU.add, [psn[ab], "es"], ["dd"])
                RECIP(S, dd, dd, ["dd"], ["dd"])
                TT(S, "dve", yw[s][:, kvh * 256:(kvh + 1) * 256].rearrange("p (g e) -> p g e", g=4), av[:, :, 0:64],
                   dd.unsqueeze(2).broadcast_to([128, 4, 64]), ALU.mult, [psn[ab], "dd"], ["yw%d_%d" % (s, kvh)])
            pb = ps[7][:].bitcast(BF16)
            TR(S, [(pb[:, k * 128:(k + 1) * 128], yw[s][:, k * 128:(k + 1) * 128]) for k in range(8)], K["ident_bf"],
               ["yw%d_%d" % (s, kvh) for kvh in range(4)], [psn[7]])
            CP(S, "act", ob[s].rearrange("p k t -> p (k t)"), pb[:, 0:1024], [psn[7]], ["ob%d" % s])
            DMA(S, "sp", scr["YWT"][:, :, qb * 128:(qb + 1) * 128].rearrange("k p t -> p k t"), ob[s], ["ob%d" % s], [])

    def phase_ssd(self, l):
        S, A, ps, psn, K = self.S, self.A, self.ps, self.psn, self.K
        W, scr = self.W, self.scr
        T, NT, NL, SEQ = self.T, self.NT, self.NL, self.SEQ
        identf = K["ident_f"]
        self.newphase()
        cw6 = A.alloc([2048], F32)
        DMA(S, "sp", cw6[0:5, :], W["ssd_conv_w"][l][:, :], [], ["cw6"])
        DMA(S, "sp", cw6[5:6, :], W["ssd_conv_b"][l][:, :], [], ["cw6"])
        TR(S, [(ps[0][:, cb * 6:(cb + 1) * 6], cw6[0:6, cb * 128:(cb + 1) * 128]) for cb in range(16)],
           identf[0:6, 0:6], ["cw6"], [psn[0]])
        cw = A.alloc([16, 6], F32)
        CP(S, "dve", cw.rearrange("p a b -> p (a b)"), ps[0][:, 0:96], [psn[0]], ["cw"])
        xin = [A.alloc([T], BF16) for _ in range(2)]
        acc = [A.alloc([T], F32) for _ in range(2)]
        u = [A.alloc([T], BF16) for _ in range(2)]
        stg = [A.alloc([8, 128], BF16) for _ in range(2)]
        tcnt = 0
        for cb in range(16):
            s = cb % 2
            DMA(S, "sp", xin[s], scr["XBCT"][cb][:, :], [], ["xin%d" % s])
            ACT(S, acc[s], xin[s], AF.Identity, ["xin%d" % s, "cw"], ["acc%d" % s], scale=cw[:, cb, 2:3], bias=cw[:, cb, 5:6])
            for j in (0, 1, 3, 4):
                d = j - 2
                for (a, b) in ((0, SEQ), (SEQ, T)):
                    lo, hi = (a - d, b) if d < 0 else (a, b - d)
                    STT(S, acc[s][:, lo:hi], xin[s][:, lo + d:hi + d], cw[:, cb, j:j + 1], acc[s][:, lo:hi],
                        ALU.mult, ALU.add, ["xin%d" % s, "cw", "acc%d" % s], ["acc%d" % s])
            ACT(S, u[s], acc[s], AF.Silu, ["acc%d" % s], ["u%d" % s])
            if cb >= 8:
                DMA(S, "sp", scr["UT"][cb][:, :], u[s], ["u%d" % s], [])
            if cb < 12:
                for t0 in range(0, NT, 8):
                    nt_ = min(8, NT - t0)
                    q = tcnt % 2
                    tcnt += 1
                    pb = ps[1 + q][:].bitcast(BF16)
                    TR(S, [(pb[:, i * 128:(i + 1) * 128], u[s][:, (t0 + i) * 128:(t0 + i + 1) * 128]) for i in range(nt_)],
                       K["ident_bf"], ["u%d" % s], [psn[1 + q]])
                    CP(S, "dve" if q else "act", stg[q].rearrange("p a b -> p (a b)")[:, 0:nt_ * 128], pb[:, 0:nt_ * 128],
                       [psn[1 + q]], ["stg%d" % q])
                    DMA(S, "sp", scr["XSB"][t0 * 128:(t0 + nt_) * 128, cb * 128:(cb + 1) * 128].rearrange("(t p) c -> p t c", p=128),
                        stg[q][:, 0:nt_, :], ["stg%d" % q], [])
        self.newphase()
        abc = A.alloc([32], F32)
        DMA(S, "sp", abc, W["ssd_a_log"][l][0:1, :].broadcast_to([128, 32]), [], ["abc"])
        ACT(S, abc, abc, AF.Exp, ["abc"], ["abc"])
        TS(S, "dve", abc, abc, -1.0, None, ALU.mult, None, ["abc"], ["abc"])
        dsk = A.alloc([16], F32)
        DMA(S, "sp", dsk, W["ssd_d"][l][0:1, :].broadcast_to([128, 16]), [], ["dsk"])
        sn8 = A.alloc([128], F32)
        DMA(S, "sp", sn8[0:8, :], W["ssd_norm"][l].rearrange("o (k p) -> (o k) p", p=128), [], ["sn8"])
        TR(S, [(ps[0][:, 0:8], sn8[0:8, :])], identf[0:8, 0:8], ["sn8"], [psn[0]])
        snc = A.alloc([8], F32)
        CP(S, "dve", snc, ps[0][:, 0:8], [psn[0]], ["snc"])
        xsb = [A.alloc([1536], BF16) for _ in range(2)]
        dtt = [A.alloc([32], F32) for _ in range(2)]
        ad = [A.alloc([32], F32) for _ in range(2)]
        cs = [A.alloc([64], F32) for _ in range(2)]
        dec = [A.alloc([96], F32) for _ in range(2)]
        ncum = [A.alloc([32], F32) for _ in range(2)]
        xdt = [A.alloc([2, 1024], BF16) for _ in range(2)]
        xdtd = [A.alloc([2, 1024], BF16) for _ in range(2)]
        adB = [A.alloc([32, 128], F32) for _ in range(2)]
        adT = [A.alloc([32, 128], F32) for _ in range(2)]
        negones = A.alloc([128], F32)
        MSET(S, "pool", negones, -1.0, [], ["negones"])

        def v16(ap):
            return ap.rearrange("p (h e) -> p h e", e=64)

        def bc(ap, n):
            return ap.unsqueeze(2).broadcast_to([128, n, 64])

        def prep(c, s, dirs, full):
            n_ = "_%d" % s
            DMA(S, "sp", xsb[s], scr["XSB"][c * 128:(c + 1) * 128, :], [], ["xsb" + n_])
            DMA(S, "sp", dtt[s], scr["DT"][c * 128:(c + 1) * 128, :], [], ["dtt" + n_])
            TT(S, "dve", ad[s], dtt[s], abc, ALU.mult, ["dtt" + n_, "abc"], ["ad" + n_])
            MM(S, [(ps[0][:, 0:16], K["tri_f"], ad[s][:, 0:16], True, True),
                   (ps[0][:, 16:32], K["tri_b"], ad[s][:, 16:32], True, True),
                   (ps[0][:, 32:64], K["ones_f"], ad[s][:, 0:32], True, True)], ["ad" + n_], [psn[0]])
            CP(S, "dve", cs[s], ps[0][:, 0:64], [psn[0]], ["cs" + n_])
            TT(S, "dve", dec[s][:, 32:64], cs[s][:, 32:64], cs[s][:, 0:32], ALU.subtract, ["cs" + n_], ["decs" + n_])
            ACT(S, dec[s][:, 32:64], dec[s][:, 32:64], AF.Exp, ["decs" + n_], ["decs" + n_])
            ACT(S, dec[s][:, 0:32], cs[s][:, 0:32], AF.Exp, ["cs" + n_], ["deco" + n_])
            ACT(S, dec[s][:, 64:96], cs[s][:, 32:64], AF.Exp, ["cs" + n_], ["dect" + n_])
            if full:
                CP(S, "pool", adB[s], ad[s].unsqueeze(2).broadcast_to([128, 32, 128]), ["ad" + n_], ["adB" + n_])
                TT(S, "pool", adT[s][:, 0:16, :], adB[s][:, 0:16, :], K["tri_f"].unsqueeze(1).broadcast_to([128, 16, 128]),
                   ALU.mult, ["adB" + n_], ["adT" + n_])
                TT(S, "dve", adT[s][:, 16:32, :], adB[s][:, 16:32, :], K["tri_b"].unsqueeze(1).broadcast_to([128, 16, 128]),
                   ALU.mult, ["adB" + n_], ["adT" + n_])
            for d in dirs:
                e1 = "pool" if d == 0 else "dve"
                TT(S, e1, v16(xdt[s][:, d, :]), v16(xsb[s][:, 0:1024]), bc(dtt[s][:, d * 16:(d + 1) * 16], 16), ALU.mult,
                   ["xsb" + n_, "dtt" + n_], ["xdt%d" % d + n_])
                TT(S, e1, v16(xdtd[s][:, d, :]), v16(xdt[s][:, d, :]), bc(dec[s][:, 32 + d * 16:32 + (d + 1) * 16], 16), ALU.mult,
                   ["xdt%d" % d + n_, "decs" + n_], ["xdtd%d" % d + n_])

        def state_update(H, s, d, hname):
            n_ = "_%d" % s
            for g in range(4):
                b = 2 + g % 2
                MM(S, [(ps[b][:, 0:256], xsb[s][:, 1024 + g * 128:1024 + (g + 1) * 128], xdtd[s][:, d, g * 256:(g + 1) * 256], True, True)],
                   ["xsb" + n_, "xdtd%d" % d + n_], [psn[b]])
                hv = v16(H[:, g * 256:(g + 1) * 256])
                TT(S, "dve", hv, hv, bc(dec[s][:, 64 + d * 16 + g * 4:64 + d * 16 + (g + 1) * 4], 4), ALU.mult,
                   [hname, "dect" + n_], [hname])
                TT(S, "dve", H[:, g * 256:(g + 1) * 256], H[:, g * 256:(g + 1) * 256], ps[b][:, 0:256], ALU.add,
                   [hname, psn[b]], [hname])

        Hb = A.alloc([1024], F32)
        Hbb = [A.alloc([1024], BF16) for _ in range(2)]
        MSET(S, "dve", Hb, 0.0, [], ["Hb"])
        order_b = [NL + 1, NL] + list(range(NL - 1, -1, -1))
        for i, c in enumerate(order_b):
            s = i % 2
            prep(c, s, [1], False)
            CP(S, "act", Hbb[s], Hb, ["Hb"], ["Hbb%d" % s])
            DMA(S, "sp", scr["HB"][c], Hbb[s], ["Hbb%d" % s], [])
            state_update(Hb, s, 1, "Hb")
        self.newphase_keep()
        Hf = A.alloc([1024], F32)
        Hfb = A.alloc([1024], BF16)
        MSET(S, "dve", Hf, 0.0, [], ["Hf"])
        bct = [A.alloc([8, 128], BF16) for _ in range(2)]
        hb = [A.alloc([1024], BF16) for _ in range(2)]
        zs = [A.alloc([1024], BF16) for _ in range(2)]
        gt = [A.alloc([128], BF16) for _ in range(2)]
        lt = [A.alloc([512], BF16) for _ in range(2)]
        mt = [A.alloc([512], BF16) for _ in range(2)]
        yo = [A.alloc([1024], F32) for _ in range(2)]
        xd = A.alloc([1024], F32)
        yt = A.alloc([1024], F32)
        junk = A.alloc([1024], BF16)
        ssq = A.alloc([1], F32)
        yn = A.alloc([1024], BF16)
        ob = [A.alloc([8, 128], BF16) for _ in range(2)]
        order_f = [NL, NL + 1] + list(range(NL))
        kk = 0
        yn2 = [A.alloc([1024], BF16) for _ in range(2)]
        late = []

        def flush_late():
            while late:
                c_, s_ = late.pop(0)
                m_ = "_%d" % s_
                pb = ps[1][:].bitcast(BF16)
                TR(S, [(pb[:, k * 128:(k + 1) * 128], yn2[s_][:, k * 128:(k + 1) * 128]) for k in range(8)], K["ident_bf"],
                   ["yn" + m_], [psn[1]])
                for k in range(8):
                    ACT(S, ob[s_][:, k, :], pb[:, k * 128:(k + 1) * 128], AF.Identity, [psn[1], "snc"], ["ob" + m_], scale=snc[:, k:k + 1])
                DMA(S, "sp", scr["YST"][:, :, c_ * 128:(c_ + 1) * 128].rearrange("k p t -> p k t"), ob[s_], ["ob" + m_], [])

        def early_tail(i_, c_, s_):
            state_update(Hf, s_, 0, "Hf")
            if i_ + 1 < len(order_f):
                cn = order_f[i_ + 1]
                wo_ = (cn < NL) or self.need_ctx
                prep(cn, 1 - s_, [0, 1] if wo_ else [0], wo_)
        for i, c in enumerate(order_f):
            s = i % 2
            n_ = "_%d" % s
            want_out = (c < NL) or self.need_ctx
            if i == 0:
                prep(c, s, [0, 1] if want_out else [0], want_out)
            if want_out:
                DMA(S, "sp", bct[s], scr["UT"][8:16, :, c * 128:(c + 1) * 128].rearrange("k p t -> p k t"), [], ["bct" + n_])
                DMA(S, "sp", hb[s], scr["HB"][c], [], ["hb" + n_])
                DMA(S, "sp", zs[s], scr["ZS"][c * 128:(c + 1) * 128, :], [], ["zs" + n_])
                CP(S, "act", Hfb, Hf, ["Hf"], ["Hfb"])
                MSET(S, "dve", ps[4][:, :], 0.0, [], [psn[4]])
                MSET(S, "dve", ps[5][:, :], 0.0, [], [psn[5]])
                for g in range(4):
                    MM(S, [(ps[1][:, 0:128], bct[s][:, g, :], bct[s][:, 4 + g, :], True, True)], ["bct" + n_], [psn[1]])
                    CP(S, "act", gt[g % 2], ps[1][:, 0:128], [psn[1]], ["gt%d" % (g % 2)])
                    for d in range(2):
                        q = kk % 2
                        kk += 1
                        b = 2 + q
                        items = []
                        for hh in range(4):
                            c_ = d * 16 + g * 4 + hh
                            o_ = ps[b][:, hh * 128:(hh + 1) * 128]
                            items += [(o_, adB[s][:, c_, :], K["tri_f"] if d == 0 else K["tri_b"], True, False),
                                      (o_, adT[s][:, c_, :], negones, False, False),
                                      (o_, identf, K["neg_f"] if d == 0 else K["neg_b"], False, True)]
                        MM(S, items, ["adB" + n_, "adT" + n_, "negones"], [psn[b]])
                        ACT(S, lt[q], ps[b][:, :], AF.Exp, [psn[b]], ["lt%d" % q])
                        TT(S, "pool" if q else "dve", mt[q].rearrange("p (a b) -> p a b", a=4), lt[q].rearrange("p (a b) -> p a b", a=4),
                           gt[g % 2].unsqueeze(1).broadcast_to([128, 4, 128]), ALU.mult, ["lt%d" % q, "gt%d" % (g % 2)], ["mt%d" % q])
                        yb = 4 + (g * 4) // 8
                        MM(S, [(ps[yb][:, ((g * 4 + hh) % 8) * 64:((g * 4 + hh) % 8 + 1) * 64], mt[q][:, hh * 128:(hh + 1) * 128],
                                xdt[s][:, d, (g * 4 + hh) * 64:(g * 4 + hh + 1) * 64], False, False) for hh in range(4)],
                           ["mt%d" % q, "xdt%d" % d + n_], [psn[yb]])
                flush_late()
                for d in range(2):
                    hsrc = Hfb if d == 0 else hb[s]
                    hn = "Hfb" if d == 0 else "hb" + n_
                    for g in range(4):
                        MM(S, [(ps[6 + g // 2][:, (g % 2) * 256:(g % 2 + 1) * 256], bct[s][:, 4 + g, :], hsrc[:, g * 256:(g + 1) * 256], True, True)],
                           ["bct" + n_, hn], [psn[6 + g // 2]])
                    for half in range(2):
                        TT(S, "dve", yo[d][:, half * 512:(half + 1) * 512].rearrange("p (h e) -> p h e", e=64),
                           ps[6 + half][:, :].rearrange("p (h e) -> p h e", e=64),
                           bc(dec[s][:, d * 16 + half * 8:d * 16 + half * 8 + 8], 8), ALU.mult,
                           [psn[6 + half], "deco" + n_], ["yo%d_%d" % (d, half)])
                early_tail(i, c, s)
                TT(S, "pool", yo[0], yo[0], yo[1], ALU.add, ["yo0_0", "yo0_1", "yo1_0", "yo1_1"], ["yo0_0", "yo0_1"])
                TT(S, "pool", v16(xd), v16(xsb[s][:, 0:1024]), bc(dsk, 16), ALU.mult, ["xsb" + n_, "dsk"], ["xd"])
                TT(S, "pool", yo[0], yo[0], xd, ALU.add, ["yo0_0", "yo0_1", "xd"], ["yo0_0", "yo0_1"])
                for half in range(2):
                    TT(S, "dve", yt[:, half * 512:(half + 1) * 512], ps[4 + half][:, :], yo[0][:, half * 512:(half + 1) * 512], ALU.add,
                       [psn[4 + half], "yo0_0", "yo0_1"], ["yt%d" % half])
                TT(S, "dve", yt, yt, zs[s], ALU.mult, ["yt0", "yt1", "zs" + n_], ["yt0", "yt1"])
                ACT(S, junk, yt, AF.Square, ["yt0", "yt1"], ["junk", "ssq"], accum=ssq)
                ACT(S, ssq, ssq, AF.Sqrt, ["ssq"], ["ssq"], scale=1.0 / 1024, bias=EPS)
                RECIP(S, ssq, ssq, ["ssq"], ["ssq"])
                TS(S, "dve", yn2[s], yt, ssq, None, ALU.mult, None, ["yt0", "yt1", "ssq"], ["yn" + n_])
                late.append((c, s))
            else:
                early_tail(i, c, s)
        flush_late()

    def load_rowG(self, l, which, ngi):
        S, A = self.S, self.A
        rowG = [A.alloc([D], F32) for _ in range(2)]
        ngb = A.alloc([D], F32)
        DMA(S, "sp", ngb, self.W["norm_g"][l][ngi:ngi + 1, :].broadcast_to([128, D]), [], ["ngb"])
        for r in range(2):
            DMA(S, "sp", rowG[r], self.scr["MOD"][r:r + 1, which * D:(which + 1) * D].broadcast_to([128, D]), [], ["rowG%d" % r])
            TT(S, "dve", rowG[r], rowG[r], ngb, ALU.mult, ["rowG%d" % r, "ngb"], ["rowG%d" % r])
        return rowG

    def resid_epilogue(self, y, yname, g, rowG, bufs, i, dst="X"):
        S = self.S
        xt, junk, ss = bufs
        s = i % 2
        r = 0 if g < self.NL else 1
        X = self.scr["X"]
        rows = slice(g * 128, (g + 1) * 128)
        DMA(S, "sp", xt[s], X[rows, :], [], ["ext%d" % s])
        ACT(S, junk, y, AF.Square, [yname], ["ejunk", "ess%d" % s], accum=ss[s])
        ACT(S, ss[s], ss[s], AF.Sqrt, ["ess%d" % s], ["ess%d" % s], scale=1.0 / D, bias=EPS)
        RECIP(S, ss[s], ss[s], ["ess%d" % s], ["ess%d" % s])
        STT(S, y, y, ss[s], rowG[r], ALU.mult, ALU.mult, [yname, "ess%d" % s, "rowG%d" % r], [yname])
        TT(S, "pool", xt[s], xt[s], y, ALU.add, ["ext%d" % s, yname], ["ext%d" % s])
        DMA(S, "sp", self.scr[dst][rows, :], xt[s], ["ext%d" % s], [])

    def phase_merge(self, l):
        S, A, ps, psn, K = self.S, self.A, self.ps, self.psn, self.K
        W, scr = self.W, self.scr
        T, NT, NL, SEQ = self.T, self.NT, self.NL, self.SEQ
        self.newphase()
        rowG = self.load_rowG(l, 2, 1)
        blocks = [list(range(t, min(NL, t + 2))) for t in range(0, NL, 2)]
        if self.need_ctx:
            blocks.append([NL, NL + 1])
        yin = [[A.alloc([8, 256], BF16) for _ in range(3)] for _ in range(2)]
        wb = [[A.alloc([8, 512], BF16) for _ in range(3)] for _ in range(2)]
        gg = [[A.alloc([4, 256], BF16) for _ in range(3)] for _ in range(2)]
        mT = A.alloc([16, 256], BF16)
        tt_ = [A.alloc([256], F32) for _ in range(3)]
        wo = [A.alloc([16, 512], BF16) for _ in range(2)]
        ysb = A.alloc([2, D], F32)
        xt = [A.alloc([D], F32) for _ in range(2)]
        junk = A.alloc([D], BF16)
        ss = [A.alloc([1], F32) for _ in range(2)]
        names = ["YDT", "YST", "YWT"]
        wn = ["w_br_diff", "w_br_ssd", "w_br_win"]
        wc = 0
        oc = 0
        ec = 0
        for bi, tiles in enumerate(blocks):
            bs = bi % 2
            tok0 = tiles[0] * 128
            ntok = len(tiles) * 128
            for br in range(3):
                DMA(S, "sp", yin[bs][br][:, :, 0:ntok], scr[names[br]][:, :, tok0:tok0 + ntok].rearrange("k p t -> p k t"),
                    [], ["yin%d%d" % (bs, br)])
            for f4 in range(4):
                ws = wc % 2
                wc += 1
                for br in range(3):
                    DMA(S, "pool", wb[ws][br], scr["B_" + wn[br]][:, f4 * 512:(f4 + 1) * 512].rearrange("(k p) c -> p k c", p=128),
                        [], ["wb%d%d" % (ws, br)])
                    DMA(S, "sp", gg[ws][br][:, :, 0:ntok],
                        scr["GT"][br * 16 + f4 * 4:br * 16 + f4 * 4 + 4, :, tok0:tok0 + ntok].rearrange("k p t -> p k t"),
                        [], ["gg%d%d" % (ws, br)])
                for m in range(4):
                    f = f4 * 4 + m
                    for br in range(3):
                        MM(S, [(ps[br][:, 0:ntok], wb[ws][br][:, k, m * 128:(m + 1) * 128], yin[bs][br][:, k, 0:ntok], k == 0, k == 7)
                               for k in range(8)], ["wb%d%d" % (ws, br), "yin%d%d" % (bs, br)], [psn[br]])
                        TT(S, "dve", tt_[br][:, 0:ntok], ps[br][:, 0:ntok], gg[ws][br][:, m, 0:ntok], ALU.mult,
                           [psn[br], "gg%d%d" % (ws, br)], ["tt%d" % br])
                    TT(S, "pool", tt_[0][:, 0:ntok], tt_[0][:, 0:ntok], tt_[1][:, 0:ntok], ALU.add, ["tt0", "tt1"], ["tt0"])
                    TT(S, "pool", mT[:, f, 0:ntok], tt_[0][:, 0:ntok], tt_[2][:, 0:ntok], ALU.add, ["tt0", "tt2"], ["mT%d" % f])
            for nch in range(4):
                os_ = oc % 2
                oc += 1
                DMA(S, "pool", wo[os_], scr["B_w_out"][:, nch * 512:(nch + 1) * 512].rearrange("(k p) c -> p k c", p=128),
                    [], ["wo%d" % os_])
                for ti in range(len(tiles)):
                    b = 3 + (ti + nch) % 4
                    MM(S, [(ps[b][:, :], mT[:, k, ti * 128:(ti + 1) * 128], wo[os_][:, k, :], k == 0, k == 15) for k in range(16)],
                       ["wo%d" % os_] + ["mT%d" % f for f in range(16)], [psn[b]])
                    CP(S, "act", ysb[:, ti, nch * 512:(nch + 1) * 512], ps[b][:, :], [psn[b]], ["ysbm%d" % ti])
            for ti, g in enumerate(tiles):
                yname = "ysbm%d" % ti
                self.resid_epilogue(ysb[:, ti, :], yname, g, rowG, (xt, junk, ss), ec)
                ec += 1

    def phase_ffn(self, l):
        S, A, ps, psn, K = self.S, self.A, self.ps, self.psn, self.K
        W, scr = self.W, self.scr
        T, NT, NL, SEQ = self.T, self.NT, self.NL, self.SEQ
        identf = K["ident_f"]
        self.newphase()
        floor0 = A.floor
        rowG = self.load_rowG(l, 5, 3)
        fcT = A.alloc([88, 4], F32)
        fc4 = A.alloc([2816], F32)
        for pc in range(4):
            DMA(S, "sp", fc4[0:3, :], W["ffn_conv_w"][l][:, pc * 2816:(pc + 1) * 2816], [], ["fc4"])
            DMA(S, "sp", fc4[3:4, :], W["ffn_conv_b"][l][:, pc * 2816:(pc + 1) * 2816], [], ["fc4"])
            TR(S, [(ps[0][:, j * 4:(j + 1) * 4], fc4[0:4, j * 128:(j + 1) * 128]) for j in range(22)], identf[0:4, 0:4],
               ["fc4"], [psn[0]])
            CP(S, "dve", fcT[:, pc * 22:(pc + 1) * 22, :].rearrange("p a b -> p (a b)"), ps[0][:, 0:88], [psn[0]], ["fcT"])
        S.barrier()
        A.keep()
        NB = 512
        blocks = [(t0, min(NB, SEQ - t0), t0 > 0, t0 + NB < SEQ) for t0 in range(0, SEQ, NB)]
        if self.need_ctx:
            blocks.append((SEQ, CTX, False, False))
        for (t0, nb, lv, rv) in blocks:
            self.newphase()
            g0 = t0 // 128
            ntl = nb // 128
            actT = A.alloc([44, nb], BF16)
            base = A.top
            h2T = A.alloc([16, nb + 256], BF16)
            jobs = [(g0 + i, 0, 128, 128 + i * 128) for i in range(ntl)]
            if lv:
                jobs.append((g0 - 1, 0, 128, 0))
            if rv:
                jobs.append((g0 + ntl, 0, 128, 128 + nb))
            top1 = A.top
            self.norm_tiles(jobs, self.colA2, self.colB2, h2T, [(0, 1), (2, 3), (4, 5), (6, 7)])
            S.barrier()
            A.top = top1
            wa = [A.alloc([16, 256], BF16) for _ in range(2)]
            wg = [A.alloc([16, 256], BF16) for _ in range(2)]
            ta = [A.alloc([256], F32) for _ in range(2)]
            tg = [A.alloc([256], F32) for _ in range(2)]
            chunks = [(c, min(256, nb - c)) for c in range(0, nb, 256)]
            cc_ = 0

            def loadup(i):
                DMA(S, "pool", wa[i % 2], scr["B_ffn_w_up"][:, i * 256:(i + 1) * 256].rearrange("(k p) c -> p k c", p=128),
                    [], ["wa%d" % (i % 2)])
                DMA(S, "pool", wg[i % 2], scr["B_ffn_w_up"][:, D_FF + i * 256:D_FF + (i + 1) * 256].rearrange("(k p) c -> p k c", p=128),
                    [], ["wg%d" % (i % 2)])
            loadup(0)
            for fb2 in range(22):
                if fb2 + 1 < 22:
                    loadup(fb2 + 1)
                ws = fb2 % 2
                for m in range(2):
                    fb = fb2 * 2 + m
                    for (c, n) in chunks:
                        q = cc_ % 2
                        cc_ += 1
                        first, last = (c == 0), (c + n == nb)
                        lo = 1 if (first and not lv) else 0
                        hi = n - 1 if (last and not rv) else n
                        for (wt_, wname, bnk, tx, txn, fidx) in ((wa[ws], "wa%d" % ws, q * 2, ta[q], "ta%d" % q, fb),
                                                               (wg[ws], "wg%d" % ws, q * 2 + 1, tg[q], "tg%d" % q, 44 + fb)):
                            MM(S, [(ps[bnk][:, 0:n + 2], wt_[:, k, m * 128:(m + 1) * 128], h2T[:, k, 127 + c:127 + c + n + 2], k == 0, k == 15)
                                   for k in range(16)], [wname], [psn[bnk]])
                            ACT(S, tx[:, 0:n], ps[bnk][:, 1:n + 1], AF.Identity, [psn[bnk]], [txn],
                                scale=fcT[:, fidx, 1:2], bias=fcT[:, fidx, 3:4])
                            STT(S, tx[:, lo:n], ps[bnk][:, lo:n], fcT[:, fidx, 0:1], tx[:, lo:n], ALU.mult, ALU.add,
                                [psn[bnk], txn], [txn])
                            STT(S, tx[:, 0:hi], ps[bnk][:, 2:2 + hi], fcT[:, fidx, 2:3], tx[:, 0:hi], ALU.mult, ALU.add,
                                [psn[bnk], txn], [txn])
                        ACT(S, ta[q][:, 0:n], ta[q][:, 0:n], AF.Silu, ["ta%d" % q], ["ta%d" % q])
                        TT(S, "pool", actT[:, fb, c:c + n], ta[q][:, 0:n], tg[q][:, 0:n], ALU.mult, ["ta%d" % q, "tg%d" % q], ["actT"])
            S.barrier()
            A.top = base
            yT = A.alloc([16, nb], F32)
            top3 = A.top
            wd = [A.alloc([44, 256], BF16) for _ in range(2)]

            def loaddn(i):
                DMA(S, "pool", wd[i % 2], scr["B_ffn_w_down"][:, i * 256:(i + 1) * 256].rearrange("(k p) c -> p k c", p=128),
                    [], ["wd%d" % (i % 2)])
            loaddn(0)
            bc_ = 0
            for fo2 in range(8):
                if fo2 + 1 < 8:
                    loaddn(fo2 + 1)
                ws = fo2 % 2
                for m in range(2):
                    fo = fo2 * 2 + m
                    b = 4 + bc_ % 4
                    bc_ += 1
                    MM(S, [(ps[b][:, 0:nb], wd[ws][:, k, m * 128:(m + 1) * 128], actT[:, k, 0:nb], k == 0, k == 43) for k in range(44)],
                       ["wd%d" % ws], [psn[b]])
                    CP(S, "act" if fo % 2 else "dve", yT[:, fo, :], ps[b][:, 0:nb], [psn[b]], ["yT%d" % fo])
            S.barrier()
            A.top = top3
            ysb = [A.alloc([D], F32) for _ in range(2)]
            xt = [A.alloc([D], F32) for _ in range(2)]
            junk = A.alloc([D], BF16)
            ss = [A.alloc([1], F32) for _ in range(2)]
            for ti in range(ntl):
                s = ti % 2
                for q4 in range(4):
                    TR(S, [(ps[q4][:, j * 128:(j + 1) * 128], yT[:, q4 * 4 + j, ti * 128:(ti + 1) * 128]) for j in range(4)], identf,
                       [], [psn[q4]])
                    CP(S, "act" if q4 % 2 else "dve", ysb[s][:, q4 * 512:(q4 + 1) * 512], ps[q4][:, :], [psn[q4]], ["fysb%d" % s])
                self.resid_epilogue(ysb[s], "fysb%d" % s, g0 + ti, rowG, (xt, junk, ss), ti, dst="X2")
        S.barrier()
        nrow = T if self.need_ctx else SEQ
        DMA(S, "sp", scr["X"][0:nrow, :], scr["X2"][0:nrow, :], [], [])
        S.barrier()
        A.floor = floor0


_CACHE = {}
SEQ_FULL = 4096
N_CORES = 4


def kernel(**inputs):
    SEQ = SEQ_FULL
    if "nc" not in _CACHE:
        _CACHE["nc"] = Builder(SEQ).build()
        _CACHE["consts"] = host_consts(SEQ)
    nc = _CACHE["nc"]
    consts = _CACHE["consts"]
    f32 = np.float32
    shared = {}
    for name, shp in W_SPECS:
        shared[name] = np.ascontiguousarray(np.asarray(inputs[name], dtype=f32).reshape([DEPTH] + shp))
    shared.update(consts)
    in_maps = []
    for b in range(N_CORES):
        m = dict(shared)
        m["x"] = np.ascontiguousarray(np.asarray(inputs["x"][b], dtype=f32))
        m["ctx"] = np.ascontiguousarray(np.asarray(inputs["ctx"][b], dtype=f32))
        m["cvec"] = np.ascontiguousarray(np.stack([np.asarray(inputs["c"][b], dtype=f32),
                                                   np.asarray(inputs["c_ctx"], dtype=f32)]))
        in_maps.append(m)
    res = run_bass_kernel_spmd(nc, in_maps, core_ids=list(range(N_CORES)))
    out = np.stack([np.asarray(res.results[b]["out"]) for b in range(N_CORES)], axis=0)
    return out.astype(f32)
```
